# Optimizing a Trainium2 kernel written in Bass

```python
import math
import jax, jax.numpy as jnp
from jax import lax
import numpy as np

D_MODEL = 1024
BATCH = 32
SEQ = 256
DEPTH = 2
DEC_BATCH = 2
DEC_SEQ = 1024
PAST_LEN = 512

GRID_W = 64
H_A = 4
DH_A = 64
W_A = H_A * 2 * DH_A
H_B = 8
P_B = 64
G_B = 2
N_B = 64
DI_B = H_B * P_B
CONV_K = 5
CONV_DIM = DI_B + 2 * G_B * N_B
SSD_CHUNK = 128
H_C = 8
DH_C = 64
W_C = H_C * DH_C
NA_KH = 8
NA_KW = 16
N_BRANCH = 3
ROPE_BASE = 10000.0
Q_BLOCK = 128
EPS = 1e-6
IN_SIZES = (W_A, W_A, W_A, W_A,
            DI_B, CONV_DIM, 2 * H_B,
            W_C, W_C, W_C, W_C,
            N_BRANCH * D_MODEL)
IN_COLS = sum(IN_SIZES)

kernel_name = 'hybrid_diffattn_ssd_natten_prefix_step'


def _rmsnorm(x, g):
    xf = x.astype(jnp.float32)
    y = xf * lax.rsqrt(jnp.mean(xf * xf, axis=-1, keepdims=True) + EPS)
    return (y * g.astype(jnp.float32)).astype(x.dtype)


def _softmax32(s):
    return jax.nn.softmax(s.astype(jnp.float32), axis=-1)


def _query_blocks(fn, qs):
    b, t = qs[0].shape[:2]
    blk = math.gcd(t, Q_BLOCK)
    nb = t // blk
    split = lambda a: jnp.moveaxis(a.reshape((b, nb, blk) + a.shape[2:]), 1, 0)
    out = lax.map(fn, tuple(split(a) for a in qs))
    out = jnp.moveaxis(out, 0, 1)
    return out.reshape((b, t) + out.shape[3:])


def _axial_rope(x):
    t, dh = x.shape[1], x.shape[-1]
    half = dh // 2
    quarter = half // 2
    pos = jnp.arange(t)
    inv = ROPE_BASE ** (-jnp.arange(quarter, dtype=jnp.float32) / quarter)

    def rot(u, p):
        ang = p.astype(jnp.float32)[:, None] * inv[None, :]
        bshape = (1, t) + (1,) * (u.ndim - 3) + (quarter,)
        cos, sin = jnp.cos(ang).reshape(bshape), jnp.sin(ang).reshape(bshape)
        u1 = u[..., :quarter].astype(jnp.float32)
        u2 = u[..., quarter:].astype(jnp.float32)
        return jnp.concatenate([u1 * cos - u2 * sin, u2 * cos + u1 * sin], axis=-1)

    out = jnp.concatenate([rot(x[..., :half], pos // GRID_W), rot(x[..., half:], pos % GRID_W)], axis=-1)
    return out.astype(x.dtype)


def _diff_lambda(lq1, lk1, lq2, lk2, lam_init):
    f = lambda a, b: jnp.exp(jnp.sum(a.astype(jnp.float32) * b.astype(jnp.float32)))
    return f(lq1, lk1) - f(lq2, lk2) + lam_init


def _diff_attention(q, k, v, lam):
    scale = DH_A ** -0.5

    def block(args):
        (qb,) = args
        p = _softmax32(jnp.einsum('bqhmd,bkhmd->bhmqk', qb, k) * scale)
        p = p[:, :, 0] - lam * p[:, :, 1]
        return jnp.einsum('bhqk,bkhe->bqhe', p.astype(v.dtype), v)

    return _query_blocks(block, (q,))


def _diff_post(o, subln_g, lam_init):
    o = _rmsnorm(o, subln_g) * (1.0 - lam_init)
    return o.reshape(o.shape[:2] + (W_A,))


def _softmax_attention(q, k, v):
    scale = q.shape[-1] ** -0.5

    def block(args):
        (qb,) = args
        p = _softmax32(jnp.einsum('bqhd,bkhd->bhqk', qb, k) * scale)
        return jnp.einsum('bhqk,bkhd->bqhd', p.astype(v.dtype), v)

    return _query_blocks(block, (q,))


def _neighborhood_attention(q, k, v, ck, cv, rpb):
    b, t, h, d = q.shape
    rows = t // GRID_W
    kh, kw = min(NA_KH, rows), NA_KW
    scale = d ** -0.5
    grid = lambda a: a.reshape(b, rows, GRID_W, h, d)
    qg, kg, vg = grid(q), grid(k), grid(v)
    r = jnp.arange(rows)
    row_idx = jnp.clip(r - kh // 2, 0, rows - kh)[:, None] + jnp.arange(kh)[None, :]
    k_rows = jnp.take(kg, row_idx, axis=1)
    v_rows = jnp.take(vg, row_idx, axis=1)
    col = jnp.arange(GRID_W)
    col_start = jnp.clip(col - kw // 2, 0, GRID_W - kw)
    in_win = (col[None, :] >= col_start[:, None]) & (col[None, :] < col_start[:, None] + kw)
    dr = row_idx - r[:, None] + (NA_KH - 1)
    dc = jnp.clip(col[None, :] - col[:, None], -(kw - 1), kw - 1) + (kw - 1)
    bias = rpb[:, dr[:, None, :, None], dc[None, :, None, :]].astype(jnp.float32)
    s_win = jnp.einsum('brqhd,brikhd->bhrqik', qg, k_rows).astype(jnp.float32) * scale + bias[None]
    s_win = jnp.where(in_win[:, None, :], s_win, -jnp.inf)
    s_ctx = jnp.einsum('brqhd,bphd->bhrqp', qg, ck).astype(jnp.float32) * scale
    nwin = kh * GRID_W
    p = _softmax32(jnp.concatenate([s_win.reshape(b, h, rows, GRID_W, nwin), s_ctx], axis=-1))
    p_win = p[..., :nwin].reshape(b, h, rows, GRID_W, kh, GRID_W).astype(v.dtype)
    p_ctx = p[..., nwin:].astype(v.dtype)
    o = jnp.einsum('bhrqik,brikhd->brqhd', p_win, v_rows) + jnp.einsum('bhrqp,bphd->brqhd', p_ctx, cv)
    return o.reshape(b, t, h * d)


def _centred_depthwise_conv(u, w, bias):
    c = u.shape[-1]
    out = lax.conv_general_dilated(u, w[:, None, :].astype(u.dtype), window_strides=(1,),
                                   padding=[(CONV_K // 2, CONV_K // 2)],
                                   dimension_numbers=('NWC', 'WIO', 'NWC'), feature_group_count=c)
    return out + bias


def _ssd_scan(x, dt, a, bm, cm, h0):
    b, l, h, p = x.shape
    g, n = bm.shape[2], bm.shape[3]
    r = h // g
    q = math.gcd(l, SSD_CHUNK)
    nc = l // q
    la = (dt * a).reshape(b, nc, q, g, r)
    xdt = (x * dt[..., None]).reshape(b, nc, q, g, r, p)
    bm = bm.reshape(b, nc, q, g, n)
    cm = cm.reshape(b, nc, q, g, n)
    acum = jnp.cumsum(la, axis=2)
    causal = jnp.arange(q)[:, None] >= jnp.arange(q)[None, :]
    seg = acum[:, :, :, None] - acum[:, :, None, :]
    decay = jnp.exp(jnp.where(causal[:, :, None, None], seg, -jnp.inf))
    cb = jnp.einsum('bcign,bcjgn->bcijg', cm, bm)
    y_diag = jnp.einsum('bcijg,bcijgr,bcjgrp->bcigrp', cb, decay, xdt)
    to_end = jnp.exp(acum[:, :, -1:] - acum)
    states = jnp.einsum('bcjgn,bcjgr,bcjgrp->bcgrpn', bm, to_end, xdt)
    chunk_decay = jnp.exp(acum[:, :, -1])

    def step(hc, inp):
        s, dcy = inp
        return dcy[..., None, None] * hc + s, hc

    h_last, h_in = lax.scan(step, h0.reshape(b, g, r, p, n),
                            (jnp.moveaxis(states, 1, 0), jnp.moveaxis(chunk_decay, 1, 0)))
    h_in = jnp.moveaxis(h_in, 0, 1)
    y_off = jnp.einsum('bcign,bcgrpn,bcigr->bcigrp', cm, h_in, jnp.exp(acum))
    return (y_diag + y_off).reshape(b, l, h, p), h_last.reshape(b, h, p, n)


def _ssd_branch(z, xbc, dt_raw, conv_w, conv_b, dt_bias, a_log, d_skip, norm_g, h0):
    b, l, _ = z.shape
    f32 = jnp.float32
    xbc = jax.nn.silu(_centred_depthwise_conv(xbc, conv_w, conv_b)).astype(f32)
    xs = xbc[..., :DI_B].reshape(b, l, H_B, P_B)
    bm = xbc[..., DI_B:DI_B + G_B * N_B].reshape(b, l, G_B, N_B)
    cm = xbc[..., DI_B + G_B * N_B:].reshape(b, l, G_B, N_B)
    dt = jax.nn.softplus(dt_raw.astype(f32).reshape(b, l, 2, H_B) + dt_bias.astype(f32))
    a = -jnp.exp(a_log.astype(f32))
    h0 = h0.astype(f32)
    flip = lambda u: jnp.flip(u, axis=1)
    y_f, h_f = _ssd_scan(xs, dt[:, :, 0], a[0], bm, cm, h0[:, 0])
    y_b, h_b = _ssd_scan(flip(xs), flip(dt[:, :, 1]), a[1], flip(bm), flip(cm), h0[:, 1])
    y = y_f + flip(y_b) + xs * jnp.sum(d_skip.astype(f32), axis=0)[:, None]
    y = y.reshape(b, l, DI_B) * jax.nn.silu(z.astype(f32))
    y = _rmsnorm(y, norm_g).astype(z.dtype)
    return y, jnp.stack([h_f, h_b], axis=1).astype(z.dtype)


def _pre(x, cvec, norm_g, w_ada, b_ada, w_in):
    ada = (jax.nn.silu(cvec) @ w_ada + b_ada)[..., None, :]
    shift, scale, gate = jnp.split(ada, 3, axis=-1)
    hmod = _rmsnorm(x, norm_g) * (1.0 + scale) + shift
    parts = jnp.split(hmod @ w_in, np.cumsum(IN_SIZES)[:-1].tolist(), axis=-1)
    return parts, gate


def _post(x, gate, ya, ga, yb, yc, gc, merge_logits, w_br_a, w_br_b, w_br_c, w_out):
    ma, mb, mc = jnp.split(jax.nn.sigmoid(merge_logits), N_BRANCH, axis=-1)
    merged = (ma * ((ya * jax.nn.silu(ga)) @ w_br_a) + mb * (yb @ w_br_b)
              + mc * ((yc * jax.nn.silu(gc)) @ w_br_c))
    return x + gate * (merged @ w_out)


def _context_layer(x, c_ctx, lam_init, lw):
    (norm_g, w_ada, b_ada, w_in, lam_q1, lam_k1, lam_q2, lam_k2, subln_g, conv_w, conv_b,
     dt_bias, a_log, d_skip, ssd_norm_g, na_rpb, w_br_a, w_br_b, w_br_c, w_out) = lw
    b, l, _ = x.shape
    (qa, ka, va, ga, z, xbc, dt_raw, qc, kc, vc, gc, merge), gate = _pre(x, c_ctx, norm_g, w_ada, b_ada, w_in)
    qa = qa.reshape(b, l, H_A, 2, DH_A)
    ka = ka.reshape(b, l, H_A, 2, DH_A)
    va = va.reshape(b, l, H_A, 2 * DH_A)
    lam = _diff_lambda(lam_q1, lam_k1, lam_q2, lam_k2, lam_init)
    ya = _diff_post(_diff_attention(qa, ka, va, lam), subln_g, lam_init)
    h0 = jnp.zeros((b, 2, H_B, P_B, N_B), jnp.float32)
    yb, ssd_state = _ssd_branch(z, xbc, dt_raw, conv_w, conv_b, dt_bias, a_log, d_skip, ssd_norm_g, h0)
    qc, kc, vc = (u.reshape(b, l, H_C, DH_C) for u in (qc, kc, vc))
    yc = _softmax_attention(qc, kc, vc).reshape(b, l, W_C)
    x = _post(x, gate, ya, ga, yb, yc, gc, merge, w_br_a, w_br_b, w_br_c, w_out)
    return x, ka.reshape(b, l, H_A, 2 * DH_A), va, kc, vc, ssd_state


def _latent_layer(x, c, lam_init, ck_a, cv_a, ck_c, cv_c, h0, lw):
    (norm_g, w_ada, b_ada, w_in, lam_q1, lam_k1, lam_q2, lam_k2, subln_g, conv_w, conv_b,
     dt_bias, a_log, d_skip, ssd_norm_g, na_rpb, w_br_a, w_br_b, w_br_c, w_out) = lw
    b, l, _ = x.shape
    plen = ck_a.shape[1]
    (qa, ka, va, ga, z, xbc, dt_raw, qc, kc, vc, gc, merge), gate = _pre(x, c, norm_g, w_ada, b_ada, w_in)
    qa = _axial_rope(qa.reshape(b, l, H_A, 2, DH_A))
    ka = _axial_rope(ka.reshape(b, l, H_A, 2, DH_A))
    k_all = jnp.concatenate([ka, ck_a.reshape(b, plen, H_A, 2, DH_A)], axis=1)
    v_all = jnp.concatenate([va.reshape(b, l, H_A, 2 * DH_A), cv_a], axis=1)
    lam = _diff_lambda(lam_q1, lam_k1, lam_q2, lam_k2, lam_init)
    ya = _diff_post(_diff_attention(qa, k_all, v_all, lam), subln_g, lam_init)
    yb, _ = _ssd_branch(z, xbc, dt_raw, conv_w, conv_b, dt_bias, a_log, d_skip, ssd_norm_g, h0)
    qc, kc, vc = (u.reshape(b, l, H_C, DH_C) for u in (qc, kc, vc))
    yc = _neighborhood_attention(qc, kc, vc, ck_c, cv_c, na_rpb)
    return _post(x, gate, ya, ga, yb, yc, gc, merge, w_br_a, w_br_b, w_br_c, w_out)


def setup_inputs(seed: int = 0) -> dict:
    key = jax.random.key(seed)
    ks = iter(jax.random.split(key, 40))
    d = D_MODEL

    def nrm(shape, s):
        return jax.random.normal(next(ks), shape, jnp.float32) * s

    dt0 = jnp.exp(jax.random.uniform(next(ks), (DEPTH, 2, H_B), jnp.float32, math.log(1e-3), math.log(1e-1)))
    a0 = jax.random.uniform(next(ks), (DEPTH, 2, H_B), jnp.float32, 1.0, 16.0)
    return {
        'x_prompt': nrm((BATCH, SEQ, d), 1.0),
        'x_sample': nrm((DEC_BATCH, DEC_SEQ, d), 1.0),
        'cache_diff_k': nrm((DEC_BATCH, DEPTH, PAST_LEN, H_A, 2 * DH_A), 1.0),
        'cache_diff_v': nrm((DEC_BATCH, DEPTH, PAST_LEN, H_A, 2 * DH_A), 1.0),
        'cache_na_k': nrm((DEC_BATCH, DEPTH, PAST_LEN, H_C, DH_C), 1.0),
        'cache_na_v': nrm((DEC_BATCH, DEPTH, PAST_LEN, H_C, DH_C), 1.0),
        'state_ssd': nrm((DEC_BATCH, DEPTH, 2, H_B, P_B, N_B), 0.5),
        'c': nrm((DEC_BATCH, d), 1.0),
        'c_ctx': nrm((d,), 1.0),
        'norm_g': 1.0 + nrm((DEPTH, d), 0.01),
        'w_ada': nrm((DEPTH, d, 3 * d), 0.5 * d ** -0.5),
        'b_ada': nrm((DEPTH, 3 * d), 0.01),
        'w_in': nrm((DEPTH, d, IN_COLS), d ** -0.5),
        'lam_q1': nrm((DEPTH, DH_A), 0.1),
        'lam_k1': nrm((DEPTH, DH_A), 0.1),
        'lam_q2': nrm((DEPTH, DH_A), 0.1),
        'lam_k2': nrm((DEPTH, DH_A), 0.1),
        'diff_subln_g': 1.0 + nrm((DEPTH, 2 * DH_A), 0.01),
        'conv_w': nrm((DEPTH, CONV_K, CONV_DIM), CONV_K ** -0.5),
        'conv_b': nrm((DEPTH, CONV_DIM), 0.01),
        'dt_bias': dt0 + jnp.log(-jnp.expm1(-dt0)),
        'a_log': jnp.log(a0),
        'd_skip': 1.0 + nrm((DEPTH, 2, H_B), 0.1),
        'ssd_norm_g': 1.0 + nrm((DEPTH, DI_B), 0.01),
        'na_rpb': nrm((DEPTH, H_C, 2 * NA_KH - 1, 2 * NA_KW - 1), 0.02),
        'w_br_a': nrm((DEPTH, W_A, d), W_A ** -0.5),
        'w_br_b': nrm((DEPTH, DI_B, d), DI_B ** -0.5),
        'w_br_c': nrm((DEPTH, W_C, d), W_C ** -0.5),
        'w_out': nrm((DEPTH, d, d), d ** -0.5),
        'final_g': 1.0 + nrm((d,), 0.01),
    }


def reference(x_prompt, x_sample, cache_diff_k, cache_diff_v, cache_na_k, cache_na_v, state_ssd,
              c, c_ctx, norm_g, w_ada, b_ada, w_in, lam_q1, lam_k1, lam_q2, lam_k2, diff_subln_g,
              conv_w, conv_b, dt_bias, a_log, d_skip, ssd_norm_g, na_rpb, w_br_a, w_br_b, w_br_c,
              w_out, final_g):
    xp, xs = x_prompt, x_sample
    new_k_a, new_v_a, new_k_c, new_v_c, new_ssd = [], [], [], [], []
    for li in range(DEPTH):
        lam_init = 0.8 - 0.6 * math.exp(-0.3 * li)
        lw = (norm_g[li], w_ada[li], b_ada[li], w_in[li], lam_q1[li], lam_k1[li], lam_q2[li], lam_k2[li],
              diff_subln_g[li], conv_w[li], conv_b[li], dt_bias[li], a_log[li], d_skip[li], ssd_norm_g[li],
              na_rpb[li], w_br_a[li], w_br_b[li], w_br_c[li], w_out[li])
        xp, ka, va, kc, vc, hs = _context_layer(xp, c_ctx, lam_init, lw)
        new_k_a.append(ka)
        new_v_a.append(va)
        new_k_c.append(kc)
        new_v_c.append(vc)
        new_ssd.append(hs)
        xs = _latent_layer(xs, c, lam_init, cache_diff_k[:, li], cache_diff_v[:, li],
                           cache_na_k[:, li], cache_na_v[:, li], state_ssd[:, li], lw)
    y_prompt = _rmsnorm(xp, final_g)
    y_sample = _rmsnorm(xs, final_g)
    return (y_prompt, y_sample, jnp.stack(new_k_a, axis=1), jnp.stack(new_v_a, axis=1),
            jnp.stack(new_k_c, axis=1), jnp.stack(new_v_c, axis=1), jnp.stack(new_ssd, axis=1))
```

```python
import contextlib
import math
import numpy as np
import concourse.bass as bass
import concourse.mybir as mybir
from concourse.bass_utils import run_bass_kernel_spmd

F32 = mybir.dt.float32
BF16 = mybir.dt.bfloat16
AF = mybir.ActivationFunctionType
ALU = mybir.AluOpType
AX = mybir.AxisListType

ENGS = ("pe", "act", "dve", "pool", "sp")
EPS = 1e-6
NEG = -30000.0


class Prog:
    def __init__(self, nc):
        self.nc = nc
        self.ops = []
        self.last_w = {}
        self.readers = {}
        self.dma_last = {}
        self.dma_count = {}
        self.stack = contextlib.ExitStack()

    def sb(self, name, shape, dt):
        return self.stack.enter_context(self.nc.sbuf_tensor("sb_" + name, list(shape), dt))

    def ps(self, name, shape, dt=F32):
        return self.stack.enter_context(self.nc.psum_tensor("ps_" + name, list(shape), dt))

    limit = None
    tag = ""

    keymap = None

    def op(self, eng, fn, reads=(), writes=(), dma=None, inc=16):
        oid = len(self.ops)
        if self.limit is not None and oid >= self.limit:
            return None
        if self.keymap is not None:
            reads = [self.keymap(k) for k in reads]
            writes = [self.keymap(k) for k in writes]
        deps = set()
        for k in reads:
            if k in self.last_w:
                deps.add(self.last_w[k])
            if isinstance(k, str) and k.startswith("ps_"):
                for r in self.readers.get(k, ()):
                    if self.ops[r]["eng"] != eng:
                        deps.add(r)
        for k in writes:
            if k in self.last_w:
                deps.add(self.last_w[k])
            last = {}
            for r in self.readers.get(k, ()):
                ro = self.ops[r]
                if ro["dma"] is not None:
                    deps.add(r)
                else:
                    last[ro["eng"]] = r
            deps.update(last.values())
        if dma is not None and dma in self.dma_last:
            deps.add(self.dma_last[dma])
        deps.discard(oid)
        if eng == "pe":
            deps = {d for d in deps if self.ops[d]["eng"] != "pe"}
        o = dict(id=oid, eng=eng, fn=fn, deps=deps, dma=dma, marked=False, mark=None, tag=self.tag)
        if dma is not None:
            self.dma_count[dma] = self.dma_count.get(dma, 0) + inc
            o["dma_val"] = self.dma_count[dma]
            o["inc"] = inc
            self.dma_last[dma] = oid
        self.ops.append(o)
        for k in reads:
            self.readers.setdefault(k, []).append(oid)
        for k in writes:
            self.last_w[k] = oid
            self.readers[k] = []
        return oid

    def pe(self, fn, r=(), w=(), ni=1):
        oid = self.op("pe", fn, r, w)
        if oid is not None:
            self.ops[oid]["ni"] = ni
        return oid

    def act(self, fn, r=(), w=()):
        return self.op("act", fn, r, w)

    def dve(self, fn, r=(), w=()):
        return self.op("dve", fn, r, w)

    def pool(self, fn, r=(), w=()):
        return self.op("pool", fn, r, w)

    def dma(self, q, key, out, in_, r=(), w=()):
        return self.op(q, lambda e: e.dma_start(out=out, in_=in_), r, w, dma=key)

    def emit(self, final_keys=()):
        nc = self.nc
        ops = self.ops
        for o in ops:
            for d in o["deps"]:
                p = ops[d]
                if p["dma"] is None:
                    p["marked"] = True
        cnt = {e: 0 for e in ENGS}
        for o in ops:
            if o["dma"] is None and o["marked"]:
                cnt[o["eng"]] += 1
                o["mark"] = cnt[o["eng"]]
        esem = {e: self.stack.enter_context(nc.semaphore("s_" + e)) for e in ENGS if e != "sp"}
        dsem = {k: self.stack.enter_context(nc.semaphore("d_%d" % i))
                for i, k in enumerate(self.dma_count)}
        per_eng = {e: [o for o in ops if o["eng"] == e] for e in ENGS}
        engobj = {"pe": "tensor", "act": "scalar", "dve": "vector", "pool": "gpsimd", "sp": "sync"}

        def run(e, eng):
            waited = {}
            for o in per_eng[e]:
                need = {}
                for d in o["deps"]:
                    p = ops[d]
                    if p["dma"] is not None:
                        sk, v = ("d", p["dma"]), p["dma_val"]
                    else:
                        sk, v = ("e", p["eng"]), p["mark"]
                    if need.get(sk, 0) < v:
                        need[sk] = v
                for sk, v in need.items():
                    if waited.get(sk, 0) >= v:
                        continue
                    sem = dsem[sk[1]] if sk[0] == "d" else esem[sk[1]]
                    eng.wait_ge(sem, v)
                    waited[sk] = v
                ins = o["fn"](eng)
                if o["dma"] is not None:
                    ins.then_inc(dsem[o["dma"]], o["inc"])
                elif o["marked"]:
                    ins.then_inc(esem[e], 1)
            if e == "sp":
                for k in final_keys:
                    eng.wait_ge(dsem[k], self.dma_count[k])

        with nc.Block() as block:
            for e in ENGS:
                getattr(block, engobj[e])(lambda eng, e=e: run(e, eng))


def MM(out, lhsT, rhs, start=True, stop=True):
    return lambda e: e.matmul(out, lhsT=lhsT, rhs=rhs, start=start, stop=stop)


def TR(out, in_, ident):
    return lambda e: e.transpose(out, in_, ident)


def ACT(out, in_, func, **kw):
    return lambda e: e.activation(out=out, in_=in_, func=func, **kw)


def TT(out, in0, in1, op):
    return lambda e: e.tensor_tensor(out=out, in0=in0, in1=in1, op=op)


def TS(out, in0, s1, op0, s2=None, op1=None):
    if op1 is None:
        return lambda e: e.tensor_scalar(out=out, in0=in0, scalar1=s1, scalar2=None, op0=op0)
    return lambda e: e.tensor_scalar(out=out, in0=in0, scalar1=s1, scalar2=s2, op0=op0, op1=op1)


def STT(out, in0, scalar, in1, op0, op1):
    return lambda e: e.scalar_tensor_tensor(out=out, in0=in0, scalar=scalar, in1=in1, op0=op0, op1=op1)


def CP(out, in_):
    return lambda e: e.tensor_copy(out=out, in_=in_)


def bc(ap, axis, shape):
    return ap.unsqueeze(axis).to_broadcast(list(shape))


QA, KA, VA, GA = 0, 512, 1024, 1536
ZB, XB, DTB = 2048, 2560, 3328
QC, KC, VC, GC = 3344, 3856, 4368, 4880
MRG = 5392
IN_COLS = 8464
T = 1024
NT = 8


def build(n_layers=2, do_prompt=True, do_sample=True, stages="ABCP", limit=None, own_last=True, own_all=True):
    nc = bass.Bass("TRN2", target_bir_lowering=False)

    def din(name, shape):
        return nc.dram_tensor(name, list(shape), F32, kind="ExternalInput").ap()

    def dout(name, shape):
        return nc.dram_tensor(name, list(shape), F32, kind="ExternalOutput").ap()

    D = {}
    for name, shape in [
        ("xp", (T, 1024)), ("xs", (T, 1024)), ("xso", (256, 1024)),
        ("cdk", (2, 512, 512)), ("cdv", (2, 512, 512)), ("cnk", (2, 512, 512)), ("cnv", (2, 512, 512)),
        ("sst", (2, 2, 512, 64)),
        ("cvecT", (2, 128, 8)), ("norm_g", (2, 1024)), ("w_ada", (2, 1024, 3072)), ("b_ada", (2, 3072)),
        ("w_in", (2, 1024, IN_COLS)), ("lamv", (2, 4, 64)), ("subln_g", (2, 128)),
        ("convwT", (2, 128, 6, 5)), ("convbT", (2, 128, 6)), ("dt_bias", (2, 16)), ("a_log", (2, 16)),
        ("d_skip", (2, 16)), ("ssd_norm_g", (2, 512)), ("rpbT", (2, 8, 15, 64, 64)),
        ("w_br_a", (2, 512, 1024)), ("w_br_b", (2, 512, 1024)), ("w_br_c", (2, 512, 1024)),
        ("w_out", (2, 1024, 1024)), ("final_g", (1024,)),
        ("ident", (128, 128)), ("triU", (128, 128)), ("triL", (128, 128)),
        ("ropeC", (128, 1024)), ("ropeS", (128, 1024)), ("ropeRT", (128, 128)), ("namask", (128, 64)),
        ("selO", (128, 16)), ("ropeCo", (128, 256)), ("ropeSo", (128, 256)),
        ("rpbO", (2, 8, 128, 32, 64)), ("maskO", (128, 32, 64)),
    ]:
        D[name] = din(name, shape)
    O = {}
    for name, shape in [
        ("yp", (T, 1024)), ("ys", (256 if (own_all or (own_last and n_layers > 1)) else T, 1024)),
        ("ndk", (4, 2, 256, 512)), ("ndv", (4, 2, 256, 512)), ("nnk", (4, 2, 256, 512)), ("nnv", (4, 2, 256, 512)),
        ("nssd", (4, 2, 2, 512, 64)),
    ]:
        O[name] = dout(name, shape)
    xscr = nc.dram_tensor("xscr", [T, 1024], F32, kind="Internal").ap()
    ada_scr = nc.dram_tensor("ada_scr", [2, 2, 128, 3072], F32, kind="Internal").ap()
    xscr_own = nc.dram_tensor("xscr_own", [256, 1024], F32, kind="Internal").ap()
    xgath = nc.dram_tensor("xgath", [T, 1024], F32, kind="Internal").ap()

    P = Prog(nc)
    P.limit = limit
    with P.stack:
        hT = P.sb("hT", [128, 8, T], BF16)
        acc = P.sb("acc", [128, NT, 1024], F32)
        YT = P.sb("YT", [128, 4, T], BF16)
        big = P.sb("big", [128, 4096], F32)
        qT = big[:, 0:2048].bitcast(BF16).rearrange("p (j t) -> p j t", j=4)
        PTb = big[:, 2048:3584].bitcast(BF16).rearrange("p (k q) -> p k q", k=12)
        ysum = big[:, :].rearrange("p (c f) -> p c f", c=8)
        kT = P.sb("kT", [128, 8, T], BF16)
        vaug = P.sb("vaug", [128, NT, 528], BF16)
        gbuf = P.sb("gbuf", [128, NT, 512], BF16)
        wb = [P.sb("wb%d" % i, [128, 8, 512], BF16) for i in range(2)]
        ada = P.sb("ada", [128, 3072], F32)
        xt = [P.sb("xt%d" % i, [128, 1024], F32) for i in range(2)]
        htmp = P.sb("htmp", [128, 1024], F32)
        hbf = [P.sb("hbf%d" % i, [128, 1024], BF16) for i in range(1)] * 2
        stgT = P.sb("stg", [128, 2, 512], F32)
        stg = [stgT[:, i, :] for i in range(2)]
        natab_full = stgT[:, :, :].rearrange("p a b -> p (a b)").bitcast(BF16).rearrange("p (s q) -> p s q", s=32)
        sgmT = P.sb("sgm", [128, 2, 512], F32)
        sgm = [sgmT[:, i, :] for i in range(2)]
        natab_band = sgmT[:, :, :].rearrange("p a b -> p (a b)").bitcast(BF16).rearrange("p (s q) -> p s q", s=32)
        identb = P.sb("identb", [128, 128], BF16)
        identf = P.sb("identf", [128, 128], F32)
        tri = [P.sb("triU", [128, 128], F32), P.sb("triL", [128, 128], F32)]
        onesb = P.sb("onesb", [128, 128], BF16)
        onesf = P.sb("onesf", [128, 128], F32)
        sm = P.sb("sm", [128, 104], F32)
        cbcs = [P.sb("cbc%d" % i, [128, 8, 128], BF16) for i in range(2)]
        sqb = [P.sb("sqb%d" % i, [128, 512], BF16) for i in range(2)]
        Ytok = P.sb("Ytok", [128, 2, 512], BF16)
        otmp = [P.sb("otmp%d" % i, [128, 128], F32) for i in range(4)]
        nrm2 = P.sb("nrm2", [128, 2, 8, 4], F32)
        negm = P.sb("negm", [128, 8, 4], F32)
        sgv = P.sb("sgv", [128, 128], F32)
        lamt = P.sb("lamt", [128, 4, 64], F32)
        kTc = P.sb("kTc", [128, 4, 512], BF16)
        vaugc = P.sb("vaugc", [128, 4, 528], BF16)
        nrm2c = P.sb("nrm2c", [128, 8], F32)
        nrm2s = P.sb("nrm2s", [128, 2, 8], F32)
        ropeRT = P.sb("ropeRT", [128, 128], F32)
        namask = P.sb("namask", [128, 64], F32)
        hin = P.sb("hin", [128, 4, 64], F32)
        hTo = P.sb("hTo", [128, 8, 256], BF16)
        selO = P.sb("selO", [128, 16], F32)
        convw = P.sb("convw", [128, 6, 5], F32)
        convb = P.sb("convb", [128, 6], F32)
        dtb = P.sb("dtb", [128, NT, 16], F32)
        lab = P.sb("lab", [128, NT, 16], F32)
        dtbias = P.sb("dtbias", [128, 16], F32)
        nega = P.sb("nega", [128, 16], F32)
        dsum = P.sb("dsum", [128, 16], F32)
        ssdg = P.sb("ssdg", [128, 512], F32)
        btok = P.sb("btok", [128, NT, 128], BF16)
        Rbraw = P.sb("Rb", [128, 1040], F32)
        Rb = Rbraw[:, 0:1024].rearrange("p (h i) -> p h i", h=8)
        xpre = Rbraw[:, :].rearrange("p (s q) -> p s q", s=4)
        seg = P.sb("seg", [128, 8, 128], F32)
        cacc = seg[:, :, :].rearrange("p a b -> p (a b)").rearrange("p (s q) -> p s q", s=4)
        dec = P.sb("dec", [128, 8, 128], BF16)
        MTb = P.sb("MTb", [128, 8, 128], BF16)
        cbm = P.sb("cbm", [128, 2, 128], BF16)
        xdt = P.sb("xdt", [128, 512], BF16)
        wxdt = P.sb("wxdt", [128, 512], BF16)
        acum = P.sb("acum", [128, 32], F32)
        hst = P.sb("hst", [128, 256], F32)
        hst2 = P.sb("hst2", [128, 256], F32)
        hstb2 = P.sb("hstb2", [128, 256], BF16)
        acum2 = P.sb("acum2", [128, 32], F32)
        cbm2 = P.sb("cbm2", [128, 2, 128], BF16)
        hstb = P.sb("hstb", [128, 256], BF16)
        ysb = htmp[:, 0:512]
        ytmp = htmp[:, 512:1024]
        houts = [P.sb("hout%d" % i, [128, 128], F32) for i in range(2)]
        pj = [P.ps("ps_pj%d" % i, [128, 512]) for i in range(3)]
        st = [P.ps("ps_st%d" % i, [128, 512]) for i in range(2)]
        ob = [P.ps("ps_ob%d" % i, [128, 512]) for i in range(2)]
        tpb = P.ps("ps_tpb", [128, 8, 128], BF16)
        ALLB = [(pj[i], "ps_pj%d" % i) for i in range(3)] + [(st[i], "ps_st%d" % i) for i in range(2)] + \
               [(ob[i], "ps_ob%d" % i) for i in range(2)]
        cnt = {"allb": 0, "pj": 0, "st": 0, "ob": 0, "w": 0, "stg": 0, "sgm": 0, "xt": 0, "sqb": 0, "otmp": 0, "hout": 0}

        def nxt(name, n):
            i = cnt[name] % n
            cnt[name] += 1
            return i

        P.dma("pool", "c_id", identb[:], D["ident"], w=["identb"])
        P.dma("sp", "c_misc", identf[:], D["ident"], w=["identf"])
        P.dma("sp", "c_misc", tri[0][:], D["triU"], w=["triU"])
        P.dma("sp", "c_misc", tri[1][:], D["triL"], w=["triL"])
        P.dve(lambda e: e.memset(onesb[:], 1.0), w=["onesb"])
        P.dve(lambda e: e.memset(onesf[:], 1.0), w=["onesf"])
        P.dma("sp", "c_misc", ropeRT[:], D["ropeRT"], w=["ropeRT"])
        P.dma("sp", "c_misc", namask[:], D["namask"], w=["namask"])
        P.dma("sp", "c_misc", selO[:], D["selO"], w=["selO"])

        def wload(view, kc, ncols):
            s = nxt("w", 2)
            P.dma("pool", "wq%d" % s, wb[s][:, 0:kc, 0:ncols], view.rearrange("(k p) c -> p k c", p=128),
                  w=["wb%d" % s])
            return wb[s], "wb%d" % s

        def rstd_from_ss(ss_ap, out_ap, n, rk, wk):
            P.act(ACT(out_ap, ss_ap, AF.Ln, scale=1.0 / n, bias=EPS), r=[rk], w=[wk + "_l"])
            P.act(ACT(out_ap, out_ap, AF.Exp, scale=-0.5), r=[wk + "_l"], w=[wk])

        def compute_ada(li):
            P.tag = "ada%d" % li
            for kd in range(2):
                c0 = 64 + kd * 16
                P.dma("sp", "ld_sm", sm[:, c0:c0 + 8], D["cvecT"][kd], w=[("sm_c", kd)])
                P.act(ACT(sm[:, c0 + 8:c0 + 16], sm[:, c0:c0 + 8], AF.Silu), r=[("sm_c", kd)], w=[("sm_sc", kd)])
                P.dve(CP(cbcs[kd][:], bc(sm[:, c0 + 8:c0 + 16], 2, [128, 8, 128])), r=[("sm_sc", kd)], w=["cbc%d" % kd])
            for cb in range(6):
                wt, wk = wload(D["w_ada"][li][:, cb * 512:(cb + 1) * 512], 8, 512)
                for kd in range(2):
                    P.dma("sp", "ld_ba%d" % kd, sgm[kd], D["b_ada"][li][cb * 512:(cb + 1) * 512].partition_broadcast(128),
                          w=["sgm%d" % kd])
                    b = nxt("pj", 3)
                    for k in range(8):
                        P.pe(MM(pj[b][:, :], cbcs[kd][:, k, :], wt[:, k, :], k == 0, k == 7),
                             r=["cbc%d" % kd, wk], w=["ps_pj%d" % b])
                    P.dve(TT(sgm[kd], sgm[kd], pj[b][:, :], ALU.add), r=["ps_pj%d" % b, "sgm%d" % kd], w=["sgm%d" % kd])
                    P.dma("sp", "st_ada%d" % kd, ada_scr[kd, li][:, cb * 512:(cb + 1) * 512], sgm[kd],
                          r=["sgm%d" % kd], w=[("ada_scr", kd, li)])

        def run_pass(kind, layers=None):
            samp = kind == 1
            xin = D["xs"] if samp else D["xp"]
            yout = O["ys"] if samp else O["yp"]

            for li in (range(n_layers) if layers is None else layers):
                xsrc = xin if li == 0 else xscr
                Win = D["w_in"][li]
                gath = samp and own_all
                if gath and li > 0:
                    xsrc = xgath
                own = gath or (samp and own_last and li == n_layers - 1 and n_layers > 1)
                next_own = samp and own_last and li == n_layers - 2 and not gath
                hTq, NTq, hqk = (hTo, 2, "hTo") if own else (hT, NT, "hT")
                xsrc_q = (D["xso"] if (gath and li == 0) else xscr_own) if own else xsrc
                xkq = "xscr_own" if own else ("xgath" if (gath and li > 0) else "xscr")
                xka = "xgath" if (gath and li > 0) else "xscr"
                P.tag = "%d.%d.pre" % (kind, li)
                P.dma("sp", "ld_ada", ada[:], ada_scr[kind, li], r=[("ada_scr", kind, li)], w=["ada"])
                P.dma("sp", "ld_gs", htmp[:], D["norm_g"][li].partition_broadcast(128),
                      w=["htmp", "ysb", "ytmp", ("ytmp", 0), ("ytmp", 1)])
                shift = ada[:, 0:1024]
                scale = ada[:, 1024:2048]
                gate = ada[:, 2048:3072]
                gs = scale
                P.dve(STT(gs, scale, 1.0, htmp[:], ALU.add, ALU.mult), r=["ada", "htmp"], w=["ada", "gs"])
                pre_tiles = [(xsrc, t, (xka, t), hT, "hT") for t in range(NT)]
                if own:
                    pre_tiles += [(xsrc_q, t, ("xscr_own", t), hTo, "hTo") for t in range(2)]
                HTK = ["htmp", "ysb", "ytmp", ("ytmp", 0), ("ytmp", 1)]

                def pre_tile(xs_, t, xkey, hdst, hk, par):
                    xtb, xk = xt[par], "xt%d" % par
                    if par == 0:
                        ht_, htk, hb_, hbk, c0, tp_, tpk = htmp[:], HTK, hbf[0][:], ["hbf0"], 16, tpb, "ps_tpb"
                    else:
                        ht_ = sgmT[:, :, :].rearrange("p a b -> p (a b)")
                        htk = ["sgm0", "sgm1"]
                        hb_, hbk, c0 = stgT[:, 0, :].bitcast(BF16), ["stg0"], 18
                        tp_, tpk = ob[1][:, :].bitcast(BF16).rearrange("p (k q) -> p k q", k=8), "ps_ob1"
                    ssk, rsk = "sm_ss%d" % par, "sm_rs%d" % par
                    P.dma("sp", "ld_x%d" % par, xtb[:], xs_[t * 128:(t + 1) * 128, :], r=[xkey], w=[xk])
                    yield
                    P.act(ACT(ht_, xtb[:], AF.Square, accum_out=sm[:, c0:c0 + 1]), r=[xk], w=htk + [ssk])
                    yield
                    P.act(ACT(sm[:, c0 + 1:c0 + 2], sm[:, c0:c0 + 1], AF.Ln, scale=1.0 / 1024, bias=EPS), r=[ssk], w=[rsk + "_l"])
                    yield
                    P.act(ACT(sm[:, c0 + 1:c0 + 2], sm[:, c0 + 1:c0 + 2], AF.Exp, scale=-0.5), r=[rsk + "_l"], w=[rsk])
                    yield
                    P.dve(STT(ht_, xtb[:], sm[:, c0 + 1:c0 + 2], gs, ALU.mult, ALU.mult), r=[xk, rsk, "gs", "ada"], w=htk)
                    yield
                    P.dve(TT(hb_, ht_, shift, ALU.add), r=htk + ["ada"], w=hbk)
                    yield
                    for k in range(8):
                        P.pe(TR(tp_[:, k, :], hb_[:, k * 128:(k + 1) * 128], identb[:]), r=hbk + ["identb"], w=[tpk])
                    yield
                    P.act(ACT(hdst[:, :, t * 128:(t + 1) * 128], tp_[:, :, :], AF.Copy), r=[tpk], w=[(hk, t)])

                def zip2(gens):
                    gens = list(gens)
                    while gens:
                        for g_ in list(gens):
                            try:
                                next(g_)
                            except StopIteration:
                                gens.remove(g_)
                for i_ in range(0, len(pre_tiles), 2):
                    zip2([pre_tile(*pre_tiles[i_ + j_], j_) for j_ in range(2) if i_ + j_ < len(pre_tiles)])
                hT_all = [("hT", t) for t in range(NT)]

                def proj_tm(col0, ncols, evac, wt=None, wk=None, wcol=0, q=False):
                    if wt is None:
                        wt, wk = wload(Win[:, col0:col0 + ncols], 8, ncols)
                        wcol = 0
                    hsrc, nt_, hk = (hTq, NTq, hqk) if q else (hT, NT, "hT")
                    for t in range(nt_):
                        b = nxt("pj", 3)
                        for k in range(8):
                            P.pe(MM(pj[b][:, 0:ncols], hsrc[:, k, t * 128:(t + 1) * 128],
                                    wt[:, k, wcol:wcol + ncols], k == 0, k == 7),
                                 r=[(hk, t), wk], w=["ps_pj%d" % b])
                        evac(pj[b], "ps_pj%d" % b, t)
                    return wt, wk

                def proj_fm(wt, wk, wcol, m, evac, q=False):
                    if q and own:
                        b = nxt("pj", 3)
                        for k in range(8):
                            P.pe(MM(pj[b][0:m, 0:256], wt[:, k, wcol:wcol + m], hTo[:, k, :], k == 0, k == 7),
                                 r=[("hTo", 0), ("hTo", 1), wk], w=["ps_pj%d" % b])
                        evac(pj[b], "ps_pj%d" % b, 0)
                        return
                    for tb in range(2):
                        b = nxt("pj", 3)
                        for k in range(8):
                            P.pe(MM(pj[b][0:m, :], wt[:, k, wcol:wcol + m], hT[:, k, tb * 512:(tb + 1) * 512],
                                    k == 0, k == 7),
                                 r=hT_all[tb * 4:(tb + 1) * 4] + [wk], w=["ps_pj%d" % b])
                        evac(pj[b], "ps_pj%d" % b, tb)

                def branch_merge(w_br, mcol, first):
                    P.tag = "%d.%d.merge" % (kind, li)
                    for cb in range(2):
                        wA, wAk = wload(w_br[:, cb * 512:(cb + 1) * 512], 4, 512)
                        wM, wMk = wload(Win[:, mcol + cb * 512: mcol + (cb + 1) * 512], 8, 512)
                        for t in range(NTq):
                            pA, kA_ = ALLB[nxt("allb", 7)]
                            for k in range(4):
                                P.pe(MM(pA[:, :], YT[:, k, t * 128:(t + 1) * 128], wA[:, k, :], k == 0, k == 3),
                                     r=[("YT", t), wAk], w=[kA_])
                            pL, kL_ = ALLB[nxt("allb", 7)]
                            for k in range(8):
                                P.pe(MM(pL[:, :], hTq[:, k, t * 128:(t + 1) * 128], wM[:, k, :], k == 0, k == 7),
                                     r=[(hqk, t), wMk], w=[kL_])
                            si = nxt("sgm", 2)
                            P.act(ACT(sgm[si], pL[:, :], AF.Sigmoid), r=[kL_], w=["sgm%d" % si])
                            asl = acc[:, t, cb * 512:(cb + 1) * 512]
                            if first:
                                P.dve(TT(asl, sgm[si], pA[:, :], ALU.mult),
                                      r=["sgm%d" % si, kA_], w=[("acc", t, cb)])
                            else:
                                P.dve(TT(sgm[si], sgm[si], pA[:, :], ALU.mult),
                                      r=["sgm%d" % si, kA_], w=["sgm%d" % si])
                                P.pool(TT(asl, asl, sgm[si], ALU.add),
                                       r=["sgm%d" % si, ("acc", t, cb)], w=[("acc", t, cb)])

                def ytok_to_YT(t, qt):
                    for c in range(4):
                        P.pe(TR(tpb[:, c, :], Ytok[:, qt, c * 128:(c + 1) * 128], identb[:]),
                             r=[("Ytok", qt), "identb"], w=["ps_tpb"])
                    P.act(ACT(YT[:, :, t * 128:(t + 1) * 128], tpb[:, 0:4, :], AF.Copy), r=["ps_tpb"], w=[("YT", t)])

                def attn_mixer(mx):
                    isA = mx == "A"
                    P.tag = "%d.%d.%s.proj" % (kind, li, mx)
                    qc, kc, vc, gc = (QA, KA, VA, GA) if isA else (QC, KC, VC, GC)
                    nh = 4 if isA else 8
                    e = 128 if isA else 64
                    nm = 2 if isA else 1
                    okey, vkey = ("ndk", "ndv") if isA else ("nnk", "nnv")
                    va = vaug[:, :, 0:nh * (e + 2)].rearrange("p t (h e) -> p t h e", h=nh)
                    P.dve(lambda en: en.memset(va[:, :, :, e:e + 1], 1.0), w=[("vaug", t) for t in range(NT)])

                    ropeCo, ropeSo = acc[:, 4, 0:256], acc[:, 4, 256:512]
                    maskO_bf = acc[:, 5, :].bitcast(BF16).rearrange("p (s q) -> p s q", s=32)
                    ropeC = Rbraw[:, 0:1024]
                    ropeS = seg[:, :, :].rearrange("p a b -> p (a b)")
                    SEGK = [("seg", 0), ("seg", 1)]
                    if samp:
                        vca = vaugc[:, :, 0:nh * (e + 2)].rearrange("p t (h e) -> p t h e", h=nh)
                        ck = D["cdk" if isA else "cnk"][li]
                        cv = D["cdv" if isA else "cnv"][li]
                        P.dma("pool", "ld_ck", gbuf[:, 0:4, :], ck.rearrange("(t p) c -> p t c", p=128),
                              w=[("gbuf", t) for t in range(4)])
                        for tl in range(4):
                            P.dma("pool", "ld_cv", vca[:, tl, :, 0:e],
                                  cv[tl * 128:(tl + 1) * 128, :].rearrange("p (h e) -> p h e", h=nh), w=["vaugc"])
                        P.dve(lambda en: en.memset(vca[:, :, :, e:e + 1], 1.0), w=["vaugc1"])
                        for tl in range(4):
                            for pr in range(4):
                                P.pe(TR(tpb[:, pr, :], gbuf[:, tl, pr * 128:(pr + 1) * 128], identb[:]),
                                     r=[("gbuf", tl), "identb"], w=["ps_tpb"])
                            P.act(ACT(kTc[:, :, tl * 128:(tl + 1) * 128], tpb[:, 0:4, :], AF.Copy), r=["ps_tpb"],
                                  w=[("kTc", tl, 0), ("kTc", tl, 1)])
                        for hm in range(8):
                            pb = (hm % 2) * 64
                            si = nxt("sqb", 2)
                            P.act(ACT(sqb[si][0:64, :], kTc[pb:pb + 64, (hm // 2), :], AF.Square),
                                  r=[("kTc", tl, (hm % 2)) for tl in range(4)], w=["sqb%d" % si])
                            b = nxt("st", 2)
                            P.pe(MM(st[b][:, :], onesb[0:64, :], sqb[si][0:64, :]), r=["sqb%d" % si, "onesb"], w=["ps_st%d" % b])
                            P.dve(lambda en, b=b, hm=hm: en.tensor_reduce(out=nrm2c[:, hm:hm + 1], in_=st[b][:, :], axis=AX.X,
                                                                          op=ALU.max), r=["ps_st%d" % b], w=[("nrm2c", hm)])
                        if isA:
                            P.dma("sp", "ld_rope", ropeC, D["ropeC"], w=["Rb"])
                            P.dma("sp", "ld_rope", ropeS, D["ropeS"], w=SEGK)
                            if own:
                                P.dma("sp", "ld_rope", acc[:, 4, 0:256], D["ropeCo"], w=[("acc", 4, 0)])
                                P.dma("sp", "ld_rope", acc[:, 4, 256:512], D["ropeSo"], w=[("acc", 4, 0)])
                        elif own:
                            P.dma("pool", "ld_mo", maskO_bf, D["maskO"], w=[("acc", 5, 0), ("acc", 5, 1)])
                    pend = []
                    for which, c0, dst in ((0, qc, qT), (1, kc, kT)):
                        wt, wk = wload(Win[:, c0:c0 + 512], 8, 512)
                        for pr in range(4):
                            def ev(ps, pk, tb, pr=pr, which=which, dst=dst):
                                qo = own and which == 0
                                W = 256 if qo else 512
                                dsl = dst[:, pr, tb * 512:tb * 512 + W]
                                wkeys = [("qkT", which, pr, tb, 0), ("qkT", which, pr, tb, 1)]
                                if samp and isA:
                                    qi = nxt("sgm", 2)
                                    qf, qk_ = sgm[qi][:, 0:W], "sgm%d" % qi
                                    rc = ropeCo if qo else ropeC[:, tb * 512:(tb + 1) * 512]
                                    rs = ropeSo if qo else ropeS[:, tb * 512:(tb + 1) * 512]
                                    rk = [("acc", 4, 0)] if qo else ["Rb"] + SEGK
                                    P.act(ACT(qf, ps[:, 0:W], AF.Copy), r=[pk], w=[qk_])

                                    def rope_part(qf=qf, qk_=qk_, rc=rc, rs=rs, rk=rk, W=W, dsl=dsl, wkeys=wkeys):
                                        br = nxt("st", 2)
                                        P.pe(MM(st[br][:, 0:W], ropeRT[:], qf), r=[qk_, "ropeRT"], w=["ps_st%d" % br],
                                             ni=2 if W == 512 else 1)
                                        P.dve(TT(qf, qf, rc, ALU.mult), r=[qk_] + rk, w=[qk_])
                                        P.dve(TT(ysb[:, 0:W], st[br][:, 0:W], rs, ALU.mult), r=["ps_st%d" % br] + rk, w=["ysb"])
                                        P.dve(TT(dsl, qf, ysb[:, 0:W], ALU.add), r=[qk_, "ysb"], w=wkeys)
                                    pend.append(rope_part)
                                else:
                                    P.act(ACT(dsl, ps[:, 0:W], AF.Copy), r=[pk], w=wkeys)
                                si = nxt("sqb", 2)
                                P.act(ACT(sqb[si][:, 0:W], ps[:, 0:W], AF.Square), r=[pk], w=["sqb%d" % si])
                                for half in range(2):
                                    def norm_part(si=si, tb=tb, half=half, hm=2 * pr + half, W=W):
                                        b = nxt("st", 2)
                                        P.pe(MM(st[b][:, 0:W], onesb[half * 64:(half + 1) * 64, :], sqb[si][half * 64:(half + 1) * 64, 0:W]),
                                             r=["sqb%d" % si, "onesb"], w=["ps_st%d" % b])
                                        ns = W // 256
                                        P.dve(lambda en, b=b: en.tensor_reduce(
                                            out=nrm2[:, which, hm, tb * 2:tb * 2 + ns],
                                            in_=st[b][:, 0:W].rearrange("p (s q) -> p s q", s=ns), axis=AX.X, op=ALU.max),
                                            r=["ps_st%d" % b], w=[("nrm2", which, hm, tb)])
                                    pend.append(norm_part)
                                while len(pend) > (3 if (samp and isA) else 2):
                                    pend.pop(0)()
                            proj_fm(wt, wk, pr * 128, 128, ev, q=(which == 0))
                        while pend:
                            pend.pop(0)()
                        if which == 1 and not samp:
                            def evk(ps, pk, t):
                                si = nxt("stg", 2)
                                P.act(ACT(stg[si], ps[:, :], AF.Copy), r=[pk], w=["stg%d" % si])
                                P.dma("sp", "o_%s%d" % (okey, si), O[okey][t // 2, li, (t % 2) * 128:(t % 2 + 1) * 128, :],
                                      stg[si], r=["stg%d" % si])
                            proj_tm(c0, 512, evk, wt, wk, 0)
                    def evv(ps, pk, t):
                        if not samp:
                            si = nxt("stg", 2)
                            P.act(ACT(stg[si], ps[:, :], AF.Copy), r=[pk], w=["stg%d" % si])
                            P.dma("sp", "o_%s%d" % (vkey, si), O[vkey][t // 2, li, (t % 2) * 128:(t % 2 + 1) * 128, :],
                                  stg[si], r=["stg%d" % si])
                        P.dve(CP(va[:, t, :, 0:e], ps[:, :].rearrange("p (h e) -> p h e", h=nh)),
                              r=[pk], w=[("vaug", t)])
                    proj_tm(vc, 512, evv)
                    def evg(ps, pk, t):
                        P.act(ACT(gbuf[:, t, :], ps[:, :], AF.Silu), r=[pk], w=[("gbuf", t)])
                    proj_tm(gc, 512, evg, q=True)
                    nr = [("nrm2", w_, hm, tb) for w_ in range(2) for hm in range(8) for tb in range(2)]
                    if own:
                        nr = [("nrm2", 0, hm, 0) for hm in range(8)] + [("nrm2", 1, hm, tb) for hm in range(8) for tb in range(2)]
                        P.dve(CP(nrm2s[:, 0, :], nrm2[:, 0, :, 0]), r=nr, w=["nrm2s"])
                        P.dve(lambda en: en.tensor_reduce(out=nrm2s[:, 1, :], in_=nrm2[:, 1, :, :], axis=AX.X, op=ALU.max),
                              r=nr + ["nrm2s"], w=["nrm2s"])
                    elif samp:
                        P.dve(lambda en: en.tensor_reduce(out=nrm2s[:], in_=nrm2[:, :, :, :], axis=AX.X, op=ALU.max),
                              r=nr, w=["nrm2s"])
                    if samp:
                        P.dve(TT(nrm2s[:, 1, :], nrm2s[:, 1, :], nrm2c[:], ALU.max),
                              r=["nrm2s"] + [("nrm2c", hm) for hm in range(8)], w=["nrm2s"])
                        nmv = negm[:, :, 0]
                        P.dve(TT(nmv, nrm2s[:, 0, :], nrm2s[:, 1, :], ALU.mult), r=["nrm2s"], w=["negm0"])
                    else:
                        nmv = negm[:]
                        P.dve(TT(nmv, nrm2[:, 0, :, :], nrm2[:, 1, :, :], ALU.mult), r=nr, w=["negm0"])
                    P.act(ACT(nmv, nmv, AF.Sqrt), r=["negm0"], w=["negm1"])
                    P.dve(TS(nmv, nmv, -0.125, ALU.mult), r=["negm1"], w=["negm"])
                    if isA:
                        lam_init = 0.8 - 0.6 * math.exp(-0.3 * li)
                        P.dma("sp", "ld_lam", lamt[:], D["lamv"][li].partition_broadcast(128), w=["lamt"])
                        P.dve(TT(lamt[:, 0:2, :], lamt[:, 0:2, :], lamt[:, 2:4, :], ALU.mult), r=["lamt"], w=["lamt2"])
                        P.dve(lambda en: en.tensor_reduce(out=sm[:, 20:22], in_=lamt[:, 0:2, :], axis=AX.X, op=ALU.add),
                              r=["lamt2"], w=["sm_l0"])
                        P.act(ACT(sm[:, 22:24], sm[:, 20:22], AF.Exp), r=["sm_l0"], w=["sm_l1"])
                        P.dve(STT(sm[:, 24:25], sm[:, 23:24], -lam_init, sm[:, 22:23], ALU.add, ALU.subtract),
                              r=["sm_l1"], w=["sm_nl"])
                        P.dma("sp", "ld_sg", sgv[:], D["subln_g"][li].partition_broadcast(128), w=["sgv0"])
                        P.dve(TS(sgv[:], sgv[:], 1.0 - lam_init, ALU.mult), r=["sgv0"], w=["sgv"])

                    if isA and kind == 0 and li == 0:
                        for l_ in range(1, n_layers):
                            compute_ada(l_)
                    P.tag = "%d.%d.%s.attn" % (kind, li, mx)

                    def o_post(ov, obk, t, h, ysl, gsl, gkey, ykey, sb=32):
                        K = lambda n: "%s_%d" % (n, sb)
                        P.dve(lambda en, ov=ov: en.reciprocal(out=sm[:, sb:sb + nm], in_=ov[:, :, e]),
                              r=[obk], w=[K("sm_rl")])
                        oi = nxt("otmp", 4)
                        yield
                        if isA:
                            P.dve(TT(sm[:, sb + 2:sb + 3], sm[:, sb + 1:sb + 2], sm[:, 24:25], ALU.mult),
                                  r=[K("sm_rl"), "sm_nl"], w=[K("sm_c1")])
                            P.act(ACT(otmp[oi][:], ov[:, 0, 0:e], AF.Copy, scale=sm[:, sb:sb + 1]),
                                  r=[obk, K("sm_rl")], w=["otmp%d" % oi])
                            yield
                            P.dve(STT(otmp[oi][:], ov[:, 1, 0:e], sm[:, sb + 2:sb + 3], otmp[oi][:], ALU.mult, ALU.add),
                                  r=[obk, K("sm_c1"), "otmp%d" % oi], w=["otmp%d" % oi])
                            oj = nxt("otmp", 4)
                            yield
                            P.act(ACT(otmp[oj][:], otmp[oi][:], AF.Square, accum_out=sm[:, sb + 4:sb + 5]),
                                  r=["otmp%d" % oi], w=["otmp%d" % oj, K("sm_os")])
                            yield
                            P.act(ACT(sm[:, sb + 5:sb + 6], sm[:, sb + 4:sb + 5], AF.Ln, scale=1.0 / 128, bias=EPS),
                                  r=[K("sm_os")], w=[K("sm_or") + "_l"])
                            yield
                            P.act(ACT(sm[:, sb + 5:sb + 6], sm[:, sb + 5:sb + 6], AF.Exp, scale=-0.5),
                                  r=[K("sm_or") + "_l"], w=[K("sm_or")])
                            yield
                            P.dve(STT(otmp[oi][:], otmp[oi][:], sm[:, sb + 5:sb + 6], sgv[:], ALU.mult, ALU.mult),
                                  r=["otmp%d" % oi, K("sm_or"), "sgv"], w=["otmp%d" % oi])
                            yield
                            P.dve(TT(ysl, otmp[oi][:], gsl, ALU.mult), r=["otmp%d" % oi, gkey], w=[ykey])
                        else:
                            P.act(ACT(otmp[oi][:, 0:e], ov[:, 0, 0:e], AF.Copy, scale=sm[:, sb:sb + 1]),
                                  r=[obk, K("sm_rl")], w=["otmp%d" % oi])
                            yield
                            P.dve(TT(ysl, otmp[oi][:, 0:e], gsl, ALU.mult), r=["otmp%d" % oi, gkey], w=[ykey])

                    def zipg(gens):
                        gens = list(gens)
                        while gens:
                            for g_ in list(gens):
                                try:
                                    next(g_)
                                except StopIteration:
                                    gens.remove(g_)

                    if samp:
                        NAT = {0: [0, 1, 2, 3], 1: [0, 1, 2, 3, 4, 5], 2: [2, 3, 4, 5, 6, 7], 3: [4, 5, 6, 7]}
                        if not isA and not own:
                            P.dve(lambda en: en.memset(natab_full[:, :, :], NEG), w=["stg0", "stg1"])
                            P.dve(lambda en: en.memset(natab_band[:, :, :], NEG), w=["sgm0", "sgm1"])
                        for h in range(nh):
                            if not isA and own:
                                P.dma("pool", "ld_rpb", natab_full[:, :, :], D["rpbO"][li, h], w=["stg0", "stg1"])
                                P.dve(STT(natab_full[:, :, :], natab_full[:, :, :], 8.0, maskO_bf, ALU.mult, ALU.add),
                                      r=["stg0", "stg1", ("acc", 5, 0), ("acc", 5, 1)], w=["stg0", "stg1"])
                            elif not isA:
                                rp = D["rpbT"][li, h]
                                for tab, tkeys, lo0, dd0, n in ((natab_full, ["stg0", "stg1"], 8, 0, 15),
                                                                (natab_band, ["sgm0", "sgm1"], 12, 4, 8)):
                                    for half in range(2):
                                        sl0 = lo0 + half
                                        P.dma("pool", "ld_rpb", tab[half * 64:(half + 1) * 64, sl0:sl0 + n, :],
                                              rp[dd0:dd0 + n].rearrange("d k q -> k d q"), w=tkeys)
                                        tsl = tab[half * 64:(half + 1) * 64, sl0:sl0 + n, :]
                                        P.dve(STT(tsl, tsl, 8.0, bc(namask[half * 64:(half + 1) * 64, :], 1, [64, n, 64]),
                                                  ALU.mult, ALU.add), r=tkeys + ["namask"], w=tkeys)
                            for qb in ([0] if own else range(4)):
                                if isA or own:
                                    tiles = [("l", kt) for kt in range(8)] + [("c", kt) for kt in range(4)]
                                else:
                                    tiles = [("l", a) for a in NAT[qb]] + [("c", kt) for kt in range(4)]
                                nt_ = len(tiles)
                                for m in range(nm):
                                    hm = nm * h + m
                                    pb = (hm % 2) * 64
                                    qv = qT[pb:pb + 64, (hm // 2), qb * 256:(qb + 1) * 256]
                                    pvq = []
                                    for i, (kd, kt) in enumerate(tiles):
                                        b = nxt("st", 2)
                                        bias = (not isA) and kd == "l"
                                        if kd == "l":
                                            ksl = kT[pb:pb + 64, (hm // 2), kt * 128:(kt + 1) * 128]
                                            kr = [("qkT", 1, (hm // 2), kt // 4, (hm % 2))]
                                        else:
                                            ksl = kTc[pb:pb + 64, (hm // 2), kt * 128:(kt + 1) * 128]
                                            kr = [("kTc", kt, (hm % 2))]
                                        P.pe(MM(st[b][:, 0:256], ksl, qv, True, not bias),
                                             r=kr + [("qkT", 0, (hm // 2), qb // 2, (hm % 2))], w=["ps_st%d" % b])
                                        if bias:
                                            tab, tkeys = (natab_full, ["stg0", "stg1"]) if (qb in (0, 3) or own) else (natab_band, ["sgm0", "sgm1"])
                                            s0 = 4 * kt if own else 15 - 2 * kt + 4 * qb
                                            P.pe(MM(st[b][:, 0:256], identb[:],
                                                    tab[:, s0:s0 + 4, :].rearrange("p s q -> p (s q)"), False, True),
                                                 r=tkeys + ["identb"], w=["ps_st%d" % b])
                                        P.act(ACT(PTb[:, i, :], st[b][:, 0:256], AF.Exp, scale=0.125, bias=negm[:, hm, 0:1]),
                                              r=["ps_st%d" % b, "negm"], w=[("PT", i)])

                                        def pv(i=i, kd=kd, kt=kt, m=m):
                                            for qt in range(2):
                                                ov = ob[qt][:, 0:nm * (e + 1)].rearrange("p (m e) -> p m e", m=nm)
                                                if kd == "l":
                                                    vsl, vr = va[:, kt, h, 0:e + 1], [("vaug", kt)]
                                                else:
                                                    vsl, vr = vca[:, kt, h, 0:e + 1], ["vaugc", "vaugc1"]
                                                P.pe(MM(ov[:, m, :], PTb[:, i, qt * 128:(qt + 1) * 128], vsl, i == 0, i == nt_ - 1),
                                                     r=[("PT", i)] + vr, w=["ps_ob%d" % qt])
                                        pvq.append(pv)
                                        while len(pvq) > 2:
                                            pvq.pop(0)()
                                    while pvq:
                                        pvq.pop(0)()
                                posts = []
                                for qt in range(2):
                                    t = 2 * qb + qt
                                    ov = ob[qt][:, 0:nm * (e + 1)].rearrange("p (m e) -> p m e", m=nm)
                                    gsl = gbuf[:, t, h * e:(h + 1) * e]
                                    posts.append(o_post(ov, "ps_ob%d" % qt, t, h, gsl, gsl, ("gbuf", t), ("gbuf", t), sb=32 if qt == 0 else 96))
                                zipg(posts)
                        for t in range(NTq):
                            for c in range(4):
                                P.pe(TR(tpb[:, c, :], gbuf[:, t, c * 128:(c + 1) * 128], identb[:]),
                                     r=[("gbuf", t), "identb"], w=["ps_tpb"])
                            P.act(ACT(YT[:, :, t * 128:(t + 1) * 128], tpb[:, 0:4, :], AF.Copy), r=["ps_tpb"], w=[("YT", t)])

                    def p_stage1(s, h, sb):
                        for m in range(nm):
                            hm = nm * h + m
                            pb = (hm % 2) * 64
                            qv = qT[pb:pb + 64, (hm // 2), s * 256:(s + 1) * 256]
                            for kt in range(2):
                                b = nxt("st", 2)
                                P.pe(MM(st[b][:, 0:256], kT[pb:pb + 64, (hm // 2), s * 256 + kt * 128: s * 256 + (kt + 1) * 128], qv),
                                     r=[("qkT", 0, (hm // 2), s // 2, (hm % 2)), ("qkT", 1, (hm // 2), s // 2, (hm % 2))], w=["ps_st%d" % b])
                                P.act(ACT(PTb[:, sb + m * 2 + kt, :], st[b][:, 0:256], AF.Exp, scale=0.125,
                                          bias=negm[:, hm, s:s + 1]),
                                      r=["ps_st%d" % b, "negm"], w=[("PT", sb + m * 2 + kt)])

                    def p_stage2(s, h, sb):
                        posts = []
                        for qt in range(2):
                            t = 2 * s + qt
                            b = nxt("ob", 2)
                            ov = ob[b][:, 0:nm * (e + 1)].rearrange("p (m e) -> p m e", m=nm)
                            for m in range(nm):
                                for kt in range(2):
                                    P.pe(MM(ov[:, m, :], PTb[:, sb + m * 2 + kt, qt * 128:(qt + 1) * 128],
                                            va[:, 2 * s + kt, h, 0:e + 1], kt == 0, kt == 1),
                                         r=[("PT", sb + m * 2 + kt), ("vaug", 2 * s + kt)], w=["ps_ob%d" % b])
                            posts.append(o_post(ov, "ps_ob%d" % b, t, h, Ytok[:, qt, h * e:(h + 1) * e], gbuf[:, t, h * e:(h + 1) * e],
                                                ("gbuf", t), ("Ytok", qt), sb=32 if qt == 0 else 96))
                        zipg(posts)

                    hcnt = 0
                    for s in range(4 if not samp else 0):
                        pending = None
                        for h in range(nh):
                            sb = (hcnt % 2) * 4
                            hcnt += 1
                            p_stage1(s, h, sb)
                            if pending is not None:
                                p_stage2(*pending)
                            pending = (s, h, sb)
                        p_stage2(*pending)
                        for qt in range(2):
                            ytok_to_YT(2 * s + qt, qt)

                    branch_merge(D["w_br_a"][li] if isA else D["w_br_c"][li], MRG + (0 if isA else 2048), isA)

                def ssd_mixer():
                    xbcT = kT
                    P.tag = "%d.%d.B.proj" % (kind, li)
                    SEGK2 = [("seg", 0), ("seg", 1)]
                    xtok = vaug[:, :, 0:512]
                    def evz(ps, pk, t):
                        P.act(ACT(gbuf[:, t, :], ps[:, :], AF.Silu), r=[pk], w=[("gbuf", t)])
                    proj_tm(ZB, 512, evz, q=True)
                    P.dma("sp", "ld_cw", convw[:], D["convwT"][li], w=["convw"])
                    P.dma("sp", "ld_cw", convb[:], D["convbT"][li], w=["convb"])
                    P.dma("sp", "ld_cw", dtbias[:], D["dt_bias"][li].partition_broadcast(128), w=["dtbias"])
                    P.dma("sp", "ld_cw", nega[:], D["a_log"][li].partition_broadcast(128), w=["nega0"])
                    P.dma("sp", "ld_cw", dsum[:], D["d_skip"][li].partition_broadcast(128), w=["dsum0"])
                    P.dma("sp", "ld_cw", ssdg[:], D["ssd_norm_g"][li].partition_broadcast(128), w=["ssdg"])
                    P.act(ACT(nega[:], nega[:], AF.Exp), r=["nega0"], w=["nega1"])
                    P.dve(TS(nega[:], nega[:], -1.0, ALU.mult), r=["nega1"], w=["nega"])
                    P.dve(TT(dsum[:, 0:8], dsum[:, 0:8], dsum[:, 8:16], ALU.add), r=["dsum0"], w=["dsum"])
                    P.dve(lambda en: en.memset(xpre[:], 0.0), w=["Rb", ("Rbh", 0), ("Rbh", 1)])
                    for blk, (c0, ncol) in enumerate(((XB, 512), (XB + 512, 256))):
                        wt, wk = wload(Win[:, c0:c0 + ncol], 8, ncol)
                        for cc in range(ncol // 128):
                            c = blk * 4 + cc

                            def evx_s(ps, pk, tb, c=c):
                                xps = Rbraw[:, 0:1028]
                                caf = seg[:, :, :].rearrange("p a b -> p (a b)")
                                P.act(ACT(xps[:, 2 + tb * 512:2 + (tb + 1) * 512], ps[:, :], AF.Copy), r=[pk], w=["Rb"])
                                if tb == 0:
                                    return
                                P.dve(TS(caf, xps[:, 0:1024], convw[:, c, 0:1], ALU.mult), r=["Rb", "convw"], w=SEGK2)
                                for j in range(1, 5):
                                    P.dve(STT(caf, xps[:, j:j + 1024], convw[:, c, j:j + 1], caf, ALU.mult, ALU.add),
                                          r=["Rb", "convw"] + SEGK2, w=SEGK2)
                                P.act(ACT(xbcT[:, c, :], caf, AF.Silu, bias=convb[:, c:c + 1]), r=SEGK2 + ["convb"],
                                      w=[("qkT", 1, c, tb_, hf) for tb_ in range(2) for hf in range(2)])

                            def evx(ps, pk, tb, c=c):
                                P.act(ACT(xpre[:, 2 * tb:2 * tb + 2, 2:258], ps[:, :].rearrange("p (s q) -> p s q", s=2),
                                          AF.Copy), r=[pk, "Rb"], w=[("Rbh", tb)])
                                sl = slice(2 * tb, 2 * tb + 2)
                                P.dve(TS(cacc[:, sl, :], xpre[:, sl, 0:256], convw[:, c, 0:1], ALU.mult),
                                      r=["Rb", ("Rbh", tb), "convw"], w=[("seg", tb)])
                                for j in range(1, 5):
                                    P.dve(STT(cacc[:, sl, :], xpre[:, sl, j:j + 256], convw[:, c, j:j + 1], cacc[:, sl, :],
                                              ALU.mult, ALU.add), r=["Rb", ("Rbh", tb), "convw", ("seg", tb)], w=[("seg", tb)])
                                P.act(ACT(xbcT[:, c, tb * 512:(tb + 1) * 512].rearrange("p (s q) -> p s q", s=2),
                                          cacc[:, sl, :], AF.Silu, bias=convb[:, c:c + 1]),
                                      r=[("seg", tb), "convb"], w=[("qkT", 1, c, tb, 0), ("qkT", 1, c, tb, 1)])
                            proj_fm(wt, wk, cc * 128, 128, evx_s if samp else evx)
                    wt_dt, wk_dt = wload(Win[:, DTB:DTB + 16], 8, 16)
                    bdt = nxt("pj", 3)
                    for t in range(NT):
                        for k in range(8):
                            P.pe(MM(pj[bdt][:, t * 16:(t + 1) * 16], hT[:, k, t * 128:(t + 1) * 128], wt_dt[:, k, 0:16], k == 0, k == 7),
                                 r=[("hT", t), wk_dt], w=["ps_pj%d" % bdt])
                    DT0 = [("dt0", t) for t in range(NT)]
                    DT1 = [("dt1", t) for t in range(NT)]
                    DTK = [("dt", t) for t in range(NT)]
                    P.dve(TT(dtb[:, :, :], pj[bdt][:, 0:128].rearrange("p (t c) -> p t c", t=NT), bc(dtbias[:], 1, [128, NT, 16]), ALU.add),
                          r=["ps_pj%d" % bdt, "dtbias"], w=DT0)
                    P.act(ACT(dtb[:, :, :], dtb[:, :, :], AF.Exp), r=DT0, w=DT1)
                    P.act(ACT(dtb[:, :, :], dtb[:, :, :], AF.Ln, bias=1.0), r=DT1, w=DTK)
                    P.dve(TT(lab[:, :, :], dtb[:, :, :], bc(nega[:], 1, [128, NT, 16]), ALU.mult), r=DTK + ["nega"],
                          w=[("la", t) for t in range(NT)])
                    for t in range(NT):
                        for c in range(5):
                            P.pe(TR(tpb[:, c, :], xbcT[:, c, t * 128:(t + 1) * 128], identb[:]),
                                 r=[("qkT", 1, c, t // 4, 0), ("qkT", 1, c, t // 4, 1), "identb"], w=["ps_tpb"])
                        P.act(ACT(xtok[:, t, :], tpb[:, 0:4, :], AF.Copy), r=["ps_tpb"], w=[("vaug", t)])
                        P.act(ACT(btok[:, t, :], tpb[:, 4, :], AF.Copy), r=["ps_tpb"], w=[("btok", t)])

                    P.tag = "%d.%d.B.loop" % (kind, li)
                    nseq = 1 if samp else 4
                    nys = 8 if samp else 2
                    nch = NT // nseq
                    D1MAP = {"Rb": "xt0", ("seg", 0): "xt1", ("seg", 1): "xt1", "dec": "stg0", ("MT", 0): "stg1", ("MT", 1): "stg1",
                             "xdt": "hbf0", "wxdt": "hbf0", "ysb": "sgm0", ("ytmp", 0): "sgm1", ("ytmp", 1): "sgm1"}
                    D1OWN = {"acum", "eacum", "wexp0", "wexp", "cdec", ("cbm", 0), ("cbm", 1), ("hst", 0), ("hst", 1), "hstb"}

                    def km1(k):
                        if k in D1MAP:
                            return D1MAP[k]
                        if k in D1OWN:
                            return (k, "d1")
                        return k
                    BUF = {0: (acum, Rb, seg, dec, cbm, MTb, xdt, wxdt, hst, hstb, ysb, ytmp),
                           1: (acum2, xt[0][:, :].rearrange("p (h i) -> p h i", h=8), xt[1][:, :].rearrange("p (h i) -> p h i", h=8),
                               stgT[:, 0, :].bitcast(BF16).rearrange("p (h i) -> p h i", h=8),
                               cbm2, stgT[:, 1, :].bitcast(BF16).rearrange("p (h i) -> p h i", h=8),
                               hbf[0][:, 0:512], hbf[0][:, 512:1024], hst2, hstb2, sgm[0], sgm[1])}

                    pcnt = {("s", 0): 0, ("s", 1): 0, ("o", 0): 0, ("o", 1): 0}

                    def chunk_step(s, d, c, hstate, ysw):
                        acum, Rb, seg, dec, cbm, MTb, xdt, wxdt, hst, hstb, ysb, ytmp = BUF[d]
                        cd0 = 40 + 16 * d
                        STL = [(st[0], "ps_st0"), (st[1], "ps_st1")] if d == 0 else [(pj[0], "ps_pj0"), (pj[1], "ps_pj1")]
                        OBL = [(ob[0], "ps_ob0"), (ob[1], "ps_ob1")] if d == 0 else [(pj[2], "ps_pj2")]

                        def nxs():
                            pcnt[("s", d)] += 1
                            return STL[pcnt[("s", d)] % len(STL)]

                        def nxo():
                            pcnt[("o", d)] += 1
                            return OBL[pcnt[("o", d)] % len(OBL)]
                        t = s * nch + c
                        la_t = lab[:, t, d * 8:(d + 1) * 8]
                        ob_b, ko_b = nxo()
                        P.pe(MM(ob_b[:, 0:8], tri[d][:], la_t), r=[("la", t), "triU", "triL"], w=[ko_b])
                        P.pe(MM(ob_b[:, 8:16], onesf[:], la_t), r=[("la", t), "onesf"], w=[ko_b])
                        P.act(ACT(acum[:, 0:16], ob_b[:, 0:16], AF.Copy), r=[ko_b], w=["acum"])
                        yield
                        P.act(ACT(acum[:, 16:24], acum[:, 0:8], AF.Exp), r=["acum"], w=["eacum"])
                        P.dve(TT(acum[:, 24:32], acum[:, 8:16], acum[:, 0:8], ALU.subtract), r=["acum"], w=["wexp0"])
                        P.act(ACT(acum[:, 24:32], acum[:, 24:32], AF.Exp), r=["wexp0"], w=["wexp"])
                        P.act(ACT(sm[:, cd0:cd0 + 8], acum[:, 8:16], AF.Exp), r=["acum"], w=["cdec"])
                        yield
                        P.dve(TT(Rb[:], bc(tri[d][:], 1, [128, 8, 128]), bc(la_t, 2, [128, 8, 128]), ALU.mult),
                              r=[("la", t), "triU", "triL"], w=["Rb", ("Rbh", 0), ("Rbh", 1)])
                        sb_ = []
                        for hh in range(2):
                            yield
                            st_b2, k_b2 = nxs()
                            P.pe(MM(st_b2[:, :], onesf[:], Rb[:, hh * 4:(hh + 1) * 4, :]), r=["Rb", "onesf"],
                                 w=[k_b2], ni=2)
                            for h4 in range(4):
                                hd = hh * 4 + h4
                                P.dve(TS(seg[:, hd, :], st_b2[:, h4 * 128:(h4 + 1) * 128], acum[:, hd:hd + 1],
                                         ALU.subtract, 0.0, ALU.min),
                                      r=[k_b2, "acum"], w=[("seg", hh)])
                        yield
                        P.act(ACT(dec[:], seg[:], AF.Exp), r=[("seg", 0), ("seg", 1)], w=["dec"])
                        yield
                        for g in range(2):
                            st_b3, k_b3 = nxs()
                            P.pe(MM(st_b3[:, 0:128], xbcT[g * 64:(g + 1) * 64, 4, t * 128:(t + 1) * 128],
                                    xbcT[g * 64:(g + 1) * 64, 5, t * 128:(t + 1) * 128]),
                                 r=[("qkT", 1, 4, t // 4, g), ("qkT", 1, 5, t // 4, g)], w=[k_b3])
                            P.dve(TT(cbm[:, g, :], st_b3[:, 0:128], tri[d][:], ALU.mult),
                                  r=[k_b3, "triU", "triL"], w=[("cbm", g)])
                        for g in range(2):
                            P.dve(TT(MTb[:, g * 4:(g + 1) * 4, :], dec[:, g * 4:(g + 1) * 4, :],
                                     bc(cbm[:, g, :], 1, [128, 4, 128]), ALU.mult), r=["dec", ("cbm", g)], w=[("MT", g)])
                        yield
                        P.dve(TT(xdt[:].rearrange("p (h q) -> p h q", h=8), xtok[:, t, :].rearrange("p (h q) -> p h q", h=8),
                                 bc(dtb[:, t, d * 8:(d + 1) * 8], 2, [128, 8, 64]), ALU.mult),
                              r=[("vaug", t), ("dt", t)], w=["xdt"])
                        P.dve(TT(wxdt[:].rearrange("p (h q) -> p h q", h=8), xdt[:].rearrange("p (h q) -> p h q", h=8),
                                 bc(acum[:, 24:32], 2, [128, 8, 64]), ALU.mult), r=["xdt", "wexp"], w=["wxdt"])
                        yield
                        ob_by, ko_by = nxo()
                        for hd in range(8):
                            P.pe(MM(ob_by[:, hd * 64:(hd + 1) * 64], MTb[:, hd, :], xdt[:, hd * 64:(hd + 1) * 64]),
                                 r=[("MT", hd // 4), "xdt"], w=[ko_by])
                        first_dir_chunk = c not in ysw
                        ysw.add(c)
                        if hstate[d]:
                            P.act(ACT(ysb, ob_by[:, :], AF.Copy), r=[ko_by], w=["ysb"])
                            for g in range(2):
                                st_bo, k_bo = nxs()
                                for r4 in range(4):
                                    P.pe(MM(st_bo[:, r4 * 64:(r4 + 1) * 64],
                                            xbcT[g * 64:(g + 1) * 64, 5, t * 128:(t + 1) * 128],
                                            hstb[g * 64:(g + 1) * 64, r4 * 64:(r4 + 1) * 64]),
                                         r=[("qkT", 1, 5, t // 4, g), "hstb"], w=[k_bo])
                                P.dve(TT(ytmp[:, g * 256:(g + 1) * 256].rearrange("p (h q) -> p h q", h=4),
                                         st_bo[:, 0:256].rearrange("p (h q) -> p h q", h=4),
                                         bc(acum[:, 16 + 4 * g:20 + 4 * g], 2, [128, 4, 64]), ALU.mult),
                                      r=[k_bo, "eacum"], w=[("ytmp", g)])
                            P.dve(TT(ysb, ysb, ytmp, ALU.add), r=["ysb", ("ytmp", 0), ("ytmp", 1)], w=["ysb"])
                            ysrc, ykey = ysb, "ysb"
                        else:
                            ysrc, ykey = ob_by[:, :], ko_by
                        if first_dir_chunk:
                            P.act(ACT(ysum[:, c % nys, :], ysrc, AF.Copy), r=[ykey], w=[("ysum", c % nys)])
                        else:
                            P.dve(TT(ysum[:, c % nys, :], ysum[:, c % nys, :], ysrc, ALU.add),
                                  r=[ykey, ("ysum", c % nys)], w=[("ysum", c % nys)])
                        yield
                        st_bs, k_bs = nxs()
                        for g in range(2):
                            P.pe(MM(st_bs[0:64, g * 256:(g + 1) * 256], btok[:, t, g * 64:(g + 1) * 64],
                                    wxdt[:, g * 256:(g + 1) * 256]), r=[("btok", t), "wxdt"], w=[k_bs])
                        for g in range(2):
                            hs = hst[g * 64:(g + 1) * 64, :]
                            if hstate[d]:
                                P.dve(TT(hs.rearrange("p (r q) -> p r q", r=4), hs.rearrange("p (r q) -> p r q", r=4),
                                         bc(sm[g * 64:(g + 1) * 64, cd0 + 4 * g:cd0 + 4 + 4 * g], 2, [64, 4, 64]), ALU.mult),
                                      r=["cdec", ("hst", g)], w=[("hst", g)])
                                P.dve(TT(hs, hs, st_bs[0:64, g * 256:(g + 1) * 256], ALU.add),
                                      r=[k_bs, ("hst", g)], w=[("hst", g)])
                            else:
                                P.act(ACT(hs, st_bs[0:64, g * 256:(g + 1) * 256], AF.Copy), r=[k_bs],
                                      w=[("hst", g)])
                        P.act(ACT(hstb[:], hst[:], AF.Copy), r=[("hst", 0), ("hst", 1)], w=["hstb"])
                        hstate[d] = True

                        yield

                    for s in range(nseq):
                        hs = {0: False, 1: False}
                        ysw = set()
                        for d in range(2):
                            acum_, Rb_, seg_, dec_, cbm_, MTb_, xdt_, wxdt_, hst_d, hstb_d, ysb_, ytmp_ = BUF[d]
                            P.keymap = km1 if d == 1 else None
                            if samp:
                                P.dma("sp", "ld_hin", hin[:], D["sst"][li, d].rearrange("(b p) n -> p b n", p=128), w=["hin"])
                                for blk in range(4):
                                    bt = nxt("st", 2)
                                    P.pe(TR(st[bt][0:64, 0:128], hin[:, blk, :], identf[:]), r=["hin", "identf"],
                                         w=["ps_st%d" % bt])
                                    g = blk // 2
                                    P.act(ACT(hst_d[g * 64:(g + 1) * 64, (blk % 2) * 128:(blk % 2 + 1) * 128], st[bt][0:64, 0:128],
                                              AF.Copy), r=["ps_st%d" % bt], w=[("hst", g)])
                                P.act(ACT(hstb_d[:], hst_d[:], AF.Copy), r=[("hst", 0), ("hst", 1)], w=["hstb"])
                                hs[d] = True
                            P.keymap = None
                        for ci in range(nch):
                            gens = [(d, chunk_step(s, d, ci if d == 0 else nch - 1 - ci, hs, ysw)) for d in range(2)]
                            while gens:
                                for dg in list(gens):
                                    P.keymap = km1 if dg[0] == 1 else None
                                    try:
                                        next(dg[1])
                                    except StopIteration:
                                        gens.remove(dg)
                                    P.keymap = None
                        for d in range(2):
                            hst_o = BUF[d][8]
                            P.keymap = km1 if d == 1 else None
                            if not samp:
                                for blk in range(2):
                                    bt = nxt("st", 2)
                                    P.pe(TR(st[bt][:, 0:128], hst_o[:, blk * 128:(blk + 1) * 128], identf[:]),
                                         r=[("hst", 0), ("hst", 1), "identf"], w=["ps_st%d" % bt])
                                    hi = nxt("hout", 2)
                                    P.act(ACT(houts[hi][:], st[bt][:, 0:128], AF.Copy), r=["ps_st%d" % bt], w=["hout%d" % hi])
                                    for g in range(2):
                                        r0 = (4 * g + 2 * blk) * 64
                                        P.dma("sp", "o_ssd%d" % hi, O["nssd"][s, li, d, r0:r0 + 128, :],
                                              houts[hi][:, g * 64:(g + 1) * 64], r=["hout%d" % hi])
                            P.keymap = None
                        if own:
                            for c in range(nch):
                                P.dve(TT(ysb.rearrange("p (h q) -> p h q", h=8), xtok[:, c, :].rearrange("p (h q) -> p h q", h=8),
                                         bc(dsum[:, 0:8], 2, [128, 8, 64]), ALU.mult), r=[("vaug", c), "dsum", "ysb"], w=["ysb"])
                                P.dve(TT(ysum[:, c, :], ysum[:, c, :], ysb, ALU.add), r=["ysb", ("ysum", c)], w=[("ysum", c)])
                            for j in range(2):
                                P.dve(TS(ysb, ysum[:, 0, :], selO[:, j * 8:j * 8 + 1], ALU.mult), r=[("ysum", 0), "selO", "ysb"], w=["ysb"])
                                for c in range(1, nch):
                                    P.dve(STT(ysb, ysum[:, c, :], selO[:, j * 8 + c:j * 8 + c + 1], ysb, ALU.mult, ALU.add),
                                          r=[("ysum", c), "selO", "ysb"], w=["ysb"])
                                P.dve(TT(ysb, ysb, gbuf[:, j, :], ALU.mult), r=["ysb", ("gbuf", j)], w=["ysb"])
                                P.act(ACT(ytmp, ysb, AF.Square, accum_out=sm[:, 50:51]), r=["ysb"],
                                      w=["ytmp", ("ytmp", 0), ("ytmp", 1), "sm_ys"])
                                rstd_from_ss(sm[:, 50:51], sm[:, 51:52], 512, "sm_ys", "sm_yr")
                                P.dve(STT(Ytok[:, j, :], ysb, sm[:, 51:52], ssdg[:], ALU.mult, ALU.mult),
                                      r=["ysb", "sm_yr", "ssdg"], w=[("Ytok", j)])
                                ytok_to_YT(j, j)
                        for c in range(nch if not own else 0):
                            t = s * nch + c
                            oi = 0
                            P.dve(TT(ysb.rearrange("p (h q) -> p h q", h=8), xtok[:, t, :].rearrange("p (h q) -> p h q", h=8),
                                     bc(dsum[:, 0:8], 2, [128, 8, 64]), ALU.mult), r=[("vaug", t), "dsum", "ysb"], w=["ysb"])
                            P.dve(TT(ysb, ysb, ysum[:, c % nys, :], ALU.add), r=["ysb", ("ysum", c % nys)], w=["ysb"])
                            P.dve(TT(ysb, ysb, gbuf[:, t, :], ALU.mult), r=["ysb", ("gbuf", t)], w=["ysb"])
                            P.act(ACT(ytmp, ysb, AF.Square, accum_out=sm[:, 50:51]), r=["ysb"], w=["ytmp", ("ytmp", 0), ("ytmp", 1), "sm_ys"])
                            rstd_from_ss(sm[:, 50:51], sm[:, 51:52], 512, "sm_ys", "sm_yr")
                            P.dve(STT(Ytok[:, c % 2, :], ysb, sm[:, 51:52], ssdg[:], ALU.mult, ALU.mult),
                                  r=["ysb", "sm_yr", "ssdg"], w=[("Ytok", c % 2)])
                            ytok_to_YT(t, c % 2)
                    branch_merge(D["w_br_b"][li], MRG + 1024, False)

                if "A" in stages:
                    attn_mixer("A")
                if "B" in stages:
                    ssd_mixer()
                if "C" in stages:
                    attn_mixer("C")
                if "P" not in stages:
                    continue

                P.tag = "%d.%d.post" % (kind, li)
                for t in range(NTq):
                    xi = nxt("xt", 2)
                    P.act(ACT(hbf[xi][:], acc[:, t, :], AF.Copy), r=[("acc", t, 0), ("acc", t, 1)], w=["hbf0"])
                    for k in range(8):
                        P.pe(TR(tpb[:, k, :], hbf[xi][:, k * 128:(k + 1) * 128], identb[:]),
                             r=["hbf0", "identb"], w=["ps_tpb"])
                    P.act(ACT(hTq[:, :, t * 128:(t + 1) * 128], tpb[:, :, :], AF.Copy), r=["ps_tpb"], w=[(hqk, t)])
                xo = big[:, :].rearrange("p (j f) -> p j f", j=4)
                for t in range(NTq):
                    P.dma("sp" if t % 2 == 0 else "act", "ld_xa%d" % t, acc[:, t, :], xsrc_q[t * 128:(t + 1) * 128, :],
                          r=[(xkq, t)], w=[("acc", t, 0), ("acc", t, 1)])
                for t in range(NTq):
                    xi = t % 2
                    xb_ = acc[:, t, :]
                    XK = [("acc", t, 0), ("acc", t, 1)]
                    if t == 0:
                        wts = [wload(D["w_out"][li][:, cb * 512:(cb + 1) * 512], 8, 512) for cb in range(2)]
                    for cb in range(2):
                        wt, wk = wts[cb]
                        pB, kB_ = ALLB[nxt("allb", 7)]
                        for k in range(8):
                            P.pe(MM(pB[:, :], hTq[:, k, t * 128:(t + 1) * 128], wt[:, k, :], k == 0, k == 7),
                                 r=[(hqk, t), wk], w=[kB_])
                        si = nxt("sgm", 2)
                        P.dve(TT(sgm[si], pB[:, :], gate[:, cb * 512:(cb + 1) * 512], ALU.mult),
                              r=[kB_, "ada"], w=["sgm%d" % si])
                        P.dve(TT(xb_[:, cb * 512:(cb + 1) * 512], xb_[:, cb * 512:(cb + 1) * 512], sgm[si], ALU.add),
                              r=["sgm%d" % si, ("acc", t, cb)], w=[("acc", t, cb)])
                    if li < n_layers - 1 and gath:
                        P.dma("sp", "st_x%d" % xi, xscr_own[t * 128:(t + 1) * 128, :], xb_, r=XK,
                              w=[("xscr_own", t)])
                        if t == NTq - 1:
                            P.op("pool", lambda e: e.collective_compute(
                                "AllGather", ALU.bypass, replica_groups=[[0, 1, 2, 3], [4, 5, 6, 7]],
                                ins=[xscr_own], outs=[xgath]),
                                reads=[("xscr_own", 0), ("xscr_own", 1)], writes=[("xgath", t_) for t_ in range(NT)],
                                dma="cc_x", inc=1)
                    elif li < n_layers - 1:
                        P.dma("sp", "st_x%d" % xi, xscr[t * 128:(t + 1) * 128, :], xb_, r=XK,
                              w=[("xscr", t)])
                        if next_own:
                            for j in range(2):
                                sc = selO[:, j * 8 + t:j * 8 + t + 1]
                                if t == 0:
                                    P.dve(TS(xo[:, j, :], xb_, sc, ALU.mult), r=XK + ["selO"], w=[("xo", j)])
                                else:
                                    P.dve(STT(xo[:, j, :], xb_, sc, xo[:, j, :], ALU.mult, ALU.add),
                                          r=XK + ["selO", ("xo", j)], w=[("xo", j)])
                            if t == NT - 1:
                                for j in range(2):
                                    P.dma("sp", "st_xo", xscr_own[j * 128:(j + 1) * 128, :], xo[:, j, :], r=[("xo", j)],
                                          w=[("xscr_own", j)])
                    else:
                        if t == 0:
                            P.dma("sp", "ld_gs", gs, D["final_g"].partition_broadcast(128), r=["ada"], w=["gs"])
                        c0_ = 16 + 2 * xi
                        P.act(ACT(xt[xi][:], xb_, AF.Square, accum_out=sm[:, c0_:c0_ + 1]),
                              r=XK, w=["xt%d" % xi, "sm_fs%d" % xi])
                        rstd_from_ss(sm[:, c0_:c0_ + 1], sm[:, c0_ + 1:c0_ + 2], 1024, "sm_fs%d" % xi, "sm_fr%d" % xi)
                        P.dve(STT(xt[xi][:], xb_, sm[:, c0_ + 1:c0_ + 2], gs, ALU.mult, ALU.mult),
                              r=XK + ["sm_fr%d" % xi, "gs", "ada"], w=["xt%d" % xi])
                        P.dma("sp" if xi == 0 else "act", "o_y%d" % xi, yout[t * 128:(t + 1) * 128, :], xt[xi][:], r=["xt%d" % xi])

        compute_ada(0)
        if not do_prompt:
            for l_ in range(1, n_layers):
                compute_ada(l_)
        if do_prompt and do_sample and own_all and n_layers == 2:
            run_pass(1, [0])
            run_pass(0)
            run_pass(1, [1])
        else:
            if do_prompt:
                run_pass(0)
            if do_sample:
                run_pass(1)
        fk = [k for k in P.dma_count if k.startswith("o_") or k.startswith("st_x")]
        print("n_ops", len(P.ops), "sbuf_free", nc.sbuf_bytes_remaining, flush=True)
        import os, json
        if os.environ.get("KTAGS"):
            json.dump({e: [o["tag"] for o in P.ops if o["eng"] == e for _ in range(o.get("ni", 1))] for e in ENGS}, open(os.environ["KTAGS"], "w"))
        P.emit(final_keys=fk)
    return nc


_NC = {}


def _consts():
    ident = np.eye(128, dtype=np.float32)
    tt = np.arange(128)
    triU = (tt[:, None] <= tt[None, :]).astype(np.float32)
    triL = (tt[:, None] >= tt[None, :]).astype(np.float32)
    pos = np.arange(1024)
    inv = (10000.0 ** (-np.arange(16, dtype=np.float32) / 16)).astype(np.float32)
    C = np.zeros((64, 1024), np.float32)
    S = np.zeros((64, 1024), np.float32)
    RT = np.zeros((64, 64), np.float32)
    for d in range(64):
        half = d // 32
        qd = (d % 32) % 16
        p = (pos // 64) if half == 0 else (pos % 64)
        ang = p.astype(np.float32) * inv[qd]
        C[d] = np.cos(ang)
        S[d] = np.sin(ang)
        if (d % 32) < 16:
            RT[d + 16, d] = -1.0
        else:
            RT[d - 16, d] = 1.0
    col = np.arange(64)
    cs = np.clip(col - 8, 0, 48)
    inwin = (col[None, :] >= cs[:, None]) & (col[None, :] < cs[:, None] + 16)
    m = np.where(inwin.T, 0.0, NEG).astype(np.float32)
    namask = np.concatenate([m, m], axis=0)
    RT2 = np.zeros((128, 128), np.float32)
    RT2[:64, :64] = RT
    RT2[64:, 64:] = RT
    return dict(ident=ident, triU=triU, triL=triL, ropeC=np.concatenate([C, C], 0), ropeS=np.concatenate([S, S], 0),
                ropeRT=RT2, namask=namask)


def kernel(**inp):
    f = lambda a: np.ascontiguousarray(np.asarray(a, dtype=np.float32))
    n_cores = 8
    cst = _consts()
    col = np.arange(64)
    dc = np.clip(col[:, None] - col[None, :], -15, 15) + 15
    rpb = f(inp["na_rpb"])
    rpbT = np.ascontiguousarray(rpb[:, :, ::-1, :][:, :, :, dc])
    shared = dict(
        norm_g=f(inp["norm_g"]), w_ada=f(inp["w_ada"]), b_ada=f(inp["b_ada"]), w_in=f(inp["w_in"]),
        lamv=f(np.stack([inp["lam_q1"], inp["lam_q2"], inp["lam_k1"], inp["lam_k2"]], axis=1)),
        subln_g=f(inp["diff_subln_g"]),
        convwT=f(np.asarray(inp["conv_w"]).reshape(2, 5, 6, 128).transpose(0, 3, 2, 1)),
        convbT=f(np.asarray(inp["conv_b"]).reshape(2, 6, 128).transpose(0, 2, 1)),
        dt_bias=f(np.asarray(inp["dt_bias"]).reshape(2, 16)), a_log=f(np.asarray(inp["a_log"]).reshape(2, 16)),
        d_skip=f(np.asarray(inp["d_skip"]).reshape(2, 16)), ssd_norm_g=f(inp["ssd_norm_g"]), rpbT=rpbT,
        w_br_a=f(inp["w_br_a"]), w_br_b=f(inp["w_br_b"]), w_br_c=f(inp["w_br_c"]), w_out=f(inp["w_out"]),
        final_g=f(inp["final_g"]), **cst)
    xp = f(inp["x_prompt"])
    xs = f(inp["x_sample"])
    own_tabs = []
    kc_i = np.arange(64)
    dcc = np.clip(kc_i[:, None] - kc_i[None, :], -15, 15) + 15
    cs = np.clip(kc_i - 8, 0, 48)
    inwin_T = ((kc_i[:, None] >= cs[None, :]) & (kc_i[:, None] < cs[None, :] + 16))
    for qb in range(4):
        selO = np.zeros((128, 16), np.float32)
        for j in range(2):
            selO[:, j * 8 + 2 * qb + j] = 1.0
        rpbO = np.zeros((2, 8, 128, 32, 64), np.float32)
        maskO = np.full((128, 32, 64), NEG, np.float32)
        for a in range(8):
            for j in range(4):
                r_ = 4 * qb + j
                rs_ = min(max(r_ - 4, 0), 8)
                for half in range(2):
                    kr = 2 * a + half
                    if rs_ <= kr <= rs_ + 7:
                        dd = kr - r_ + 7
                        rpbO[:, :, half * 64:(half + 1) * 64, a * 4 + j, :] = rpb[:, :, dd, :][:, :, dcc]
                        maskO[half * 64:(half + 1) * 64, a * 4 + j, :] = np.where(inwin_T, 0.0, NEG)
        own_tabs.append(dict(selO=selO, rpbO=rpbO, maskO=maskO,
                             ropeCo=np.ascontiguousarray(cst["ropeC"][:, qb * 256:(qb + 1) * 256]),
                             ropeSo=np.ascontiguousarray(cst["ropeS"][:, qb * 256:(qb + 1) * 256])))
    in_maps = []
    for c in range(n_cores):
        b = c // 4
        cv = np.stack([f(inp["c_ctx"]), f(inp["c"])[b]], axis=0)
        d = dict(shared)
        d.update(
            xp=np.ascontiguousarray(xp[4 * c:4 * c + 4].reshape(T, 1024)),
            xs=np.ascontiguousarray(xs[b]),
            xso=np.ascontiguousarray(xs[b, (c % 4) * 256:(c % 4 + 1) * 256]),
            cdk=f(inp["cache_diff_k"])[b].reshape(2, 512, 512), cdv=f(inp["cache_diff_v"])[b].reshape(2, 512, 512),
            cnk=f(inp["cache_na_k"])[b].reshape(2, 512, 512), cnv=f(inp["cache_na_v"])[b].reshape(2, 512, 512),
            sst=f(inp["state_ssd"])[b].reshape(2, 2, 512, 64),
            cvecT=np.ascontiguousarray(cv.reshape(2, 8, 128).transpose(0, 2, 1)),
            **own_tabs[c % 4],
        )
        in_maps.append({k: np.ascontiguousarray(v) for k, v in d.items()})
    if "nc" not in _NC:
        _NC["nc"] = build()
    import os
    ncr = int(os.environ.get("KCORES", "8"))
    res = run_bass_kernel_spmd(_NC["nc"], in_maps[:ncr], core_ids=list(range(ncr)))
    R = list(res.results) + [res.results[0]] * (n_cores - ncr)
    y_prompt = np.concatenate([R[c]["yp"].reshape(4, 256, 1024) for c in range(n_cores)], axis=0)
    if R[0]["ys"].shape[0] == T:
        y_sample = np.stack([R[0]["ys"], R[4]["ys"]], axis=0)
    else:
        y_sample = np.stack([np.concatenate([R[4 * b + q]["ys"] for q in range(4)], axis=0) for b in range(2)], axis=0)
    cat = lambda k, shp: np.concatenate([R[c][k].reshape((4,) + shp) for c in range(n_cores)], axis=0)
    return (y_prompt, y_sample,
            cat("ndk", (2, 256, 4, 128)), cat("ndv", (2, 256, 4, 128)),
            cat("nnk", (2, 256, 8, 64)), cat("nnv", (2, 256, 8, 64)),
            cat("nssd", (2, 2, 8, 64, 64)))
```

```python
import contextlib
import math
import numpy as np
import concourse.bass as bass
import concourse.mybir as mybir
from concourse.bass_utils import run_bass_kernel_spmd

F32 = mybir.dt.float32
BF16 = mybir.dt.bfloat16
AF = mybir.ActivationFunctionType
ALU = mybir.AluOpType
AX = mybir.AxisListType

ENGS = ("pe", "act", "dve", "pool", "sp")
EPS = 1e-6
NEG = -30000.0


class Prog:
    def __init__(self, nc):
        self.nc = nc
        self.ops = []
        self.last_w = {}
        self.readers = {}
        self.dma_last = {}
        self.dma_count = {}
        self.stack = contextlib.ExitStack()

    def sb(self, name, shape, dt):
        return self.stack.enter_context(self.nc.sbuf_tensor("sb_" + name, list(shape), dt))

    def ps(self, name, shape, dt=F32):
        return self.stack.enter_context(self.nc.psum_tensor("ps_" + name, list(shape), dt))

    limit = None
    tag = ""

    keymap = None

    def op(self, eng, fn, reads=(), writes=(), dma=None, inc=16):
        oid = len(self.ops)
        if self.limit is not None and oid >= self.limit:
            return None
        if self.keymap is not None:
            reads = [self.keymap(k) for k in reads]
            writes = [self.keymap(k) for k in writes]
        deps = set()
        for k in reads:
            if k in self.last_w:
                deps.add(self.last_w[k])
            if isinstance(k, str) and k.startswith("ps_"):
                for r in self.readers.get(k, ()):
                    if self.ops[r]["eng"] != eng:
                        deps.add(r)
        for k in writes:
            if k in self.last_w:
                deps.add(self.last_w[k])
            last = {}
            for r in self.readers.get(k, ()):
                ro = self.ops[r]
                if ro["dma"] is not None:
                    deps.add(r)
                else:
                    last[ro["eng"]] = r
            deps.update(last.values())
        if dma is not None and dma in self.dma_last:
            deps.add(self.dma_last[dma])
        deps.discard(oid)
        if eng == "pe":
            deps = {d for d in deps if self.ops[d]["eng"] != "pe"}
        o = dict(id=oid, eng=eng, fn=fn, deps=deps, dma=dma, marked=False, mark=None, tag=self.tag)
        if dma is not None:
            self.dma_count[dma] = self.dma_count.get(dma, 0) + inc
            o["dma_val"] = self.dma_count[dma]
            o["inc"] = inc
            self.dma_last[dma] = oid
        self.ops.append(o)
        for k in reads:
            self.readers.setdefault(k, []).append(oid)
        for k in writes:
            self.last_w[k] = oid
            self.readers[k] = []
        return oid

    def pe(self, fn, r=(), w=(), ni=1):
        oid = self.op("pe", fn, r, w)
        if oid is not None:
            self.ops[oid]["ni"] = ni
        return oid

    def act(self, fn, r=(), w=()):
        return self.op("act", fn, r, w)

    def dve(self, fn, r=(), w=()):
        return self.op("dve", fn, r, w)

    def pool(self, fn, r=(), w=()):
        return self.op("pool", fn, r, w)

    def dma(self, q, key, out, in_, r=(), w=()):
        return self.op(q, lambda e: e.dma_start(out=out, in_=in_), r, w, dma=key)

    def emit(self, final_keys=()):
        nc = self.nc
        ops = self.ops
        for o in ops:
            for d in o["deps"]:
                p = ops[d]
                if p["dma"] is None:
                    p["marked"] = True
        cnt = {e: 0 for e in ENGS}
        for o in ops:
            if o["dma"] is None and o["marked"]:
                cnt[o["eng"]] += 1
                o["mark"] = cnt[o["eng"]]
        esem = {e: self.stack.enter_context(nc.semaphore("s_" + e)) for e in ENGS if e != "sp"}
        dsem = {k: self.stack.enter_context(nc.semaphore("d_%d" % i))
                for i, k in enumerate(self.dma_count)}
        per_eng = {e: [o for o in ops if o["eng"] == e] for e in ENGS}
        engobj = {"pe": "tensor", "act": "scalar", "dve": "vector", "pool": "gpsimd", "sp": "sync"}

        def run(e, eng):
            waited = {}
            for o in per_eng[e]:
                need = {}
                for d in o["deps"]:
                    p = ops[d]
                    if p["dma"] is not None:
                        sk, v = ("d", p["dma"]), p["dma_val"]
                    else:
                        sk, v = ("e", p["eng"]), p["mark"]
                    if need.get(sk, 0) < v:
                        need[sk] = v
                for sk, v in need.items():
                    if waited.get(sk, 0) >= v:
                        continue
                    sem = dsem[sk[1]] if sk[0] == "d" else esem[sk[1]]
                    eng.wait_ge(sem, v)
                    waited[sk] = v
                ins = o["fn"](eng)
                if o["dma"] is not None:
                    ins.then_inc(dsem[o["dma"]], o["inc"])
                elif o["marked"]:
                    ins.then_inc(esem[e], 1)
            if e == "sp":
                for k in final_keys:
                    eng.wait_ge(dsem[k], self.dma_count[k])

        with nc.Block() as block:
            for e in ENGS:
                getattr(block, engobj[e])(lambda eng, e=e: run(e, eng))


def MM(out, lhsT, rhs, start=True, stop=True):
    return lambda e: e.matmul(out, lhsT=lhsT, rhs=rhs, start=start, stop=stop)


def TR(out, in_, ident):
    return lambda e: e.transpose(out, in_, ident)


def ACT(out, in_, func, **kw):
    return lambda e: e.activation(out=out, in_=in_, func=func, **kw)


def TT(out, in0, in1, op):
    return lambda e: e.tensor_tensor(out=out, in0=in0, in1=in1, op=op)


def TS(out, in0, s1, op0, s2=None, op1=None):
    if op1 is None:
        return lambda e: e.tensor_scalar(out=out, in0=in0, scalar1=s1, scalar2=None, op0=op0)
    return lambda e: e.tensor_scalar(out=out, in0=in0, scalar1=s1, scalar2=s2, op0=op0, op1=op1)


def STT(out, in0, scalar, in1, op0, op1):
    return lambda e: e.scalar_tensor_tensor(out=out, in0=in0, scalar=scalar, in1=in1, op0=op0, op1=op1)


def CP(out, in_):
    return lambda e: e.tensor_copy(out=out, in_=in_)


def bc(ap, axis, shape):
    return ap.unsqueeze(axis).to_broadcast(list(shape))


QA, KA, VA, GA = 0, 512, 1024, 1536
ZB, XB, DTB = 2048, 2560, 3328
QC, KC, VC, GC = 3344, 3856, 4368, 4880
MRG = 5392
IN_COLS = 8464
T = 1024
NT = 8


def build(n_layers=2, do_prompt=True, do_sample=True, stages="ABCP", limit=None, own_last=True, own_all=True):
    nc = bass.Bass("TRN2", target_bir_lowering=False)

    def din(name, shape):
        return nc.dram_tensor(name, list(shape), F32, kind="ExternalInput").ap()

    def dout(name, shape):
        return nc.dram_tensor(name, list(shape), F32, kind="ExternalOutput").ap()

    D = {}
    for name, shape in [
        ("xp", (T, 1024)), ("xs", (T, 1024)), ("xso", (256, 1024)),
        ("cdk", (2, 512, 512)), ("cdv", (2, 512, 512)), ("cnk", (2, 512, 512)), ("cnv", (2, 512, 512)),
        ("sst", (2, 2, 512, 64)),
        ("cvecT", (2, 128, 8)), ("norm_g", (2, 1024)), ("w_ada", (2, 1024, 3072)), ("b_ada", (2, 3072)),
        ("w_in", (2, 1024, IN_COLS)), ("lamv", (2, 4, 64)), ("subln_g", (2, 128)),
        ("convwT", (2, 128, 6, 5)), ("convbT", (2, 128, 6)), ("dt_bias", (2, 16)), ("a_log", (2, 16)),
        ("d_skip", (2, 16)), ("ssd_norm_g", (2, 512)), ("rpbT", (2, 8, 15, 64, 64)),
        ("w_br_a", (2, 512, 1024)), ("w_br_b", (2, 512, 1024)), ("w_br_c", (2, 512, 1024)),
        ("w_out", (2, 1024, 1024)), ("final_g", (1024,)),
        ("ident", (128, 128)), ("triU", (128, 128)), ("triL", (128, 128)),
        ("ropeC", (128, 1024)), ("ropeS", (128, 1024)), ("ropeRT", (128, 128)), ("namask", (128, 64)),
        ("selO", (128, 16)), ("ropeCo", (128, 256)), ("ropeSo", (128, 256)),
        ("rpbO", (2, 8, 128, 32, 64)), ("maskO", (128, 32, 64)),
    ]:
        D[name] = din(name, shape)
    O = {}
    for name, shape in [
        ("yp", (T, 1024)), ("ys", (256 if (own_all or (own_last and n_layers > 1)) else T, 1024)),
        ("ndk", (4, 2, 256, 512)), ("ndv", (4, 2, 256, 512)), ("nnk", (4, 2, 256, 512)), ("nnv", (4, 2, 256, 512)),
        ("nssd", (4, 2, 2, 512, 64)),
    ]:
        O[name] = dout(name, shape)
    xscr = nc.dram_tensor("xscr", [T, 1024], F32, kind="Internal").ap()
    ada_scr = nc.dram_tensor("ada_scr", [2, 2, 128, 3072], F32, kind="Internal").ap()
    xscr_own = nc.dram_tensor("xscr_own", [256, 1024], F32, kind="Internal").ap()
    xgath = nc.dram_tensor("xgath", [T, 1024], F32, kind="Internal").ap()

    P = Prog(nc)
    P.limit = limit
    with P.stack:
        hT = P.sb("hT", [128, 8, T], BF16)
        acc = P.sb("acc", [128, NT, 1024], F32)
        YT = P.sb("YT", [128, 4, T], BF16)
        big = P.sb("big", [128, 4096], F32)
        qT = big[:, 0:2048].bitcast(BF16).rearrange("p (j t) -> p j t", j=4)
        PTb = big[:, 2048:3584].bitcast(BF16).rearrange("p (k q) -> p k q", k=12)
        ysum = big[:, :].rearrange("p (c f) -> p c f", c=8)
        kT = P.sb("kT", [128, 8, T], BF16)
        vaug = P.sb("vaug", [128, NT, 528], BF16)
        gbuf = P.sb("gbuf", [128, NT, 512], BF16)
        wb = [P.sb("wb%d" % i, [128, 8, 512], BF16) for i in range(2)]
        ada = P.sb("ada", [128, 3072], F32)
        xt = [P.sb("xt%d" % i, [128, 1024], F32) for i in range(2)]
        htmp = P.sb("htmp", [128, 1024], F32)
        hbf = [P.sb("hbf%d" % i, [128, 1024], BF16) for i in range(1)] * 2
        stgT = P.sb("stg", [128, 2, 512], F32)
        stg = [stgT[:, i, :] for i in range(2)]
        natab_full = stgT[:, :, :].rearrange("p a b -> p (a b)").bitcast(BF16).rearrange("p (s q) -> p s q", s=32)
        sgmT = P.sb("sgm", [128, 2, 512], F32)
        sgm = [sgmT[:, i, :] for i in range(2)]
        natab_band = sgmT[:, :, :].rearrange("p a b -> p (a b)").bitcast(BF16).rearrange("p (s q) -> p s q", s=32)
        identb = P.sb("identb", [128, 128], BF16)
        identf = P.sb("identf", [128, 128], F32)
        tri = [P.sb("triU", [128, 128], F32), P.sb("triL", [128, 128], F32)]
        onesb = P.sb("onesb", [128, 128], BF16)
        onesf = P.sb("onesf", [128, 128], F32)
        sm = P.sb("sm", [128, 104], F32)
        cbcs = [P.sb("cbc%d" % i, [128, 8, 128], BF16) for i in range(2)]
        sqb = [P.sb("sqb%d" % i, [128, 512], BF16) for i in range(2)]
        Ytok = P.sb("Ytok", [128, 2, 512], BF16)
        otmp = [P.sb("otmp%d" % i, [128, 128], F32) for i in range(4)]
        nrm2 = P.sb("nrm2", [128, 2, 8, 4], F32)
        negm = P.sb("negm", [128, 8, 4], F32)
        sgv = P.sb("sgv", [128, 128], F32)
        lamt = P.sb("lamt", [128, 4, 64], F32)
        kTc = P.sb("kTc", [128, 4, 512], BF16)
        vaugc = P.sb("vaugc", [128, 4, 528], BF16)
        nrm2c = P.sb("nrm2c", [128, 8], F32)
        nrm2s = P.sb("nrm2s", [128, 2, 8], F32)
        ropeRT = P.sb("ropeRT", [128, 128], F32)
        namask = P.sb("namask", [128, 64], F32)
        hin = P.sb("hin", [128, 4, 64], F32)
        hTo = P.sb("hTo", [128, 8, 256], BF16)
        selO = P.sb("selO", [128, 16], F32)
        convw = P.sb("convw", [128, 6, 5], F32)
        convb = P.sb("convb", [128, 6], F32)
        dtb = P.sb("dtb", [128, NT, 16], F32)
        lab = P.sb("lab", [128, NT, 16], F32)
        dtbias = P.sb("dtbias", [128, 16], F32)
        nega = P.sb("nega", [128, 16], F32)
        dsum = P.sb("dsum", [128, 16], F32)
        ssdg = P.sb("ssdg", [128, 512], F32)
        btok = P.sb("btok", [128, NT, 128], BF16)
        Rbraw = P.sb("Rb", [128, 1040], F32)
        Rb = Rbraw[:, 0:1024].rearrange("p (h i) -> p h i", h=8)
        xpre = Rbraw[:, :].rearrange("p (s q) -> p s q", s=4)
        seg = P.sb("seg", [128, 8, 128], F32)
        cacc = seg[:, :, :].rearrange("p a b -> p (a b)").rearrange("p (s q) -> p s q", s=4)
        dec = P.sb("dec", [128, 8, 128], BF16)
        MTb = P.sb("MTb", [128, 8, 128], BF16)
        cbm = P.sb("cbm", [128, 2, 128], BF16)
        xdt = P.sb("xdt", [128, 512], BF16)
        wxdt = P.sb("wxdt", [128, 512], BF16)
        acum = P.sb("acum", [128, 32], F32)
        hst = P.sb("hst", [128, 256], F32)
        hst2 = P.sb("hst2", [128, 256], F32)
        hstb2 = P.sb("hstb2", [128, 256], BF16)
        acum2 = P.sb("acum2", [128, 32], F32)
        cbm2 = P.sb("cbm2", [128, 2, 128], BF16)
        hstb = P.sb("hstb", [128, 256], BF16)
        ysb = htmp[:, 0:512]
        ytmp = htmp[:, 512:1024]
        houts = [P.sb("hout%d" % i, [128, 128], F32) for i in range(2)]
        pj = [P.ps("ps_pj%d" % i, [128, 512]) for i in range(3)]
        st = [P.ps("ps_st%d" % i, [128, 512]) for i in range(2)]
        ob = [P.ps("ps_ob%d" % i, [128, 512]) for i in range(2)]
        tpb = P.ps("ps_tpb", [128, 8, 128], BF16)
        ALLB = [(pj[i], "ps_pj%d" % i) for i in range(3)] + [(st[i], "ps_st%d" % i) for i in range(2)] + \
               [(ob[i], "ps_ob%d" % i) for i in range(2)]
        cnt = {"allb": 0, "pj": 0, "st": 0, "ob": 0, "w": 0, "stg": 0, "sgm": 0, "xt": 0, "sqb": 0, "otmp": 0, "hout": 0}

        def nxt(name, n):
            i = cnt[name] % n
            cnt[name] += 1
            return i

        P.dma("pool", "c_id", identb[:], D["ident"], w=["identb"])
        P.dma("sp", "c_misc", identf[:], D["ident"], w=["identf"])
        P.dma("sp", "c_misc", tri[0][:], D["triU"], w=["triU"])
        P.dma("sp", "c_misc", tri[1][:], D["triL"], w=["triL"])
        P.dve(lambda e: e.memset(onesb[:], 1.0), w=["onesb"])
        P.dve(lambda e: e.memset(onesf[:], 1.0), w=["onesf"])
        P.dma("sp", "c_misc", ropeRT[:], D["ropeRT"], w=["ropeRT"])
        P.dma("sp", "c_misc", namask[:], D["namask"], w=["namask"])
        P.dma("sp", "c_misc", selO[:], D["selO"], w=["selO"])

        def wload(view, kc, ncols):
            s = nxt("w", 2)
            P.dma("pool", "wq%d" % s, wb[s][:, 0:kc, 0:ncols], view.rearrange("(k p) c -> p k c", p=128),
                  w=["wb%d" % s])
            return wb[s], "wb%d" % s

        def rstd_from_ss(ss_ap, out_ap, n, rk, wk):
            P.act(ACT(out_ap, ss_ap, AF.Ln, scale=1.0 / n, bias=EPS), r=[rk], w=[wk + "_l"])
            P.act(ACT(out_ap, out_ap, AF.Exp, scale=-0.5), r=[wk + "_l"], w=[wk])

        def compute_ada(li):
            P.tag = "ada%d" % li
            for kd in range(2):
                c0 = 64 + kd * 16
                P.dma("sp", "ld_sm", sm[:, c0:c0 + 8], D["cvecT"][kd], w=[("sm_c", kd)])
                P.act(ACT(sm[:, c0 + 8:c0 + 16], sm[:, c0:c0 + 8], AF.Silu), r=[("sm_c", kd)], w=[("sm_sc", kd)])
                P.dve(CP(cbcs[kd][:], bc(sm[:, c0 + 8:c0 + 16], 2, [128, 8, 128])), r=[("sm_sc", kd)], w=["cbc%d" % kd])
            for cb in range(6):
                wt, wk = wload(D["w_ada"][li][:, cb * 512:(cb + 1) * 512], 8, 512)
                for kd in range(2):
                    P.dma("sp", "ld_ba%d" % kd, sgm[kd], D["b_ada"][li][cb * 512:(cb + 1) * 512].partition_broadcast(128),
                          w=["sgm%d" % kd])
                    b = nxt("pj", 3)
                    for k in range(8):
                        P.pe(MM(pj[b][:, :], cbcs[kd][:, k, :], wt[:, k, :], k == 0, k == 7),
                             r=["cbc%d" % kd, wk], w=["ps_pj%d" % b])
                    P.dve(TT(sgm[kd], sgm[kd], pj[b][:, :], ALU.add), r=["ps_pj%d" % b, "sgm%d" % kd], w=["sgm%d" % kd])
                    P.dma("sp", "st_ada%d" % kd, ada_scr[kd, li][:, cb * 512:(cb + 1) * 512], sgm[kd],
                          r=["sgm%d" % kd], w=[("ada_scr", kd, li)])

        def run_pass(kind, layers=None):
            samp = kind == 1
            xin = D["xs"] if samp else D["xp"]
            yout = O["ys"] if samp else O["yp"]

            for li in (range(n_layers) if layers is None else layers):
                xsrc = xin if li == 0 else xscr
                Win = D["w_in"][li]
                gath = samp and own_all
                if gath and li > 0:
                    xsrc = xgath
                own = gath or (samp and own_last and li == n_layers - 1 and n_layers > 1)
                next_own = samp and own_last and li == n_layers - 2 and not gath
                hTq, NTq, hqk = (hTo, 2, "hTo") if own else (hT, NT, "hT")
                xsrc_q = (D["xso"] if (gath and li == 0) else xscr_own) if own else xsrc
                xkq = "xscr_own" if own else ("xgath" if (gath and li > 0) else "xscr")
                xka = "xgath" if (gath and li > 0) else "xscr"
                P.tag = "%d.%d.pre" % (kind, li)
                P.dma("sp", "ld_ada", ada[:], ada_scr[kind, li], r=[("ada_scr", kind, li)], w=["ada"])
                P.dma("sp", "ld_gs", htmp[:], D["norm_g"][li].partition_broadcast(128),
                      w=["htmp", "ysb", "ytmp", ("ytmp", 0), ("ytmp", 1)])
                shift = ada[:, 0:1024]
                scale = ada[:, 1024:2048]
                gate = ada[:, 2048:3072]
                gs = scale
                P.dve(STT(gs, scale, 1.0, htmp[:], ALU.add, ALU.mult), r=["ada", "htmp"], w=["ada", "gs"])
                pre_tiles = [(xsrc, t, (xka, t), hT, "hT") for t in range(NT)]
                if own:
                    pre_tiles += [(xsrc_q, t, ("xscr_own", t), hTo, "hTo") for t in range(2)]
                HTK = ["htmp", "ysb", "ytmp", ("ytmp", 0), ("ytmp", 1)]

                def pre_tile(xs_, t, xkey, hdst, hk, par):
                    xtb, xk = xt[par], "xt%d" % par
                    if par == 0:
                        ht_, htk, hb_, hbk, c0, tp_, tpk = htmp[:], HTK, hbf[0][:], ["hbf0"], 16, tpb, "ps_tpb"
                    else:
                        ht_ = sgmT[:, :, :].rearrange("p a b -> p (a b)")
                        htk = ["sgm0", "sgm1"]
                        hb_, hbk, c0 = stgT[:, 0, :].bitcast(BF16), ["stg0"], 18
                        tp_, tpk = ob[1][:, :].bitcast(BF16).rearrange("p (k q) -> p k q", k=8), "ps_ob1"
                    ssk, rsk = "sm_ss%d" % par, "sm_rs%d" % par
                    P.dma("sp", "ld_x%d" % par, xtb[:], xs_[t * 128:(t + 1) * 128, :], r=[xkey], w=[xk])
                    yield
                    P.act(ACT(ht_, xtb[:], AF.Square, accum_out=sm[:, c0:c0 + 1]), r=[xk], w=htk + [ssk])
                    yield
                    P.act(ACT(sm[:, c0 + 1:c0 + 2], sm[:, c0:c0 + 1], AF.Ln, scale=1.0 / 1024, bias=EPS), r=[ssk], w=[rsk + "_l"])
                    yield
                    P.act(ACT(sm[:, c0 + 1:c0 + 2], sm[:, c0 + 1:c0 + 2], AF.Exp, scale=-0.5), r=[rsk + "_l"], w=[rsk])
                    yield
                    P.dve(STT(ht_, xtb[:], sm[:, c0 + 1:c0 + 2], gs, ALU.mult, ALU.mult), r=[xk, rsk, "gs", "ada"], w=htk)
                    yield
                    P.dve(TT(hb_, ht_, shift, ALU.add), r=htk + ["ada"], w=hbk)
                    yield
                    for k in range(8):
                        P.pe(TR(tp_[:, k, :], hb_[:, k * 128:(k + 1) * 128], identb[:]), r=hbk + ["identb"], w=[tpk])
                    yield
                    P.act(ACT(hdst[:, :, t * 128:(t + 1) * 128], tp_[:, :, :], AF.Copy), r=[tpk], w=[(hk, t)])

                def zip2(gens):
                    gens = list(gens)
                    while gens:
                        for g_ in list(gens):
                            try:
                                next(g_)
                            except StopIteration:
                                gens.remove(g_)
                for i_ in range(0, len(pre_tiles), 2):
                    zip2([pre_tile(*pre_tiles[i_ + j_], j_) for j_ in range(2) if i_ + j_ < len(pre_tiles)])
                hT_all = [("hT", t) for t in range(NT)]

                def proj_tm(col0, ncols, evac, wt=None, wk=None, wcol=0, q=False):
                    if wt is None:
                        wt, wk = wload(Win[:, col0:col0 + ncols], 8, ncols)
                        wcol = 0
                    hsrc, nt_, hk = (hTq, NTq, hqk) if q else (hT, NT, "hT")
                    for t in range(nt_):
                        b = nxt("pj", 3)
                        for k in range(8):
                            P.pe(MM(pj[b][:, 0:ncols], hsrc[:, k, t * 128:(t + 1) * 128],
                                    wt[:, k, wcol:wcol + ncols], k == 0, k == 7),
                                 r=[(hk, t), wk], w=["ps_pj%d" % b])
                        evac(pj[b], "ps_pj%d" % b, t)
                    return wt, wk

                def proj_fm(wt, wk, wcol, m, evac, q=False):
                    if q and own:
                        b = nxt("pj", 3)
                        for k in range(8):
                            P.pe(MM(pj[b][0:m, 0:256], wt[:, k, wcol:wcol + m], hTo[:, k, :], k == 0, k == 7),
                                 r=[("hTo", 0), ("hTo", 1), wk], w=["ps_pj%d" % b])
                        evac(pj[b], "ps_pj%d" % b, 0)
                        return
                    for tb in range(2):
                        b = nxt("pj", 3)
                        for k in range(8):
                            P.pe(MM(pj[b][0:m, :], wt[:, k, wcol:wcol + m], hT[:, k, tb * 512:(tb + 1) * 512],
                                    k == 0, k == 7),
                                 r=hT_all[tb * 4:(tb + 1) * 4] + [wk], w=["ps_pj%d" % b])
                        evac(pj[b], "ps_pj%d" % b, tb)

                def branch_merge(w_br, mcol, first):
                    P.tag = "%d.%d.merge" % (kind, li)
                    for cb in range(2):
                        wA, wAk = wload(w_br[:, cb * 512:(cb + 1) * 512], 4, 512)
                        wM, wMk = wload(Win[:, mcol + cb * 512: mcol + (cb + 1) * 512], 8, 512)
                        for t in range(NTq):
                            pA, kA_ = ALLB[nxt("allb", 7)]
                            for k in range(4):
                                P.pe(MM(pA[:, :], YT[:, k, t * 128:(t + 1) * 128], wA[:, k, :], k == 0, k == 3),
                                     r=[("YT", t), wAk], w=[kA_])
                            pL, kL_ = ALLB[nxt("allb", 7)]
                            for k in range(8):
                                P.pe(MM(pL[:, :], hTq[:, k, t * 128:(t + 1) * 128], wM[:, k, :], k == 0, k == 7),
                                     r=[(hqk, t), wMk], w=[kL_])
                            si = nxt("sgm", 2)
                            P.act(ACT(sgm[si], pL[:, :], AF.Sigmoid), r=[kL_], w=["sgm%d" % si])
                            asl = acc[:, t, cb * 512:(cb + 1) * 512]
                            if first:
                                P.dve(TT(asl, sgm[si], pA[:, :], ALU.mult),
                                      r=["sgm%d" % si, kA_], w=[("acc", t, cb)])
                            else:
                                P.dve(TT(sgm[si], sgm[si], pA[:, :], ALU.mult),
                                      r=["sgm%d" % si, kA_], w=["sgm%d" % si])
                                P.pool(TT(asl, asl, sgm[si], ALU.add),
                                       r=["sgm%d" % si, ("acc", t, cb)], w=[("acc", t, cb)])

                def ytok_to_YT(t, qt):
                    for c in range(4):
                        P.pe(TR(tpb[:, c, :], Ytok[:, qt, c * 128:(c + 1) * 128], identb[:]),
                             r=[("Ytok", qt), "identb"], w=["ps_tpb"])
                    P.act(ACT(YT[:, :, t * 128:(t + 1) * 128], tpb[:, 0:4, :], AF.Copy), r=["ps_tpb"], w=[("YT", t)])

                def attn_mixer(mx):
                    isA = mx == "A"
                    P.tag = "%d.%d.%s.proj" % (kind, li, mx)
                    qc, kc, vc, gc = (QA, KA, VA, GA) if isA else (QC, KC, VC, GC)
                    nh = 4 if isA else 8
                    e = 128 if isA else 64
                    nm = 2 if isA else 1
                    okey, vkey = ("ndk", "ndv") if isA else ("nnk", "nnv")
                    va = vaug[:, :, 0:nh * (e + 2)].rearrange("p t (h e) -> p t h e", h=nh)
                    P.dve(lambda en: en.memset(va[:, :, :, e:e + 1], 1.0), w=[("vaug", t) for t in range(NT)])

                    ropeCo, ropeSo = acc[:, 4, 0:256], acc[:, 4, 256:512]
                    maskO_bf = acc[:, 5, :].bitcast(BF16).rearrange("p (s q) -> p s q", s=32)
                    ropeC = Rbraw[:, 0:1024]
                    ropeS = seg[:, :, :].rearrange("p a b -> p (a b)")
                    SEGK = [("seg", 0), ("seg", 1)]
                    if samp:
                        vca = vaugc[:, :, 0:nh * (e + 2)].rearrange("p t (h e) -> p t h e", h=nh)
                        ck = D["cdk" if isA else "cnk"][li]
                        cv = D["cdv" if isA else "cnv"][li]
                        P.dma("pool", "ld_ck", gbuf[:, 0:4, :], ck.rearrange("(t p) c -> p t c", p=128),
                              w=[("gbuf", t) for t in range(4)])
                        for tl in range(4):
                            P.dma("pool", "ld_cv", vca[:, tl, :, 0:e],
                                  cv[tl * 128:(tl + 1) * 128, :].rearrange("p (h e) -> p h e", h=nh), w=["vaugc"])
                        P.dve(lambda en: en.memset(vca[:, :, :, e:e + 1], 1.0), w=["vaugc1"])
                        def ctx_tail():
                            for tl in range(4):
                                for pr in range(4):
                                    P.pe(TR(tpb[:, pr, :], gbuf[:, tl, pr * 128:(pr + 1) * 128], identb[:]),
                                         r=[("gbuf", tl), "identb"], w=["ps_tpb"])
                                P.act(ACT(kTc[:, :, tl * 128:(tl + 1) * 128], tpb[:, 0:4, :], AF.Copy), r=["ps_tpb"],
                                      w=[("kTc", tl, 0), ("kTc", tl, 1)])
                            for hm in range(8):
                                pb = (hm % 2) * 64
                                si = nxt("sqb", 2)
                                P.act(ACT(sqb[si][0:64, :], kTc[pb:pb + 64, (hm // 2), :], AF.Square),
                                      r=[("kTc", tl, (hm % 2)) for tl in range(4)], w=["sqb%d" % si])
                                b = nxt("st", 2)
                                P.pe(MM(st[b][:, :], onesb[0:64, :], sqb[si][0:64, :]), r=["sqb%d" % si, "onesb"], w=["ps_st%d" % b])
                                P.dve(lambda en, b=b, hm=hm: en.tensor_reduce(out=nrm2c[:, hm:hm + 1], in_=st[b][:, :], axis=AX.X,
                                                                              op=ALU.max), r=["ps_st%d" % b], w=[("nrm2c", hm)])
                        if isA:
                            P.dma("sp", "ld_rope", ropeC, D["ropeC"], w=["Rb"])
                            P.dma("sp", "ld_rope", ropeS, D["ropeS"], w=SEGK)
                            if own:
                                P.dma("sp", "ld_rope", acc[:, 4, 0:256], D["ropeCo"], w=[("acc", 4, 0)])
                                P.dma("sp", "ld_rope", acc[:, 4, 256:512], D["ropeSo"], w=[("acc", 4, 0)])
                        elif own:
                            P.dma("pool", "ld_mo", maskO_bf, D["maskO"], w=[("acc", 5, 0), ("acc", 5, 1)])
                    pend = []
                    for which, c0, dst in ((0, qc, qT), (1, kc, kT)):
                        wt, wk = wload(Win[:, c0:c0 + 512], 8, 512)
                        for pr in range(4):
                            def ev(ps, pk, tb, pr=pr, which=which, dst=dst):
                                qo = own and which == 0
                                W = 256 if qo else 512
                                dsl = dst[:, pr, tb * 512:tb * 512 + W]
                                wkeys = [("qkT", which, pr, tb, 0), ("qkT", which, pr, tb, 1)]
                                if samp and isA:
                                    qi = nxt("sgm", 2)
                                    qf, qk_ = sgm[qi][:, 0:W], "sgm%d" % qi
                                    rc = ropeCo if qo else ropeC[:, tb * 512:(tb + 1) * 512]
                                    rs = ropeSo if qo else ropeS[:, tb * 512:(tb + 1) * 512]
                                    rk = [("acc", 4, 0)] if qo else ["Rb"] + SEGK
                                    P.act(ACT(qf, ps[:, 0:W], AF.Copy), r=[pk], w=[qk_])

                                    def rope_part(qf=qf, qk_=qk_, rc=rc, rs=rs, rk=rk, W=W, dsl=dsl, wkeys=wkeys):
                                        br = nxt("st", 2)
                                        P.pe(MM(st[br][:, 0:W], ropeRT[:], qf), r=[qk_, "ropeRT"], w=["ps_st%d" % br],
                                             ni=2 if W == 512 else 1)
                                        P.dve(TT(qf, qf, rc, ALU.mult), r=[qk_] + rk, w=[qk_])
                                        P.dve(TT(ysb[:, 0:W], st[br][:, 0:W], rs, ALU.mult), r=["ps_st%d" % br] + rk, w=["ysb"])
                                        P.dve(TT(dsl, qf, ysb[:, 0:W], ALU.add), r=[qk_, "ysb"], w=wkeys)
                                    pend.append(rope_part)
                                else:
                                    P.act(ACT(dsl, ps[:, 0:W], AF.Copy), r=[pk], w=wkeys)
                                si = nxt("sqb", 2)
                                P.act(ACT(sqb[si][:, 0:W], ps[:, 0:W], AF.Square), r=[pk], w=["sqb%d" % si])
                                for half in range(2):
                                    def norm_part(si=si, tb=tb, half=half, hm=2 * pr + half, W=W):
                                        b = nxt("st", 2)
                                        P.pe(MM(st[b][:, 0:W], onesb[half * 64:(half + 1) * 64, :], sqb[si][half * 64:(half + 1) * 64, 0:W]),
                                             r=["sqb%d" % si, "onesb"], w=["ps_st%d" % b])
                                        ns = W // 256
                                        P.dve(lambda en, b=b: en.tensor_reduce(
                                            out=nrm2[:, which, hm, tb * 2:tb * 2 + ns],
                                            in_=st[b][:, 0:W].rearrange("p (s q) -> p s q", s=ns), axis=AX.X, op=ALU.max),
                                            r=["ps_st%d" % b], w=[("nrm2", which, hm, tb)])
                                    pend.append(norm_part)
                                while len(pend) > (3 if (samp and isA) else 2):
                                    pend.pop(0)()
                            proj_fm(wt, wk, pr * 128, 128, ev, q=(which == 0))
                        while pend:
                            pend.pop(0)()
                        if which == 1 and not samp:
                            def evk(ps, pk, t):
                                si = nxt("stg", 2)
                                P.act(ACT(stg[si], ps[:, :], AF.Copy), r=[pk], w=["stg%d" % si])
                                P.dma("sp", "o_%s%d" % (okey, si), O[okey][t // 2, li, (t % 2) * 128:(t % 2 + 1) * 128, :],
                                      stg[si], r=["stg%d" % si])
                            proj_tm(c0, 512, evk, wt, wk, 0)
                    if samp:
                        ctx_tail()
                    def evv(ps, pk, t):
                        if not samp:
                            si = nxt("stg", 2)
                            P.act(ACT(stg[si], ps[:, :], AF.Copy), r=[pk], w=["stg%d" % si])
                            P.dma("sp", "o_%s%d" % (vkey, si), O[vkey][t // 2, li, (t % 2) * 128:(t % 2 + 1) * 128, :],
                                  stg[si], r=["stg%d" % si])
                        P.dve(CP(va[:, t, :, 0:e], ps[:, :].rearrange("p (h e) -> p h e", h=nh)),
                              r=[pk], w=[("vaug", t)])
                    proj_tm(vc, 512, evv)
                    def evg(ps, pk, t):
                        P.act(ACT(gbuf[:, t, :], ps[:, :], AF.Silu), r=[pk], w=[("gbuf", t)])
                    proj_tm(gc, 512, evg, q=True)
                    nr = [("nrm2", w_, hm, tb) for w_ in range(2) for hm in range(8) for tb in range(2)]
                    if own:
                        nr = [("nrm2", 0, hm, 0) for hm in range(8)] + [("nrm2", 1, hm, tb) for hm in range(8) for tb in range(2)]
                        P.dve(CP(nrm2s[:, 0, :], nrm2[:, 0, :, 0]), r=nr, w=["nrm2s"])
                        P.dve(lambda en: en.tensor_reduce(out=nrm2s[:, 1, :], in_=nrm2[:, 1, :, :], axis=AX.X, op=ALU.max),
                              r=nr + ["nrm2s"], w=["nrm2s"])
                    elif samp:
                        P.dve(lambda en: en.tensor_reduce(out=nrm2s[:], in_=nrm2[:, :, :, :], axis=AX.X, op=ALU.max),
                              r=nr, w=["nrm2s"])
                    if samp:
                        P.dve(TT(nrm2s[:, 1, :], nrm2s[:, 1, :], nrm2c[:], ALU.max),
                              r=["nrm2s"] + [("nrm2c", hm) for hm in range(8)], w=["nrm2s"])
                        nmv = negm[:, :, 0]
                        P.dve(TT(nmv, nrm2s[:, 0, :], nrm2s[:, 1, :], ALU.mult), r=["nrm2s"], w=["negm0"])
                    else:
                        nmv = negm[:]
                        P.dve(TT(nmv, nrm2[:, 0, :, :], nrm2[:, 1, :, :], ALU.mult), r=nr, w=["negm0"])
                    P.act(ACT(nmv, nmv, AF.Sqrt), r=["negm0"], w=["negm1"])
                    P.dve(TS(nmv, nmv, -0.125, ALU.mult), r=["negm1"], w=["negm"])
                    if isA:
                        lam_init = 0.8 - 0.6 * math.exp(-0.3 * li)
                        P.dma("sp", "ld_lam", lamt[:], D["lamv"][li].partition_broadcast(128), w=["lamt"])
                        P.dve(TT(lamt[:, 0:2, :], lamt[:, 0:2, :], lamt[:, 2:4, :], ALU.mult), r=["lamt"], w=["lamt2"])
                        P.dve(lambda en: en.tensor_reduce(out=sm[:, 20:22], in_=lamt[:, 0:2, :], axis=AX.X, op=ALU.add),
                              r=["lamt2"], w=["sm_l0"])
                        P.act(ACT(sm[:, 22:24], sm[:, 20:22], AF.Exp), r=["sm_l0"], w=["sm_l1"])
                        P.dve(STT(sm[:, 24:25], sm[:, 23:24], -lam_init, sm[:, 22:23], ALU.add, ALU.subtract),
                              r=["sm_l1"], w=["sm_nl"])
                        P.dma("sp", "ld_sg", sgv[:], D["subln_g"][li].partition_broadcast(128), w=["sgv0"])
                        P.dve(TS(sgv[:], sgv[:], 1.0 - lam_init, ALU.mult), r=["sgv0"], w=["sgv"])

                    if isA and kind == 0 and li == 0:
                        for l_ in range(1, n_layers):
                            compute_ada(l_)
                    P.tag = "%d.%d.%s.attn" % (kind, li, mx)

                    def o_post(ov, obk, t, h, ysl, gsl, gkey, ykey, sb=32):
                        K = lambda n: "%s_%d" % (n, sb)
                        P.dve(lambda en, ov=ov: en.reciprocal(out=sm[:, sb:sb + nm], in_=ov[:, :, e]),
                              r=[obk], w=[K("sm_rl")])
                        oi = nxt("otmp", 4)
                        yield
                        if isA:
                            P.dve(TT(sm[:, sb + 2:sb + 3], sm[:, sb + 1:sb + 2], sm[:, 24:25], ALU.mult),
                                  r=[K("sm_rl"), "sm_nl"], w=[K("sm_c1")])
                            P.act(ACT(otmp[oi][:], ov[:, 0, 0:e], AF.Copy, scale=sm[:, sb:sb + 1]),
                                  r=[obk, K("sm_rl")], w=["otmp%d" % oi])
                            yield
                            P.dve(STT(otmp[oi][:], ov[:, 1, 0:e], sm[:, sb + 2:sb + 3], otmp[oi][:], ALU.mult, ALU.add),
                                  r=[obk, K("sm_c1"), "otmp%d" % oi], w=["otmp%d" % oi])
                            oj = nxt("otmp", 4)
                            yield
                            P.act(ACT(otmp[oj][:], otmp[oi][:], AF.Square, accum_out=sm[:, sb + 4:sb + 5]),
                                  r=["otmp%d" % oi], w=["otmp%d" % oj, K("sm_os")])
                            yield
                            P.act(ACT(sm[:, sb + 5:sb + 6], sm[:, sb + 4:sb + 5], AF.Ln, scale=1.0 / 128, bias=EPS),
                                  r=[K("sm_os")], w=[K("sm_or") + "_l"])
                            yield
                            P.act(ACT(sm[:, sb + 5:sb + 6], sm[:, sb + 5:sb + 6], AF.Exp, scale=-0.5),
                                  r=[K("sm_or") + "_l"], w=[K("sm_or")])
                            yield
                            P.dve(STT(otmp[oi][:], otmp[oi][:], sm[:, sb + 5:sb + 6], sgv[:], ALU.mult, ALU.mult),
                                  r=["otmp%d" % oi, K("sm_or"), "sgv"], w=["otmp%d" % oi])
                            yield
                            P.dve(TT(ysl, otmp[oi][:], gsl, ALU.mult), r=["otmp%d" % oi, gkey], w=[ykey])
                        else:
                            P.act(ACT(otmp[oi][:, 0:e], ov[:, 0, 0:e], AF.Copy, scale=sm[:, sb:sb + 1]),
                                  r=[obk, K("sm_rl")], w=["otmp%d" % oi])
                            yield
                            P.dve(TT(ysl, otmp[oi][:, 0:e], gsl, ALU.mult), r=["otmp%d" % oi, gkey], w=[ykey])

                    def zipg(gens):
                        gens = list(gens)
                        while gens:
                            for g_ in list(gens):
                                try:
                                    next(g_)
                                except StopIteration:
                                    gens.remove(g_)

                    if samp:
                        NAT = {0: [0, 1, 2, 3], 1: [0, 1, 2, 3, 4, 5], 2: [2, 3, 4, 5, 6, 7], 3: [4, 5, 6, 7]}
                        if not isA and not own:
                            P.dve(lambda en: en.memset(natab_full[:, :, :], NEG), w=["stg0", "stg1"])
                            P.dve(lambda en: en.memset(natab_band[:, :, :], NEG), w=["sgm0", "sgm1"])
                        for h in range(nh):
                            if not isA and own:
                                P.dma("pool", "ld_rpb", natab_full[:, :, :], D["rpbO"][li, h], w=["stg0", "stg1"])
                                P.dve(STT(natab_full[:, :, :], natab_full[:, :, :], 8.0, maskO_bf, ALU.mult, ALU.add),
                                      r=["stg0", "stg1", ("acc", 5, 0), ("acc", 5, 1)], w=["stg0", "stg1"])
                            elif not isA:
                                rp = D["rpbT"][li, h]
                                for tab, tkeys, lo0, dd0, n in ((natab_full, ["stg0", "stg1"], 8, 0, 15),
                                                                (natab_band, ["sgm0", "sgm1"], 12, 4, 8)):
                                    for half in range(2):
                                        sl0 = lo0 + half
                                        P.dma("pool", "ld_rpb", tab[half * 64:(half + 1) * 64, sl0:sl0 + n, :],
                                              rp[dd0:dd0 + n].rearrange("d k q -> k d q"), w=tkeys)
                                        tsl = tab[half * 64:(half + 1) * 64, sl0:sl0 + n, :]
                                        P.dve(STT(tsl, tsl, 8.0, bc(namask[half * 64:(half + 1) * 64, :], 1, [64, n, 64]),
                                                  ALU.mult, ALU.add), r=tkeys + ["namask"], w=tkeys)
                            for qb in ([0] if own else range(4)):
                                if isA or own:
                                    tiles = [("l", kt) for kt in range(8)] + [("c", kt) for kt in range(4)]
                                else:
                                    tiles = [("l", a) for a in NAT[qb]] + [("c", kt) for kt in range(4)]
                                nt_ = len(tiles)
                                for m in range(nm):
                                    hm = nm * h + m
                                    pb = (hm % 2) * 64
                                    qv = qT[pb:pb + 64, (hm // 2), qb * 256:(qb + 1) * 256]
                                    pvq = []
                                    for i, (kd, kt) in enumerate(tiles):
                                        b = nxt("st", 2)
                                        bias = (not isA) and kd == "l"
                                        if kd == "l":
                                            ksl = kT[pb:pb + 64, (hm // 2), kt * 128:(kt + 1) * 128]
                                            kr = [("qkT", 1, (hm // 2), kt // 4, (hm % 2))]
                                        else:
                                            ksl = kTc[pb:pb + 64, (hm // 2), kt * 128:(kt + 1) * 128]
                                            kr = [("kTc", kt, (hm % 2))]
                                        P.pe(MM(st[b][:, 0:256], ksl, qv, True, not bias),
                                             r=kr + [("qkT", 0, (hm // 2), qb // 2, (hm % 2))], w=["ps_st%d" % b])
                                        if bias:
                                            tab, tkeys = (natab_full, ["stg0", "stg1"]) if (qb in (0, 3) or own) else (natab_band, ["sgm0", "sgm1"])
                                            s0 = 4 * kt if own else 15 - 2 * kt + 4 * qb
                                            P.pe(MM(st[b][:, 0:256], identb[:],
                                                    tab[:, s0:s0 + 4, :].rearrange("p s q -> p (s q)"), False, True),
                                                 r=tkeys + ["identb"], w=["ps_st%d" % b])
                                        P.act(ACT(PTb[:, i, :], st[b][:, 0:256], AF.Exp, scale=0.125, bias=negm[:, hm, 0:1]),
                                              r=["ps_st%d" % b, "negm"], w=[("PT", i)])

                                        def pv(i=i, kd=kd, kt=kt, m=m):
                                            for qt in range(2):
                                                ov = ob[qt][:, 0:nm * (e + 1)].rearrange("p (m e) -> p m e", m=nm)
                                                if kd == "l":
                                                    vsl, vr = va[:, kt, h, 0:e + 1], [("vaug", kt)]
                                                else:
                                                    vsl, vr = vca[:, kt, h, 0:e + 1], ["vaugc", "vaugc1"]
                                                P.pe(MM(ov[:, m, :], PTb[:, i, qt * 128:(qt + 1) * 128], vsl, i == 0, i == nt_ - 1),
                                                     r=[("PT", i)] + vr, w=["ps_ob%d" % qt])
                                        pvq.append(pv)
                                        while len(pvq) > 2:
                                            pvq.pop(0)()
                                    while pvq:
                                        pvq.pop(0)()
                                posts = []
                                for qt in range(2):
                                    t = 2 * qb + qt
                                    ov = ob[qt][:, 0:nm * (e + 1)].rearrange("p (m e) -> p m e", m=nm)
                                    gsl = gbuf[:, t, h * e:(h + 1) * e]
                                    posts.append(o_post(ov, "ps_ob%d" % qt, t, h, gsl, gsl, ("gbuf", t), ("gbuf", t), sb=32 if qt == 0 else 96))
                                zipg(posts)
                        for t in range(NTq):
                            for c in range(4):
                                P.pe(TR(tpb[:, c, :], gbuf[:, t, c * 128:(c + 1) * 128], identb[:]),
                                     r=[("gbuf", t), "identb"], w=["ps_tpb"])
                            P.act(ACT(YT[:, :, t * 128:(t + 1) * 128], tpb[:, 0:4, :], AF.Copy), r=["ps_tpb"], w=[("YT", t)])

                    def p_stage1(s, h, sb):
                        for m in range(nm):
                            hm = nm * h + m
                            pb = (hm % 2) * 64
                            qv = qT[pb:pb + 64, (hm // 2), s * 256:(s + 1) * 256]
                            for kt in range(2):
                                b = nxt("st", 2)
                                P.pe(MM(st[b][:, 0:256], kT[pb:pb + 64, (hm // 2), s * 256 + kt * 128: s * 256 + (kt + 1) * 128], qv),
                                     r=[("qkT", 0, (hm // 2), s // 2, (hm % 2)), ("qkT", 1, (hm // 2), s // 2, (hm % 2))], w=["ps_st%d" % b])
                                P.act(ACT(PTb[:, sb + m * 2 + kt, :], st[b][:, 0:256], AF.Exp, scale=0.125,
                                          bias=negm[:, hm, s:s + 1]),
                                      r=["ps_st%d" % b, "negm"], w=[("PT", sb + m * 2 + kt)])

                    def p_stage2(s, h, sb):
                        posts = []
                        for qt in range(2):
                            t = 2 * s + qt
                            b = nxt("ob", 2)
                            ov = ob[b][:, 0:nm * (e + 1)].rearrange("p (m e) -> p m e", m=nm)
                            for m in range(nm):
                                for kt in range(2):
                                    P.pe(MM(ov[:, m, :], PTb[:, sb + m * 2 + kt, qt * 128:(qt + 1) * 128],
                                            va[:, 2 * s + kt, h, 0:e + 1], kt == 0, kt == 1),
                                         r=[("PT", sb + m * 2 + kt), ("vaug", 2 * s + kt)], w=["ps_ob%d" % b])
                            posts.append(o_post(ov, "ps_ob%d" % b, t, h, Ytok[:, qt, h * e:(h + 1) * e], gbuf[:, t, h * e:(h + 1) * e],
                                                ("gbuf", t), ("Ytok", qt), sb=32 if qt == 0 else 96))
                        zipg(posts)

                    hcnt = 0
                    for s in range(4 if not samp else 0):
                        pending = None
                        for h in range(nh):
                            sb = (hcnt % 2) * 4
                            hcnt += 1
                            p_stage1(s, h, sb)
                            if pending is not None:
                                p_stage2(*pending)
                            pending = (s, h, sb)
                        p_stage2(*pending)
                        for qt in range(2):
                            ytok_to_YT(2 * s + qt, qt)

                    branch_merge(D["w_br_a"][li] if isA else D["w_br_c"][li], MRG + (0 if isA else 2048), isA)

                def ssd_mixer():
                    xbcT = kT
                    P.tag = "%d.%d.B.proj" % (kind, li)
                    SEGK2 = [("seg", 0), ("seg", 1)]
                    xtok = vaug[:, :, 0:512]
                    def evz(ps, pk, t):
                        P.act(ACT(gbuf[:, t, :], ps[:, :], AF.Silu), r=[pk], w=[("gbuf", t)])
                    proj_tm(ZB, 512, evz, q=True)
                    P.dma("sp", "ld_cw", convw[:], D["convwT"][li], w=["convw"])
                    P.dma("sp", "ld_cw", convb[:], D["convbT"][li], w=["convb"])
                    P.dma("sp", "ld_cw", dtbias[:], D["dt_bias"][li].partition_broadcast(128), w=["dtbias"])
                    P.dma("sp", "ld_cw", nega[:], D["a_log"][li].partition_broadcast(128), w=["nega0"])
                    P.dma("sp", "ld_cw", dsum[:], D["d_skip"][li].partition_broadcast(128), w=["dsum0"])
                    P.dma("sp", "ld_cw", ssdg[:], D["ssd_norm_g"][li].partition_broadcast(128), w=["ssdg"])
                    P.act(ACT(nega[:], nega[:], AF.Exp), r=["nega0"], w=["nega1"])
                    P.dve(TS(nega[:], nega[:], -1.0, ALU.mult), r=["nega1"], w=["nega"])
                    P.dve(TT(dsum[:, 0:8], dsum[:, 0:8], dsum[:, 8:16], ALU.add), r=["dsum0"], w=["dsum"])
                    P.dve(lambda en: en.memset(xpre[:], 0.0), w=["Rb", ("Rbh", 0), ("Rbh", 1)])
                    for blk, (c0, ncol) in enumerate(((XB, 512), (XB + 512, 256))):
                        wt, wk = wload(Win[:, c0:c0 + ncol], 8, ncol)
                        for cc in range(ncol // 128):
                            c = blk * 4 + cc

                            def evx_s(ps, pk, tb, c=c):
                                xps = Rbraw[:, 0:1028]
                                caf = seg[:, :, :].rearrange("p a b -> p (a b)")
                                P.act(ACT(xps[:, 2 + tb * 512:2 + (tb + 1) * 512], ps[:, :], AF.Copy), r=[pk], w=["Rb"])
                                if tb == 0:
                                    return
                                P.dve(TS(caf, xps[:, 0:1024], convw[:, c, 0:1], ALU.mult), r=["Rb", "convw"], w=SEGK2)
                                for j in range(1, 5):
                                    P.dve(STT(caf, xps[:, j:j + 1024], convw[:, c, j:j + 1], caf, ALU.mult, ALU.add),
                                          r=["Rb", "convw"] + SEGK2, w=SEGK2)
                                P.act(ACT(xbcT[:, c, :], caf, AF.Silu, bias=convb[:, c:c + 1]), r=SEGK2 + ["convb"],
                                      w=[("qkT", 1, c, tb_, hf) for tb_ in range(2) for hf in range(2)])

                            def evx(ps, pk, tb, c=c):
                                P.act(ACT(xpre[:, 2 * tb:2 * tb + 2, 2:258], ps[:, :].rearrange("p (s q) -> p s q", s=2),
                                          AF.Copy), r=[pk, "Rb"], w=[("Rbh", tb)])
                                sl = slice(2 * tb, 2 * tb + 2)
                                P.dve(TS(cacc[:, sl, :], xpre[:, sl, 0:256], convw[:, c, 0:1], ALU.mult),
                                      r=["Rb", ("Rbh", tb), "convw"], w=[("seg", tb)])
                                for j in range(1, 5):
                                    P.dve(STT(cacc[:, sl, :], xpre[:, sl, j:j + 256], convw[:, c, j:j + 1], cacc[:, sl, :],
                                              ALU.mult, ALU.add), r=["Rb", ("Rbh", tb), "convw", ("seg", tb)], w=[("seg", tb)])
                                P.act(ACT(xbcT[:, c, tb * 512:(tb + 1) * 512].rearrange("p (s q) -> p s q", s=2),
                                          cacc[:, sl, :], AF.Silu, bias=convb[:, c:c + 1]),
                                      r=[("seg", tb), "convb"], w=[("qkT", 1, c, tb, 0), ("qkT", 1, c, tb, 1)])
                            proj_fm(wt, wk, cc * 128, 128, evx_s if samp else evx)
                    wt_dt, wk_dt = wload(Win[:, DTB:DTB + 16], 8, 16)
                    bdt = nxt("pj", 3)
                    for t in range(NT):
                        for k in range(8):
                            P.pe(MM(pj[bdt][:, t * 16:(t + 1) * 16], hT[:, k, t * 128:(t + 1) * 128], wt_dt[:, k, 0:16], k == 0, k == 7),
                                 r=[("hT", t), wk_dt], w=["ps_pj%d" % bdt])
                    DT0 = [("dt0", t) for t in range(NT)]
                    DT1 = [("dt1", t) for t in range(NT)]
                    DTK = [("dt", t) for t in range(NT)]
                    P.dve(TT(dtb[:, :, :], pj[bdt][:, 0:128].rearrange("p (t c) -> p t c", t=NT), bc(dtbias[:], 1, [128, NT, 16]), ALU.add),
                          r=["ps_pj%d" % bdt, "dtbias"], w=DT0)
                    P.act(ACT(dtb[:, :, :], dtb[:, :, :], AF.Exp), r=DT0, w=DT1)
                    P.act(ACT(dtb[:, :, :], dtb[:, :, :], AF.Ln, bias=1.0), r=DT1, w=DTK)
                    P.dve(TT(lab[:, :, :], dtb[:, :, :], bc(nega[:], 1, [128, NT, 16]), ALU.mult), r=DTK + ["nega"],
                          w=[("la", t) for t in range(NT)])
                    for t in range(NT):
                        for c in range(5):
                            P.pe(TR(tpb[:, c, :], xbcT[:, c, t * 128:(t + 1) * 128], identb[:]),
                                 r=[("qkT", 1, c, t // 4, 0), ("qkT", 1, c, t // 4, 1), "identb"], w=["ps_tpb"])
                        P.act(ACT(xtok[:, t, :], tpb[:, 0:4, :], AF.Copy), r=["ps_tpb"], w=[("vaug", t)])
                        P.act(ACT(btok[:, t, :], tpb[:, 4, :], AF.Copy), r=["ps_tpb"], w=[("btok", t)])

                    P.tag = "%d.%d.B.loop" % (kind, li)
                    nseq = 1 if samp else 4
                    nys = 8 if samp else 2
                    nch = NT // nseq
                    D1MAP = {"Rb": "xt0", ("seg", 0): "xt1", ("seg", 1): "xt1", "dec": "stg0", ("MT", 0): "stg1", ("MT", 1): "stg1",
                             "xdt": "hbf0", "wxdt": "hbf0", "ysb": "sgm0", ("ytmp", 0): "sgm1", ("ytmp", 1): "sgm1"}
                    D1OWN = {"acum", "eacum", "wexp0", "wexp", "cdec", ("cbm", 0), ("cbm", 1), ("hst", 0), ("hst", 1), "hstb"}

                    def km1(k):
                        if k in D1MAP:
                            return D1MAP[k]
                        if k in D1OWN:
                            return (k, "d1")
                        return k
                    BUF = {0: (acum, Rb, seg, dec, cbm, MTb, xdt, wxdt, hst, hstb, ysb, ytmp),
                           1: (acum2, xt[0][:, :].rearrange("p (h i) -> p h i", h=8), xt[1][:, :].rearrange("p (h i) -> p h i", h=8),
                               stgT[:, 0, :].bitcast(BF16).rearrange("p (h i) -> p h i", h=8),
                               cbm2, stgT[:, 1, :].bitcast(BF16).rearrange("p (h i) -> p h i", h=8),
                               hbf[0][:, 0:512], hbf[0][:, 512:1024], hst2, hstb2, sgm[0], sgm[1])}

                    pcnt = {("s", 0): 0, ("s", 1): 0, ("o", 0): 0, ("o", 1): 0}

                    def chunk_step(s, d, c, hstate, ysw):
                        acum, Rb, seg, dec, cbm, MTb, xdt, wxdt, hst, hstb, ysb, ytmp = BUF[d]
                        cd0 = 40 + 16 * d
                        STL = [(st[0], "ps_st0"), (st[1], "ps_st1")] if d == 0 else [(pj[0], "ps_pj0"), (pj[1], "ps_pj1")]
                        OBL = [(ob[0], "ps_ob0"), (ob[1], "ps_ob1")] if d == 0 else [(pj[2], "ps_pj2")]

                        def nxs():
                            pcnt[("s", d)] += 1
                            return STL[pcnt[("s", d)] % len(STL)]

                        def nxo():
                            pcnt[("o", d)] += 1
                            return OBL[pcnt[("o", d)] % len(OBL)]
                        t = s * nch + c
                        la_t = lab[:, t, d * 8:(d + 1) * 8]
                        ob_b, ko_b = nxo()
                        P.pe(MM(ob_b[:, 0:8], tri[d][:], la_t), r=[("la", t), "triU", "triL"], w=[ko_b])
                        P.pe(MM(ob_b[:, 8:16], onesf[:], la_t), r=[("la", t), "onesf"], w=[ko_b])
                        P.act(ACT(acum[:, 0:16], ob_b[:, 0:16], AF.Copy), r=[ko_b], w=["acum"])
                        yield
                        P.act(ACT(acum[:, 16:24], acum[:, 0:8], AF.Exp), r=["acum"], w=["eacum"])
                        P.dve(TT(acum[:, 24:32], acum[:, 8:16], acum[:, 0:8], ALU.subtract), r=["acum"], w=["wexp0"])
                        P.act(ACT(acum[:, 24:32], acum[:, 24:32], AF.Exp), r=["wexp0"], w=["wexp"])
                        P.act(ACT(sm[:, cd0:cd0 + 8], acum[:, 8:16], AF.Exp), r=["acum"], w=["cdec"])
                        yield
                        P.dve(TT(Rb[:], bc(tri[d][:], 1, [128, 8, 128]), bc(la_t, 2, [128, 8, 128]), ALU.mult),
                              r=[("la", t), "triU", "triL"], w=["Rb", ("Rbh", 0), ("Rbh", 1)])
                        sb_ = []
                        for hh in range(2):
                            yield
                            st_b2, k_b2 = nxs()
                            P.pe(MM(st_b2[:, :], onesf[:], Rb[:, hh * 4:(hh + 1) * 4, :]), r=["Rb", "onesf"],
                                 w=[k_b2], ni=2)
                            for h4 in range(4):
                                hd = hh * 4 + h4
                                P.dve(TS(seg[:, hd, :], st_b2[:, h4 * 128:(h4 + 1) * 128], acum[:, hd:hd + 1],
                                         ALU.subtract, 0.0, ALU.min),
                                      r=[k_b2, "acum"], w=[("seg", hh)])
                        yield
                        P.act(ACT(dec[:], seg[:], AF.Exp), r=[("seg", 0), ("seg", 1)], w=["dec"])
                        yield
                        for g in range(2):
                            st_b3, k_b3 = nxs()
                            P.pe(MM(st_b3[:, 0:128], xbcT[g * 64:(g + 1) * 64, 4, t * 128:(t + 1) * 128],
                                    xbcT[g * 64:(g + 1) * 64, 5, t * 128:(t + 1) * 128]),
                                 r=[("qkT", 1, 4, t // 4, g), ("qkT", 1, 5, t // 4, g)], w=[k_b3])
                            P.dve(TT(cbm[:, g, :], st_b3[:, 0:128], tri[d][:], ALU.mult),
                                  r=[k_b3, "triU", "triL"], w=[("cbm", g)])
                        for g in range(2):
                            P.dve(TT(MTb[:, g * 4:(g + 1) * 4, :], dec[:, g * 4:(g + 1) * 4, :],
                                     bc(cbm[:, g, :], 1, [128, 4, 128]), ALU.mult), r=["dec", ("cbm", g)], w=[("MT", g)])
                        yield
                        P.dve(TT(xdt[:].rearrange("p (h q) -> p h q", h=8), xtok[:, t, :].rearrange("p (h q) -> p h q", h=8),
                                 bc(dtb[:, t, d * 8:(d + 1) * 8], 2, [128, 8, 64]), ALU.mult),
                              r=[("vaug", t), ("dt", t)], w=["xdt"])
                        P.dve(TT(wxdt[:].rearrange("p (h q) -> p h q", h=8), xdt[:].rearrange("p (h q) -> p h q", h=8),
                                 bc(acum[:, 24:32], 2, [128, 8, 64]), ALU.mult), r=["xdt", "wexp"], w=["wxdt"])
                        yield
                        ob_by, ko_by = nxo()
                        for hd in range(8):
                            P.pe(MM(ob_by[:, hd * 64:(hd + 1) * 64], MTb[:, hd, :], xdt[:, hd * 64:(hd + 1) * 64]),
                                 r=[("MT", hd // 4), "xdt"], w=[ko_by])
                        first_dir_chunk = c not in ysw
                        ysw.add(c)
                        if hstate[d]:
                            P.act(ACT(ysb, ob_by[:, :], AF.Copy), r=[ko_by], w=["ysb"])
                            for g in range(2):
                                st_bo, k_bo = nxs()
                                for r4 in range(4):
                                    P.pe(MM(st_bo[:, r4 * 64:(r4 + 1) * 64],
                                            xbcT[g * 64:(g + 1) * 64, 5, t * 128:(t + 1) * 128],
                                            hstb[g * 64:(g + 1) * 64, r4 * 64:(r4 + 1) * 64]),
                                         r=[("qkT", 1, 5, t // 4, g), "hstb"], w=[k_bo])
                                P.dve(TT(ytmp[:, g * 256:(g + 1) * 256].rearrange("p (h q) -> p h q", h=4),
                                         st_bo[:, 0:256].rearrange("p (h q) -> p h q", h=4),
                                         bc(acum[:, 16 + 4 * g:20 + 4 * g], 2, [128, 4, 64]), ALU.mult),
                                      r=[k_bo, "eacum"], w=[("ytmp", g)])
                            P.dve(TT(ysb, ysb, ytmp, ALU.add), r=["ysb", ("ytmp", 0), ("ytmp", 1)], w=["ysb"])
                            ysrc, ykey = ysb, "ysb"
                        else:
                            ysrc, ykey = ob_by[:, :], ko_by
                        if first_dir_chunk:
                            P.act(ACT(ysum[:, c % nys, :], ysrc, AF.Copy), r=[ykey], w=[("ysum", c % nys)])
                        else:
                            P.dve(TT(ysum[:, c % nys, :], ysum[:, c % nys, :], ysrc, ALU.add),
                                  r=[ykey, ("ysum", c % nys)], w=[("ysum", c % nys)])
                        yield
                        st_bs, k_bs = nxs()
                        for g in range(2):
                            P.pe(MM(st_bs[0:64, g * 256:(g + 1) * 256], btok[:, t, g * 64:(g + 1) * 64],
                                    wxdt[:, g * 256:(g + 1) * 256]), r=[("btok", t), "wxdt"], w=[k_bs])
                        for g in range(2):
                            hs = hst[g * 64:(g + 1) * 64, :]
                            if hstate[d]:
                                P.dve(TT(hs.rearrange("p (r q) -> p r q", r=4), hs.rearrange("p (r q) -> p r q", r=4),
                                         bc(sm[g * 64:(g + 1) * 64, cd0 + 4 * g:cd0 + 4 + 4 * g], 2, [64, 4, 64]), ALU.mult),
                                      r=["cdec", ("hst", g)], w=[("hst", g)])
                                P.dve(TT(hs, hs, st_bs[0:64, g * 256:(g + 1) * 256], ALU.add),
                                      r=[k_bs, ("hst", g)], w=[("hst", g)])
                            else:
                                P.act(ACT(hs, st_bs[0:64, g * 256:(g + 1) * 256], AF.Copy), r=[k_bs],
                                      w=[("hst", g)])
                        P.act(ACT(hstb[:], hst[:], AF.Copy), r=[("hst", 0), ("hst", 1)], w=["hstb"])
                        hstate[d] = True

                        yield

                    for s in range(nseq):
                        hs = {0: False, 1: False}
                        ysw = set()
                        for d in range(2):
                            acum_, Rb_, seg_, dec_, cbm_, MTb_, xdt_, wxdt_, hst_d, hstb_d, ysb_, ytmp_ = BUF[d]
                            P.keymap = km1 if d == 1 else None
                            if samp:
                                P.dma("sp", "ld_hin", hin[:], D["sst"][li, d].rearrange("(b p) n -> p b n", p=128), w=["hin"])
                                for blk in range(4):
                                    bt = nxt("st", 2)
                                    P.pe(TR(st[bt][0:64, 0:128], hin[:, blk, :], identf[:]), r=["hin", "identf"],
                                         w=["ps_st%d" % bt])
                                    g = blk // 2
                                    P.act(ACT(hst_d[g * 64:(g + 1) * 64, (blk % 2) * 128:(blk % 2 + 1) * 128], st[bt][0:64, 0:128],
                                              AF.Copy), r=["ps_st%d" % bt], w=[("hst", g)])
                                P.act(ACT(hstb_d[:], hst_d[:], AF.Copy), r=[("hst", 0), ("hst", 1)], w=["hstb"])
                                hs[d] = True
                            P.keymap = None
                        for ci in range(nch):
                            gens = [(d, chunk_step(s, d, ci if d == 0 else nch - 1 - ci, hs, ysw)) for d in range(2)]
                            while gens:
                                for dg in list(gens):
                                    P.keymap = km1 if dg[0] == 1 else None
                                    try:
                                        next(dg[1])
                                    except StopIteration:
                                        gens.remove(dg)
                                    P.keymap = None
                        for d in range(2):
                            hst_o = BUF[d][8]
                            P.keymap = km1 if d == 1 else None
                            if not samp:
                                for blk in range(2):
                                    bt = nxt("st", 2)
                                    P.pe(TR(st[bt][:, 0:128], hst_o[:, blk * 128:(blk + 1) * 128], identf[:]),
                                         r=[("hst", 0), ("hst", 1), "identf"], w=["ps_st%d" % bt])
                                    hi = nxt("hout", 2)
                                    P.act(ACT(houts[hi][:], st[bt][:, 0:128], AF.Copy), r=["ps_st%d" % bt], w=["hout%d" % hi])
                                    for g in range(2):
                                        r0 = (4 * g + 2 * blk) * 64
                                        P.dma("sp", "o_ssd%d" % hi, O["nssd"][s, li, d, r0:r0 + 128, :],
                                              houts[hi][:, g * 64:(g + 1) * 64], r=["hout%d" % hi])
                            P.keymap = None
                        if own:
                            for c in range(nch):
                                P.dve(TT(ysb.rearrange("p (h q) -> p h q", h=8), xtok[:, c, :].rearrange("p (h q) -> p h q", h=8),
                                         bc(dsum[:, 0:8], 2, [128, 8, 64]), ALU.mult), r=[("vaug", c), "dsum", "ysb"], w=["ysb"])
                                P.dve(TT(ysum[:, c, :], ysum[:, c, :], ysb, ALU.add), r=["ysb", ("ysum", c)], w=[("ysum", c)])
                            for j in range(2):
                                P.dve(TS(ysb, ysum[:, 0, :], selO[:, j * 8:j * 8 + 1], ALU.mult), r=[("ysum", 0), "selO", "ysb"], w=["ysb"])
                                for c in range(1, nch):
                                    P.dve(STT(ysb, ysum[:, c, :], selO[:, j * 8 + c:j * 8 + c + 1], ysb, ALU.mult, ALU.add),
                                          r=[("ysum", c), "selO", "ysb"], w=["ysb"])
                                P.dve(TT(ysb, ysb, gbuf[:, j, :], ALU.mult), r=["ysb", ("gbuf", j)], w=["ysb"])
                                P.act(ACT(ytmp, ysb, AF.Square, accum_out=sm[:, 50:51]), r=["ysb"],
                                      w=["ytmp", ("ytmp", 0), ("ytmp", 1), "sm_ys"])
                                rstd_from_ss(sm[:, 50:51], sm[:, 51:52], 512, "sm_ys", "sm_yr")
                                P.dve(STT(Ytok[:, j, :], ysb, sm[:, 51:52], ssdg[:], ALU.mult, ALU.mult),
                                      r=["ysb", "sm_yr", "ssdg"], w=[("Ytok", j)])
                                ytok_to_YT(j, j)
                        for c in range(nch if not own else 0):
                            t = s * nch + c
                            oi = 0
                            P.dve(TT(ysb.rearrange("p (h q) -> p h q", h=8), xtok[:, t, :].rearrange("p (h q) -> p h q", h=8),
                                     bc(dsum[:, 0:8], 2, [128, 8, 64]), ALU.mult), r=[("vaug", t), "dsum", "ysb"], w=["ysb"])
                            P.dve(TT(ysb, ysb, ysum[:, c % nys, :], ALU.add), r=["ysb", ("ysum", c % nys)], w=["ysb"])
                            P.dve(TT(ysb, ysb, gbuf[:, t, :], ALU.mult), r=["ysb", ("gbuf", t)], w=["ysb"])
                            P.act(ACT(ytmp, ysb, AF.Square, accum_out=sm[:, 50:51]), r=["ysb"], w=["ytmp", ("ytmp", 0), ("ytmp", 1), "sm_ys"])
                            rstd_from_ss(sm[:, 50:51], sm[:, 51:52], 512, "sm_ys", "sm_yr")
                            P.dve(STT(Ytok[:, c % 2, :], ysb, sm[:, 51:52], ssdg[:], ALU.mult, ALU.mult),
                                  r=["ysb", "sm_yr", "ssdg"], w=[("Ytok", c % 2)])
                            ytok_to_YT(t, c % 2)
                    branch_merge(D["w_br_b"][li], MRG + 1024, False)

                if "A" in stages:
                    attn_mixer("A")
                if "B" in stages:
                    ssd_mixer()
                if "C" in stages:
                    attn_mixer("C")
                if "P" not in stages:
                    continue

                P.tag = "%d.%d.post" % (kind, li)
                for t in range(NTq):
                    xi = nxt("xt", 2)
                    P.act(ACT(hbf[xi][:], acc[:, t, :], AF.Copy), r=[("acc", t, 0), ("acc", t, 1)], w=["hbf0"])
                    for k in range(8):
                        P.pe(TR(tpb[:, k, :], hbf[xi][:, k * 128:(k + 1) * 128], identb[:]),
                             r=["hbf0", "identb"], w=["ps_tpb"])
                    P.act(ACT(hTq[:, :, t * 128:(t + 1) * 128], tpb[:, :, :], AF.Copy), r=["ps_tpb"], w=[(hqk, t)])
                xo = big[:, :].rearrange("p (j f) -> p j f", j=4)
                for t in range(NTq):
                    P.dma("sp" if t % 2 == 0 else "act", "ld_xa%d" % t, acc[:, t, :], xsrc_q[t * 128:(t + 1) * 128, :],
                          r=[(xkq, t)], w=[("acc", t, 0), ("acc", t, 1)])
                for t in range(NTq):
                    xi = t % 2
                    xb_ = acc[:, t, :]
                    XK = [("acc", t, 0), ("acc", t, 1)]
                    if t == 0:
                        wts = [wload(D["w_out"][li][:, cb * 512:(cb + 1) * 512], 8, 512) for cb in range(2)]
                    for cb in range(2):
                        wt, wk = wts[cb]
                        pB, kB_ = ALLB[nxt("allb", 7)]
                        for k in range(8):
                            P.pe(MM(pB[:, :], hTq[:, k, t * 128:(t + 1) * 128], wt[:, k, :], k == 0, k == 7),
                                 r=[(hqk, t), wk], w=[kB_])
                        si = nxt("sgm", 2)
                        P.dve(TT(sgm[si], pB[:, :], gate[:, cb * 512:(cb + 1) * 512], ALU.mult),
                              r=[kB_, "ada"], w=["sgm%d" % si])
                        P.dve(TT(xb_[:, cb * 512:(cb + 1) * 512], xb_[:, cb * 512:(cb + 1) * 512], sgm[si], ALU.add),
                              r=["sgm%d" % si, ("acc", t, cb)], w=[("acc", t, cb)])
                    if li < n_layers - 1 and gath:
                        P.dma("sp", "st_x%d" % xi, xscr_own[t * 128:(t + 1) * 128, :], xb_, r=XK,
                              w=[("xscr_own", t)])
                        if t == NTq - 1:
                            P.op("pool", lambda e: e.collective_compute(
                                "AllGather", ALU.bypass, replica_groups=[[0, 1, 2, 3], [4, 5, 6, 7]],
                                ins=[xscr_own], outs=[xgath]),
                                reads=[("xscr_own", 0), ("xscr_own", 1)], writes=[("xgath", t_) for t_ in range(NT)],
                                dma="cc_x", inc=1)
                    elif li < n_layers - 1:
                        P.dma("sp", "st_x%d" % xi, xscr[t * 128:(t + 1) * 128, :], xb_, r=XK,
                              w=[("xscr", t)])
                        if next_own:
                            for j in range(2):
                                sc = selO[:, j * 8 + t:j * 8 + t + 1]
                                if t == 0:
                                    P.dve(TS(xo[:, j, :], xb_, sc, ALU.mult), r=XK + ["selO"], w=[("xo", j)])
                                else:
                                    P.dve(STT(xo[:, j, :], xb_, sc, xo[:, j, :], ALU.mult, ALU.add),
                                          r=XK + ["selO", ("xo", j)], w=[("xo", j)])
                            if t == NT - 1:
                                for j in range(2):
                                    P.dma("sp", "st_xo", xscr_own[j * 128:(j + 1) * 128, :], xo[:, j, :], r=[("xo", j)],
                                          w=[("xscr_own", j)])
                    else:
                        if t == 0:
                            P.dma("sp", "ld_gs", gs, D["final_g"].partition_broadcast(128), r=["ada"], w=["gs"])
                        c0_ = 16 + 2 * xi
                        P.act(ACT(xt[xi][:], xb_, AF.Square, accum_out=sm[:, c0_:c0_ + 1]),
                              r=XK, w=["xt%d" % xi, "sm_fs%d" % xi])
                        rstd_from_ss(sm[:, c0_:c0_ + 1], sm[:, c0_ + 1:c0_ + 2], 1024, "sm_fs%d" % xi, "sm_fr%d" % xi)
                        P.dve(STT(xt[xi][:], xb_, sm[:, c0_ + 1:c0_ + 2], gs, ALU.mult, ALU.mult),
                              r=XK + ["sm_fr%d" % xi, "gs", "ada"], w=["xt%d" % xi])
                        P.dma("sp" if xi == 0 else "act", "o_y%d" % xi, yout[t * 128:(t + 1) * 128, :], xt[xi][:], r=["xt%d" % xi])

        compute_ada(0)
        if not do_prompt:
            for l_ in range(1, n_layers):
                compute_ada(l_)
        if do_prompt and do_sample and own_all and n_layers == 2:
            run_pass(1, [0])
            run_pass(0)
            run_pass(1, [1])
        else:
            if do_prompt:
                run_pass(0)
            if do_sample:
                run_pass(1)
        fk = [k for k in P.dma_count if k.startswith("o_") or k.startswith("st_x")]
        print("n_ops", len(P.ops), "sbuf_free", nc.sbuf_bytes_remaining, flush=True)
        import os, json
        if os.environ.get("KTAGS"):
            json.dump({e: [o["tag"] for o in P.ops if o["eng"] == e for _ in range(o.get("ni", 1))] for e in ENGS}, open(os.environ["KTAGS"], "w"))
        P.emit(final_keys=fk)
    return nc


_NC = {}


def _consts():
    ident = np.eye(128, dtype=np.float32)
    tt = np.arange(128)
    triU = (tt[:, None] <= tt[None, :]).astype(np.float32)
    triL = (tt[:, None] >= tt[None, :]).astype(np.float32)
    pos = np.arange(1024)
    inv = (10000.0 ** (-np.arange(16, dtype=np.float32) / 16)).astype(np.float32)
    C = np.zeros((64, 1024), np.float32)
    S = np.zeros((64, 1024), np.float32)
    RT = np.zeros((64, 64), np.float32)
    for d in range(64):
        half = d // 32
        qd = (d % 32) % 16
        p = (pos // 64) if half == 0 else (pos % 64)
        ang = p.astype(np.float32) * inv[qd]
        C[d] = np.cos(ang)
        S[d] = np.sin(ang)
        if (d % 32) < 16:
            RT[d + 16, d] = -1.0
        else:
            RT[d - 16, d] = 1.0
    col = np.arange(64)
    cs = np.clip(col - 8, 0, 48)
    inwin = (col[None, :] >= cs[:, None]) & (col[None, :] < cs[:, None] + 16)
    m = np.where(inwin.T, 0.0, NEG).astype(np.float32)
    namask = np.concatenate([m, m], axis=0)
    RT2 = np.zeros((128, 128), np.float32)
    RT2[:64, :64] = RT
    RT2[64:, 64:] = RT
    return dict(ident=ident, triU=triU, triL=triL, ropeC=np.concatenate([C, C], 0), ropeS=np.concatenate([S, S], 0),
                ropeRT=RT2, namask=namask)


def kernel(**inp):
    f = lambda a: np.ascontiguousarray(np.asarray(a, dtype=np.float32))
    n_cores = 8
    cst = _consts()
    col = np.arange(64)
    dc = np.clip(col[:, None] - col[None, :], -15, 15) + 15
    rpb = f(inp["na_rpb"])
    rpbT = np.ascontiguousarray(rpb[:, :, ::-1, :][:, :, :, dc])
    shared = dict(
        norm_g=f(inp["norm_g"]), w_ada=f(inp["w_ada"]), b_ada=f(inp["b_ada"]), w_in=f(inp["w_in"]),
        lamv=f(np.stack([inp["lam_q1"], inp["lam_q2"], inp["lam_k1"], inp["lam_k2"]], axis=1)),
        subln_g=f(inp["diff_subln_g"]),
        convwT=f(np.asarray(inp["conv_w"]).reshape(2, 5, 6, 128).transpose(0, 3, 2, 1)),
        convbT=f(np.asarray(inp["conv_b"]).reshape(2, 6, 128).transpose(0, 2, 1)),
        dt_bias=f(np.asarray(inp["dt_bias"]).reshape(2, 16)), a_log=f(np.asarray(inp["a_log"]).reshape(2, 16)),
        d_skip=f(np.asarray(inp["d_skip"]).reshape(2, 16)), ssd_norm_g=f(inp["ssd_norm_g"]), rpbT=rpbT,
        w_br_a=f(inp["w_br_a"]), w_br_b=f(inp["w_br_b"]), w_br_c=f(inp["w_br_c"]), w_out=f(inp["w_out"]),
        final_g=f(inp["final_g"]), **cst)
    xp = f(inp["x_prompt"])
    xs = f(inp["x_sample"])
    own_tabs = []
    kc_i = np.arange(64)
    dcc = np.clip(kc_i[:, None] - kc_i[None, :], -15, 15) + 15
    cs = np.clip(kc_i - 8, 0, 48)
    inwin_T = ((kc_i[:, None] >= cs[None, :]) & (kc_i[:, None] < cs[None, :] + 16))
    for qb in range(4):
        selO = np.zeros((128, 16), np.float32)
        for j in range(2):
            selO[:, j * 8 + 2 * qb + j] = 1.0
        rpbO = np.zeros((2, 8, 128, 32, 64), np.float32)
        maskO = np.full((128, 32, 64), NEG, np.float32)
        for a in range(8):
            for j in range(4):
                r_ = 4 * qb + j
                rs_ = min(max(r_ - 4, 0), 8)
                for half in range(2):
                    kr = 2 * a + half
                    if rs_ <= kr <= rs_ + 7:
                        dd = kr - r_ + 7
                        rpbO[:, :, half * 64:(half + 1) * 64, a * 4 + j, :] = rpb[:, :, dd, :][:, :, dcc]
                        maskO[half * 64:(half + 1) * 64, a * 4 + j, :] = np.where(inwin_T, 0.0, NEG)
        own_tabs.append(dict(selO=selO, rpbO=rpbO, maskO=maskO,
                             ropeCo=np.ascontiguousarray(cst["ropeC"][:, qb * 256:(qb + 1) * 256]),
                             ropeSo=np.ascontiguousarray(cst["ropeS"][:, qb * 256:(qb + 1) * 256])))
    in_maps = []
    for c in range(n_cores):
        b = c // 4
        cv = np.stack([f(inp["c_ctx"]), f(inp["c"])[b]], axis=0)
        d = dict(shared)
        d.update(
            xp=np.ascontiguousarray(xp[4 * c:4 * c + 4].reshape(T, 1024)),
            xs=np.ascontiguousarray(xs[b]),
            xso=np.ascontiguousarray(xs[b, (c % 4) * 256:(c % 4 + 1) * 256]),
            cdk=f(inp["cache_diff_k"])[b].reshape(2, 512, 512), cdv=f(inp["cache_diff_v"])[b].reshape(2, 512, 512),
            cnk=f(inp["cache_na_k"])[b].reshape(2, 512, 512), cnv=f(inp["cache_na_v"])[b].reshape(2, 512, 512),
            sst=f(inp["state_ssd"])[b].reshape(2, 2, 512, 64),
            cvecT=np.ascontiguousarray(cv.reshape(2, 8, 128).transpose(0, 2, 1)),
            **own_tabs[c % 4],
        )
        in_maps.append({k: np.ascontiguousarray(v) for k, v in d.items()})
    if "nc" not in _NC:
        _NC["nc"] = build()
    import os
    ncr = int(os.environ.get("KCORES", "8"))
    res = run_bass_kernel_spmd(_NC["nc"], in_maps[:ncr], core_ids=list(range(ncr)))
    R = list(res.results) + [res.results[0]] * (n_cores - ncr)
    y_prompt = np.concatenate([R[c]["yp"].reshape(4, 256, 1024) for c in range(n_cores)], axis=0)
    if R[0]["ys"].shape[0] == T:
        y_sample = np.stack([R[0]["ys"], R[4]["ys"]], axis=0)
    else:
        y_sample = np.stack([np.concatenate([R[4 * b + q]["ys"] for q in range(4)], axis=0) for b in range(2)], axis=0)
    cat = lambda k, shp: np.concatenate([R[c][k].reshape((4,) + shp) for c in range(n_cores)], axis=0)
    return (y_prompt, y_sample,
            cat("ndk", (2, 256, 4, 128)), cat("ndv", (2, 256, 4, 128)),
            cat("nnk", (2, 256, 8, 64)), cat("nnv", (2, 256, 8, 64)),
            cat("nssd", (2, 2, 8, 64, 64)))
```

```python
import contextlib
import math
import numpy as np
import concourse.bass as bass
import concourse.mybir as mybir
from concourse.bass_utils import run_bass_kernel_spmd

F32 = mybir.dt.float32
BF16 = mybir.dt.bfloat16
AF = mybir.ActivationFunctionType
ALU = mybir.AluOpType
AX = mybir.AxisListType

ENGS = ("pe", "act", "dve", "pool", "sp")
EPS = 1e-6
NEG = -30000.0


class Prog:
    def __init__(self, nc):
        self.nc = nc
        self.ops = []
        self.last_w = {}
        self.readers = {}
        self.dma_last = {}
        self.dma_count = {}
        self.stack = contextlib.ExitStack()

    def sb(self, name, shape, dt):
        return self.stack.enter_context(self.nc.sbuf_tensor("sb_" + name, list(shape), dt))

    def ps(self, name, shape, dt=F32):
        return self.stack.enter_context(self.nc.psum_tensor("ps_" + name, list(shape), dt))

    limit = None
    tag = ""

    keymap = None

    def op(self, eng, fn, reads=(), writes=(), dma=None, inc=16):
        oid = len(self.ops)
        if self.limit is not None and oid >= self.limit:
            return None
        if self.keymap is not None:
            reads = [self.keymap(k) for k in reads]
            writes = [self.keymap(k) for k in writes]
        deps = set()
        for k in reads:
            if k in self.last_w:
                deps.add(self.last_w[k])
            if isinstance(k, str) and k.startswith("ps_"):
                for r in self.readers.get(k, ()):
                    if self.ops[r]["eng"] != eng:
                        deps.add(r)
        for k in writes:
            if k in self.last_w:
                deps.add(self.last_w[k])
            last = {}
            for r in self.readers.get(k, ()):
                ro = self.ops[r]
                if ro["dma"] is not None:
                    deps.add(r)
                else:
                    last[ro["eng"]] = r
            deps.update(last.values())
        if dma is not None and dma in self.dma_last:
            deps.add(self.dma_last[dma])
        deps.discard(oid)
        if eng == "pe":
            deps = {d for d in deps if self.ops[d]["eng"] != "pe"}
        o = dict(id=oid, eng=eng, fn=fn, deps=deps, dma=dma, marked=False, mark=None, tag=self.tag)
        if dma is not None:
            self.dma_count[dma] = self.dma_count.get(dma, 0) + inc
            o["dma_val"] = self.dma_count[dma]
            o["inc"] = inc
            self.dma_last[dma] = oid
        self.ops.append(o)
        for k in reads:
            self.readers.setdefault(k, []).append(oid)
        for k in writes:
            self.last_w[k] = oid
            self.readers[k] = []
        return oid

    def pe(self, fn, r=(), w=(), ni=1):
        oid = self.op("pe", fn, r, w)
        if oid is not None:
            self.ops[oid]["ni"] = ni
        return oid

    def act(self, fn, r=(), w=()):
        return self.op("act", fn, r, w)

    def dve(self, fn, r=(), w=()):
        return self.op("dve", fn, r, w)

    def pool(self, fn, r=(), w=()):
        return self.op("pool", fn, r, w)

    def dma(self, q, key, out, in_, r=(), w=()):
        return self.op(q, lambda e: e.dma_start(out=out, in_=in_), r, w, dma=key)

    def emit(self, final_keys=()):
        nc = self.nc
        ops = self.ops
        for o in ops:
            for d in o["deps"]:
                p = ops[d]
                if p["dma"] is None:
                    p["marked"] = True
        cnt = {e: 0 for e in ENGS}
        for o in ops:
            if o["dma"] is None and o["marked"]:
                cnt[o["eng"]] += 1
                o["mark"] = cnt[o["eng"]]
        esem = {e: self.stack.enter_context(nc.semaphore("s_" + e)) for e in ENGS if e != "sp"}
        dsem = {k: self.stack.enter_context(nc.semaphore("d_%d" % i))
                for i, k in enumerate(self.dma_count)}
        per_eng = {e: [o for o in ops if o["eng"] == e] for e in ENGS}
        engobj = {"pe": "tensor", "act": "scalar", "dve": "vector", "pool": "gpsimd", "sp": "sync"}

        def run(e, eng):
            waited = {}
            for o in per_eng[e]:
                need = {}
                for d in o["deps"]:
                    p = ops[d]
                    if p["dma"] is not None:
                        sk, v = ("d", p["dma"]), p["dma_val"]
                    else:
                        sk, v = ("e", p["eng"]), p["mark"]
                    if need.get(sk, 0) < v:
                        need[sk] = v
                for sk, v in need.items():
                    if waited.get(sk, 0) >= v:
                        continue
                    sem = dsem[sk[1]] if sk[0] == "d" else esem[sk[1]]
                    eng.wait_ge(sem, v)
                    waited[sk] = v
                ins = o["fn"](eng)
                if o["dma"] is not None:
                    ins.then_inc(dsem[o["dma"]], o["inc"])
                elif o["marked"]:
                    ins.then_inc(esem[e], 1)
            if e == "sp":
                for k in final_keys:
                    eng.wait_ge(dsem[k], self.dma_count[k])

        with nc.Block() as block:
            for e in ENGS:
                getattr(block, engobj[e])(lambda eng, e=e: run(e, eng))


def MM(out, lhsT, rhs, start=True, stop=True):
    return lambda e: e.matmul(out, lhsT=lhsT, rhs=rhs, start=start, stop=stop)


def TR(out, in_, ident):
    return lambda e: e.transpose(out, in_, ident)


def ACT(out, in_, func, **kw):
    return lambda e: e.activation(out=out, in_=in_, func=func, **kw)


def TT(out, in0, in1, op):
    return lambda e: e.tensor_tensor(out=out, in0=in0, in1=in1, op=op)


def TS(out, in0, s1, op0, s2=None, op1=None):
    if op1 is None:
        return lambda e: e.tensor_scalar(out=out, in0=in0, scalar1=s1, scalar2=None, op0=op0)
    return lambda e: e.tensor_scalar(out=out, in0=in0, scalar1=s1, scalar2=s2, op0=op0, op1=op1)


def STT(out, in0, scalar, in1, op0, op1):
    return lambda e: e.scalar_tensor_tensor(out=out, in0=in0, scalar=scalar, in1=in1, op0=op0, op1=op1)


def CP(out, in_):
    return lambda e: e.tensor_copy(out=out, in_=in_)


def bc(ap, axis, shape):
    return ap.unsqueeze(axis).to_broadcast(list(shape))


QA, KA, VA, GA = 0, 512, 1024, 1536
ZB, XB, DTB = 2048, 2560, 3328
QC, KC, VC, GC = 3344, 3856, 4368, 4880
MRG = 5392
IN_COLS = 8464
T = 1024
NT = 8


def build(n_layers=2, do_prompt=True, do_sample=True, stages="ABCP", limit=None, own_last=True, own_all=True):
    nc = bass.Bass("TRN2", target_bir_lowering=False)

    def din(name, shape):
        return nc.dram_tensor(name, list(shape), F32, kind="ExternalInput").ap()

    def dout(name, shape):
        return nc.dram_tensor(name, list(shape), F32, kind="ExternalOutput").ap()

    D = {}
    for name, shape in [
        ("xp", (T, 1024)), ("xs", (T, 1024)), ("xso", (256, 1024)),
        ("cdk", (2, 512, 512)), ("cdv", (2, 512, 512)), ("cnk", (2, 512, 512)), ("cnv", (2, 512, 512)),
        ("sst", (2, 2, 512, 64)),
        ("cvecT", (2, 128, 8)), ("norm_g", (2, 1024)), ("w_ada", (2, 1024, 3072)), ("b_ada", (2, 3072)),
        ("w_in", (2, 1024, IN_COLS)), ("lamv", (2, 4, 64)), ("subln_g", (2, 128)),
        ("convwT", (2, 128, 6, 5)), ("convbT", (2, 128, 6)), ("dt_bias", (2, 16)), ("a_log", (2, 16)),
        ("d_skip", (2, 16)), ("ssd_norm_g", (2, 512)), ("rpbT", (2, 8, 15, 64, 64)),
        ("w_br_a", (2, 512, 1024)), ("w_br_b", (2, 512, 1024)), ("w_br_c", (2, 512, 1024)),
        ("w_out", (2, 1024, 1024)), ("final_g", (1024,)),
        ("ident", (128, 128)), ("triU", (128, 128)), ("triL", (128, 128)),
        ("ropeC", (128, 1024)), ("ropeS", (128, 1024)), ("ropeRT", (128, 128)), ("namask", (128, 64)),
        ("selO", (128, 16)), ("ropeCo", (128, 256)), ("ropeSo", (128, 256)),
        ("rpbO", (2, 8, 128, 32, 64)), ("maskO", (128, 32, 64)),
    ]:
        D[name] = din(name, shape)
    O = {}
    for name, shape in [
        ("yp", (T, 1024)), ("ys", (256 if (own_all or (own_last and n_layers > 1)) else T, 1024)),
        ("ndk", (4, 2, 256, 512)), ("ndv", (4, 2, 256, 512)), ("nnk", (4, 2, 256, 512)), ("nnv", (4, 2, 256, 512)),
        ("nssd", (4, 2, 2, 512, 64)),
    ]:
        O[name] = dout(name, shape)
    xscr = nc.dram_tensor("xscr", [T, 1024], F32, kind="Internal").ap()
    ada_scr = nc.dram_tensor("ada_scr", [2, 2, 128, 3072], F32, kind="Internal").ap()
    xscr_own = nc.dram_tensor("xscr_own", [256, 1024], F32, kind="Internal").ap()
    xgath = nc.dram_tensor("xgath", [T, 1024], F32, kind="Internal").ap()

    P = Prog(nc)
    P.limit = limit
    with P.stack:
        hT = P.sb("hT", [128, 8, T], BF16)
        acc = P.sb("acc", [128, NT, 1024], F32)
        YT = P.sb("YT", [128, 4, T], BF16)
        big = P.sb("big", [128, 4096], F32)
        qT = big[:, 0:2048].bitcast(BF16).rearrange("p (j t) -> p j t", j=4)
        PTb = big[:, 2048:3584].bitcast(BF16).rearrange("p (k q) -> p k q", k=12)
        ysum = big[:, :].rearrange("p (c f) -> p c f", c=8)
        kT = P.sb("kT", [128, 8, T], BF16)
        vaug = P.sb("vaug", [128, NT, 528], BF16)
        gbuf = P.sb("gbuf", [128, NT, 512], BF16)
        wb = [P.sb("wb%d" % i, [128, 8, 512], BF16) for i in range(2)]
        ada = P.sb("ada", [128, 3072], F32)
        xt = [P.sb("xt%d" % i, [128, 1024], F32) for i in range(2)]
        htmp = P.sb("htmp", [128, 1024], F32)
        hbf = [P.sb("hbf%d" % i, [128, 1024], BF16) for i in range(1)] * 2
        stgT = P.sb("stg", [128, 2, 512], F32)
        stg = [stgT[:, i, :] for i in range(2)]
        natab_full = stgT[:, :, :].rearrange("p a b -> p (a b)").bitcast(BF16).rearrange("p (s q) -> p s q", s=32)
        sgmT = P.sb("sgm", [128, 2, 512], F32)
        sgm = [sgmT[:, i, :] for i in range(2)]
        natab_band = sgmT[:, :, :].rearrange("p a b -> p (a b)").bitcast(BF16).rearrange("p (s q) -> p s q", s=32)
        identb = P.sb("identb", [128, 128], BF16)
        identf = P.sb("identf", [128, 128], F32)
        tri = [P.sb("triU", [128, 128], F32), P.sb("triL", [128, 128], F32)]
        onesb = P.sb("onesb", [128, 128], BF16)
        onesf = P.sb("onesf", [128, 128], F32)
        sm = P.sb("sm", [128, 104], F32)
        cbcs = [P.sb("cbc%d" % i, [128, 8, 128], BF16) for i in range(2)]
        sqb = [P.sb("sqb%d" % i, [128, 512], BF16) for i in range(2)]
        Ytok = P.sb("Ytok", [128, 2, 512], BF16)
        otmp = [P.sb("otmp%d" % i, [128, 128], F32) for i in range(4)]
        nrm2 = P.sb("nrm2", [128, 2, 8, 4], F32)
        negm = P.sb("negm", [128, 8, 4], F32)
        sgv = P.sb("sgv", [128, 128], F32)
        lamt = P.sb("lamt", [128, 4, 64], F32)
        kTc = P.sb("kTc", [128, 4, 512], BF16)
        vaugc = P.sb("vaugc", [128, 4, 528], BF16)
        nrm2c = P.sb("nrm2c", [128, 8], F32)
        nrm2s = P.sb("nrm2s", [128, 2, 8], F32)
        ropeRT = P.sb("ropeRT", [128, 128], F32)
        namask = P.sb("namask", [128, 64], F32)
        hin = P.sb("hin", [128, 4, 64], F32)
        hTo = P.sb("hTo", [128, 8, 256], BF16)
        selO = P.sb("selO", [128, 16], F32)
        convw = P.sb("convw", [128, 6, 5], F32)
        convb = P.sb("convb", [128, 6], F32)
        dtb = P.sb("dtb", [128, NT, 16], F32)
        lab = P.sb("lab", [128, NT, 16], F32)
        dtbias = P.sb("dtbias", [128, 16], F32)
        nega = P.sb("nega", [128, 16], F32)
        dsum = P.sb("dsum", [128, 16], F32)
        ssdg = P.sb("ssdg", [128, 512], F32)
        btok = P.sb("btok", [128, NT, 128], BF16)
        Rbraw = P.sb("Rb", [128, 1040], F32)
        Rb = Rbraw[:, 0:1024].rearrange("p (h i) -> p h i", h=8)
        xpre = Rbraw[:, :].rearrange("p (s q) -> p s q", s=4)
        seg = P.sb("seg", [128, 8, 128], F32)
        cacc = seg[:, :, :].rearrange("p a b -> p (a b)").rearrange("p (s q) -> p s q", s=4)
        dec = P.sb("dec", [128, 8, 128], BF16)
        MTb = P.sb("MTb", [128, 8, 128], BF16)
        cbm = P.sb("cbm", [128, 2, 128], BF16)
        xdt = P.sb("xdt", [128, 512], BF16)
        wxdt = P.sb("wxdt", [128, 512], BF16)
        acum = P.sb("acum", [128, 32], F32)
        hst = P.sb("hst", [128, 256], F32)
        hst2 = P.sb("hst2", [128, 256], F32)
        hstb2 = P.sb("hstb2", [128, 256], BF16)
        acum2 = P.sb("acum2", [128, 32], F32)
        cbm2 = P.sb("cbm2", [128, 2, 128], BF16)
        hstb = P.sb("hstb", [128, 256], BF16)
        ysb = htmp[:, 0:512]
        ytmp = htmp[:, 512:1024]
        houts = [P.sb("hout%d" % i, [128, 128], F32) for i in range(2)]
        pj = [P.ps("ps_pj%d" % i, [128, 512]) for i in range(3)]
        st = [P.ps("ps_st%d" % i, [128, 512]) for i in range(2)]
        ob = [P.ps("ps_ob%d" % i, [128, 512]) for i in range(2)]
        tpb = P.ps("ps_tpb", [128, 8, 128], BF16)
        ALLB = [(pj[i], "ps_pj%d" % i) for i in range(3)] + [(st[i], "ps_st%d" % i) for i in range(2)] + \
               [(ob[i], "ps_ob%d" % i) for i in range(2)]
        cnt = {"allb": 0, "pj": 0, "st": 0, "ob": 0, "w": 0, "stg": 0, "sgm": 0, "xt": 0, "sqb": 0, "otmp": 0, "hout": 0}

        def nxt(name, n):
            i = cnt[name] % n
            cnt[name] += 1
            return i

        P.dma("pool", "c_id", identb[:], D["ident"], w=["identb"])
        P.dma("sp", "c_misc", identf[:], D["ident"], w=["identf"])
        P.dma("sp", "c_misc", tri[0][:], D["triU"], w=["triU"])
        P.dma("sp", "c_misc", tri[1][:], D["triL"], w=["triL"])
        P.dve(lambda e: e.memset(onesb[:], 1.0), w=["onesb"])
        P.dve(lambda e: e.memset(onesf[:], 1.0), w=["onesf"])
        P.dma("sp", "c_misc", ropeRT[:], D["ropeRT"], w=["ropeRT"])
        P.dma("sp", "c_misc", namask[:], D["namask"], w=["namask"])
        P.dma("sp", "c_misc", selO[:], D["selO"], w=["selO"])

        def wload(view, kc, ncols):
            s = nxt("w", 2)
            P.dma("pool", "wq%d" % s, wb[s][:, 0:kc, 0:ncols], view.rearrange("(k p) c -> p k c", p=128),
                  w=["wb%d" % s])
            return wb[s], "wb%d" % s

        def rstd_from_ss(ss_ap, out_ap, n, rk, wk):
            P.act(ACT(out_ap, ss_ap, AF.Ln, scale=1.0 / n, bias=EPS), r=[rk], w=[wk + "_l"])
            P.act(ACT(out_ap, out_ap, AF.Exp, scale=-0.5), r=[wk + "_l"], w=[wk])

        def compute_ada(li):
            P.tag = "ada%d" % li
            for kd in range(2):
                c0 = 64 + kd * 16
                P.dma("sp", "ld_sm", sm[:, c0:c0 + 8], D["cvecT"][kd], w=[("sm_c", kd)])
                P.act(ACT(sm[:, c0 + 8:c0 + 16], sm[:, c0:c0 + 8], AF.Silu), r=[("sm_c", kd)], w=[("sm_sc", kd)])
                P.dve(CP(cbcs[kd][:], bc(sm[:, c0 + 8:c0 + 16], 2, [128, 8, 128])), r=[("sm_sc", kd)], w=["cbc%d" % kd])
            for cb in range(6):
                wt, wk = wload(D["w_ada"][li][:, cb * 512:(cb + 1) * 512], 8, 512)
                for kd in range(2):
                    P.dma("sp", "ld_ba%d" % kd, sgm[kd], D["b_ada"][li][cb * 512:(cb + 1) * 512].partition_broadcast(128),
                          w=["sgm%d" % kd])
                    b = nxt("pj", 3)
                    for k in range(8):
                        P.pe(MM(pj[b][:, :], cbcs[kd][:, k, :], wt[:, k, :], k == 0, k == 7),
                             r=["cbc%d" % kd, wk], w=["ps_pj%d" % b])
                    P.dve(TT(sgm[kd], sgm[kd], pj[b][:, :], ALU.add), r=["ps_pj%d" % b, "sgm%d" % kd], w=["sgm%d" % kd])
                    P.dma("sp", "st_ada%d" % kd, ada_scr[kd, li][:, cb * 512:(cb + 1) * 512], sgm[kd],
                          r=["sgm%d" % kd], w=[("ada_scr", kd, li)])

        def run_pass(kind, layers=None):
            samp = kind == 1
            xin = D["xs"] if samp else D["xp"]
            yout = O["ys"] if samp else O["yp"]

            for li in (range(n_layers) if layers is None else layers):
                xsrc = xin if li == 0 else xscr
                Win = D["w_in"][li]
                gath = samp and own_all
                if gath and li > 0:
                    xsrc = xgath
                own = gath or (samp and own_last and li == n_layers - 1 and n_layers > 1)
                next_own = samp and own_last and li == n_layers - 2 and not gath
                hTq, NTq, hqk = (hTo, 2, "hTo") if own else (hT, NT, "hT")
                xsrc_q = (D["xso"] if (gath and li == 0) else xscr_own) if own else xsrc
                xkq = "xscr_own" if own else ("xgath" if (gath and li > 0) else "xscr")
                xka = "xgath" if (gath and li > 0) else "xscr"
                P.tag = "%d.%d.pre" % (kind, li)
                P.dma("sp", "ld_ada", ada[:], ada_scr[kind, li], r=[("ada_scr", kind, li)], w=["ada"])
                P.dma("sp", "ld_gs", htmp[:], D["norm_g"][li].partition_broadcast(128),
                      w=["htmp", "ysb", "ytmp", ("ytmp", 0), ("ytmp", 1)])
                shift = ada[:, 0:1024]
                scale = ada[:, 1024:2048]
                gate = ada[:, 2048:3072]
                gs = scale
                P.dve(STT(gs, scale, 1.0, htmp[:], ALU.add, ALU.mult), r=["ada", "htmp"], w=["ada", "gs"])
                pre_tiles = [(xsrc, t, (xka, t), hT, "hT") for t in range(NT)]
                if own:
                    pre_tiles += [(xsrc_q, t, ("xscr_own", t), hTo, "hTo") for t in range(2)]
                HTK = ["htmp", "ysb", "ytmp", ("ytmp", 0), ("ytmp", 1)]

                def pre_tile(xs_, t, xkey, hdst, hk, par):
                    xtb, xk = xt[par], "xt%d" % par
                    if par == 0:
                        ht_, htk, hb_, hbk, c0, tp_, tpk = htmp[:], HTK, hbf[0][:], ["hbf0"], 16, tpb, "ps_tpb"
                    else:
                        ht_ = sgmT[:, :, :].rearrange("p a b -> p (a b)")
                        htk = ["sgm0", "sgm1"]
                        hb_, hbk, c0 = stgT[:, 0, :].bitcast(BF16), ["stg0"], 18
                        tp_, tpk = ob[1][:, :].bitcast(BF16).rearrange("p (k q) -> p k q", k=8), "ps_ob1"
                    ssk, rsk = "sm_ss%d" % par, "sm_rs%d" % par
                    P.dma("sp", "ld_x%d" % par, xtb[:], xs_[t * 128:(t + 1) * 128, :], r=[xkey], w=[xk])
                    yield
                    P.act(ACT(ht_, xtb[:], AF.Square, accum_out=sm[:, c0:c0 + 1]), r=[xk], w=htk + [ssk])
                    yield
                    P.act(ACT(sm[:, c0 + 1:c0 + 2], sm[:, c0:c0 + 1], AF.Ln, scale=1.0 / 1024, bias=EPS), r=[ssk], w=[rsk + "_l"])
                    yield
                    P.act(ACT(sm[:, c0 + 1:c0 + 2], sm[:, c0 + 1:c0 + 2], AF.Exp, scale=-0.5), r=[rsk + "_l"], w=[rsk])
                    yield
                    P.dve(STT(ht_, xtb[:], sm[:, c0 + 1:c0 + 2], gs, ALU.mult, ALU.mult), r=[xk, rsk, "gs", "ada"], w=htk)
                    yield
                    P.dve(TT(hb_, ht_, shift, ALU.add), r=htk + ["ada"], w=hbk)
                    yield
                    for k in range(8):
                        P.pe(TR(tp_[:, k, :], hb_[:, k * 128:(k + 1) * 128], identb[:]), r=hbk + ["identb"], w=[tpk])
                    yield
                    P.act(ACT(hdst[:, :, t * 128:(t + 1) * 128], tp_[:, :, :], AF.Copy), r=[tpk], w=[(hk, t)])

                def zip2(gens):
                    gens = list(gens)
                    while gens:
                        for g_ in list(gens):
                            try:
                                next(g_)
                            except StopIteration:
                                gens.remove(g_)
                for i_ in range(0, len(pre_tiles), 2):
                    zip2([pre_tile(*pre_tiles[i_ + j_], j_) for j_ in range(2) if i_ + j_ < len(pre_tiles)])
                hT_all = [("hT", t) for t in range(NT)]

                def proj_tm(col0, ncols, evac, wt=None, wk=None, wcol=0, q=False):
                    if wt is None:
                        wt, wk = wload(Win[:, col0:col0 + ncols], 8, ncols)
                        wcol = 0
                    hsrc, nt_, hk = (hTq, NTq, hqk) if q else (hT, NT, "hT")
                    for t in range(nt_):
                        b = nxt("pj", 3)
                        for k in range(8):
                            P.pe(MM(pj[b][:, 0:ncols], hsrc[:, k, t * 128:(t + 1) * 128],
                                    wt[:, k, wcol:wcol + ncols], k == 0, k == 7),
                                 r=[(hk, t), wk], w=["ps_pj%d" % b])
                        evac(pj[b], "ps_pj%d" % b, t)
                    return wt, wk

                def proj_fm(wt, wk, wcol, m, evac, q=False, allb=False):
                    if q and own:
                        b = nxt("pj", 3)
                        for k in range(8):
                            P.pe(MM(pj[b][0:m, 0:256], wt[:, k, wcol:wcol + m], hTo[:, k, :], k == 0, k == 7),
                                 r=[("hTo", 0), ("hTo", 1), wk], w=["ps_pj%d" % b])
                        evac(pj[b], "ps_pj%d" % b, 0)
                        return
                    for tb in range(2):
                        pb_, kb_ = ALLB[nxt("allb", 7)] if allb else (lambda b_: (pj[b_], "ps_pj%d" % b_))(nxt("pj", 3))
                        for k in range(8):
                            P.pe(MM(pb_[0:m, :], wt[:, k, wcol:wcol + m], hT[:, k, tb * 512:(tb + 1) * 512],
                                    k == 0, k == 7),
                                 r=hT_all[tb * 4:(tb + 1) * 4] + [wk], w=[kb_])
                        evac(pb_, kb_, tb)

                def branch_merge(w_br, mcol, first):
                    P.tag = "%d.%d.merge" % (kind, li)
                    for cb in range(2):
                        wA, wAk = wload(w_br[:, cb * 512:(cb + 1) * 512], 4, 512)
                        wM, wMk = wload(Win[:, mcol + cb * 512: mcol + (cb + 1) * 512], 8, 512)
                        for t in range(NTq):
                            pA, kA_ = ALLB[nxt("allb", 7)]
                            for k in range(4):
                                P.pe(MM(pA[:, :], YT[:, k, t * 128:(t + 1) * 128], wA[:, k, :], k == 0, k == 3),
                                     r=[("YT", t), wAk], w=[kA_])
                            pL, kL_ = ALLB[nxt("allb", 7)]
                            for k in range(8):
                                P.pe(MM(pL[:, :], hTq[:, k, t * 128:(t + 1) * 128], wM[:, k, :], k == 0, k == 7),
                                     r=[(hqk, t), wMk], w=[kL_])
                            si = nxt("sgm", 2)
                            P.act(ACT(sgm[si], pL[:, :], AF.Sigmoid), r=[kL_], w=["sgm%d" % si])
                            asl = acc[:, t, cb * 512:(cb + 1) * 512]
                            if first:
                                P.dve(TT(asl, sgm[si], pA[:, :], ALU.mult),
                                      r=["sgm%d" % si, kA_], w=[("acc", t, cb)])
                            else:
                                P.dve(TT(sgm[si], sgm[si], pA[:, :], ALU.mult),
                                      r=["sgm%d" % si, kA_], w=["sgm%d" % si])
                                P.pool(TT(asl, asl, sgm[si], ALU.add),
                                       r=["sgm%d" % si, ("acc", t, cb)], w=[("acc", t, cb)])

                def ytok_to_YT(t, qt):
                    for c in range(4):
                        P.pe(TR(tpb[:, c, :], Ytok[:, qt, c * 128:(c + 1) * 128], identb[:]),
                             r=[("Ytok", qt), "identb"], w=["ps_tpb"])
                    P.act(ACT(YT[:, :, t * 128:(t + 1) * 128], tpb[:, 0:4, :], AF.Copy), r=["ps_tpb"], w=[("YT", t)])

                def attn_mixer(mx):
                    isA = mx == "A"
                    P.tag = "%d.%d.%s.proj" % (kind, li, mx)
                    qc, kc, vc, gc = (QA, KA, VA, GA) if isA else (QC, KC, VC, GC)
                    nh = 4 if isA else 8
                    e = 128 if isA else 64
                    nm = 2 if isA else 1
                    okey, vkey = ("ndk", "ndv") if isA else ("nnk", "nnv")
                    va = vaug[:, :, 0:nh * (e + 2)].rearrange("p t (h e) -> p t h e", h=nh)
                    P.dve(lambda en: en.memset(va[:, :, :, e:e + 1], 1.0), w=[("vaug", t) for t in range(NT)])

                    ropeCo, ropeSo = acc[:, 4, 0:256], acc[:, 4, 256:512]
                    maskO_bf = acc[:, 5, :].bitcast(BF16).rearrange("p (s q) -> p s q", s=32)
                    ropeC = Rbraw[:, 0:1024]
                    ropeS = seg[:, :, :].rearrange("p a b -> p (a b)")
                    SEGK = [("seg", 0), ("seg", 1)]
                    if samp:
                        vca = vaugc[:, :, 0:nh * (e + 2)].rearrange("p t (h e) -> p t h e", h=nh)
                        ck = D["cdk" if isA else "cnk"][li]
                        cv = D["cdv" if isA else "cnv"][li]
                        P.dma("pool", "ld_ck", gbuf[:, 0:4, :], ck.rearrange("(t p) c -> p t c", p=128),
                              w=[("gbuf", t) for t in range(4)])
                        for tl in range(4):
                            P.dma("pool", "ld_cv", vca[:, tl, :, 0:e],
                                  cv[tl * 128:(tl + 1) * 128, :].rearrange("p (h e) -> p h e", h=nh), w=["vaugc"])
                        P.dve(lambda en: en.memset(vca[:, :, :, e:e + 1], 1.0), w=["vaugc1"])
                        for tl in range(4):
                            for pr in range(4):
                                P.pe(TR(tpb[:, pr, :], gbuf[:, tl, pr * 128:(pr + 1) * 128], identb[:]),
                                     r=[("gbuf", tl), "identb"], w=["ps_tpb"])
                            P.act(ACT(kTc[:, :, tl * 128:(tl + 1) * 128], tpb[:, 0:4, :], AF.Copy), r=["ps_tpb"],
                                  w=[("kTc", tl, 0), ("kTc", tl, 1)])
                        for hm in range(8):
                            pb = (hm % 2) * 64
                            si = nxt("sqb", 2)
                            P.act(ACT(sqb[si][0:64, :], kTc[pb:pb + 64, (hm // 2), :], AF.Square),
                                  r=[("kTc", tl, (hm % 2)) for tl in range(4)], w=["sqb%d" % si])
                            b = nxt("st", 2)
                            P.pe(MM(st[b][:, :], onesb[0:64, :], sqb[si][0:64, :]), r=["sqb%d" % si, "onesb"], w=["ps_st%d" % b])
                            P.dve(lambda en, b=b, hm=hm: en.tensor_reduce(out=nrm2c[:, hm:hm + 1], in_=st[b][:, :], axis=AX.X,
                                                                          op=ALU.max), r=["ps_st%d" % b], w=[("nrm2c", hm)])
                        if isA:
                            P.dma("sp", "ld_rope", ropeC, D["ropeC"], w=["Rb"])
                            P.dma("sp", "ld_rope", ropeS, D["ropeS"], w=SEGK)
                            if own:
                                P.dma("sp", "ld_rope", acc[:, 4, 0:256], D["ropeCo"], w=[("acc", 4, 0)])
                                P.dma("sp", "ld_rope", acc[:, 4, 256:512], D["ropeSo"], w=[("acc", 4, 0)])
                        elif own:
                            P.dma("pool", "ld_mo", maskO_bf, D["maskO"], w=[("acc", 5, 0), ("acc", 5, 1)])
                    pend = []
                    for which, c0, dst in ((0, qc, qT), (1, kc, kT)):
                        wt, wk = wload(Win[:, c0:c0 + 512], 8, 512)
                        for pr in range(4):
                            def ev(ps, pk, tb, pr=pr, which=which, dst=dst):
                                qo = own and which == 0
                                W = 256 if qo else 512
                                dsl = dst[:, pr, tb * 512:tb * 512 + W]
                                wkeys = [("qkT", which, pr, tb, 0), ("qkT", which, pr, tb, 1)]
                                if samp and isA:
                                    qi = nxt("sgm", 2)
                                    qf, qk_ = sgm[qi][:, 0:W], "sgm%d" % qi
                                    rc = ropeCo if qo else ropeC[:, tb * 512:(tb + 1) * 512]
                                    rs = ropeSo if qo else ropeS[:, tb * 512:(tb + 1) * 512]
                                    rk = [("acc", 4, 0)] if qo else ["Rb"] + SEGK
                                    P.act(ACT(qf, ps[:, 0:W], AF.Copy), r=[pk], w=[qk_])

                                    def rope_part(qf=qf, qk_=qk_, rc=rc, rs=rs, rk=rk, W=W, dsl=dsl, wkeys=wkeys):
                                        br = nxt("st", 2)
                                        P.pe(MM(st[br][:, 0:W], ropeRT[:], qf), r=[qk_, "ropeRT"], w=["ps_st%d" % br],
                                             ni=2 if W == 512 else 1)
                                        P.dve(TT(qf, qf, rc, ALU.mult), r=[qk_] + rk, w=[qk_])
                                        P.dve(TT(ysb[:, 0:W], st[br][:, 0:W], rs, ALU.mult), r=["ps_st%d" % br] + rk, w=["ysb"])
                                        P.dve(TT(dsl, qf, ysb[:, 0:W], ALU.add), r=[qk_, "ysb"], w=wkeys)
                                    pend.append(rope_part)
                                else:
                                    P.act(ACT(dsl, ps[:, 0:W], AF.Copy), r=[pk], w=wkeys)
                                si = nxt("sqb", 2)
                                P.act(ACT(sqb[si][:, 0:W], ps[:, 0:W], AF.Square), r=[pk], w=["sqb%d" % si])
                                for half in range(2):
                                    def norm_part(si=si, tb=tb, half=half, hm=2 * pr + half, W=W):
                                        b = nxt("st", 2)
                                        P.pe(MM(st[b][:, 0:W], onesb[half * 64:(half + 1) * 64, :], sqb[si][half * 64:(half + 1) * 64, 0:W]),
                                             r=["sqb%d" % si, "onesb"], w=["ps_st%d" % b])
                                        ns = W // 256
                                        P.dve(lambda en, b=b: en.tensor_reduce(
                                            out=nrm2[:, which, hm, tb * 2:tb * 2 + ns],
                                            in_=st[b][:, 0:W].rearrange("p (s q) -> p s q", s=ns), axis=AX.X, op=ALU.max),
                                            r=["ps_st%d" % b], w=[("nrm2", which, hm, tb)])
                                    pend.append(norm_part)
                                while len(pend) > (3 if (samp and isA) else 2):
                                    pend.pop(0)()
                            proj_fm(wt, wk, pr * 128, 128, ev, q=(which == 0))
                        while pend:
                            pend.pop(0)()
                        if which == 1 and not samp:
                            def evk(ps, pk, t):
                                si = nxt("stg", 2)
                                P.act(ACT(stg[si], ps[:, :], AF.Copy), r=[pk], w=["stg%d" % si])
                                P.dma("sp", "o_%s%d" % (okey, si), O[okey][t // 2, li, (t % 2) * 128:(t % 2 + 1) * 128, :],
                                      stg[si], r=["stg%d" % si])
                            proj_tm(c0, 512, evk, wt, wk, 0)
                    def evv(ps, pk, t):
                        if not samp:
                            si = nxt("stg", 2)
                            P.act(ACT(stg[si], ps[:, :], AF.Copy), r=[pk], w=["stg%d" % si])
                            P.dma("sp", "o_%s%d" % (vkey, si), O[vkey][t // 2, li, (t % 2) * 128:(t % 2 + 1) * 128, :],
                                  stg[si], r=["stg%d" % si])
                        P.dve(CP(va[:, t, :, 0:e], ps[:, :].rearrange("p (h e) -> p h e", h=nh)),
                              r=[pk], w=[("vaug", t)])
                    proj_tm(vc, 512, evv)
                    def evg(ps, pk, t):
                        P.act(ACT(gbuf[:, t, :], ps[:, :], AF.Silu), r=[pk], w=[("gbuf", t)])
                    proj_tm(gc, 512, evg, q=True)
                    nr = [("nrm2", w_, hm, tb) for w_ in range(2) for hm in range(8) for tb in range(2)]
                    if own:
                        nr = [("nrm2", 0, hm, 0) for hm in range(8)] + [("nrm2", 1, hm, tb) for hm in range(8) for tb in range(2)]
                        P.dve(CP(nrm2s[:, 0, :], nrm2[:, 0, :, 0]), r=nr, w=["nrm2s"])
                        P.dve(lambda en: en.tensor_reduce(out=nrm2s[:, 1, :], in_=nrm2[:, 1, :, :], axis=AX.X, op=ALU.max),
                              r=nr + ["nrm2s"], w=["nrm2s"])
                    elif samp:
                        P.dve(lambda en: en.tensor_reduce(out=nrm2s[:], in_=nrm2[:, :, :, :], axis=AX.X, op=ALU.max),
                              r=nr, w=["nrm2s"])
                    if samp:
                        P.dve(TT(nrm2s[:, 1, :], nrm2s[:, 1, :], nrm2c[:], ALU.max),
                              r=["nrm2s"] + [("nrm2c", hm) for hm in range(8)], w=["nrm2s"])
                        nmv = negm[:, :, 0]
                        P.dve(TT(nmv, nrm2s[:, 0, :], nrm2s[:, 1, :], ALU.mult), r=["nrm2s"], w=["negm0"])
                    else:
                        nmv = negm[:]
                        P.dve(TT(nmv, nrm2[:, 0, :, :], nrm2[:, 1, :, :], ALU.mult), r=nr, w=["negm0"])
                    P.act(ACT(nmv, nmv, AF.Sqrt), r=["negm0"], w=["negm1"])
                    P.dve(TS(nmv, nmv, -0.125, ALU.mult), r=["negm1"], w=["negm"])
                    if isA:
                        lam_init = 0.8 - 0.6 * math.exp(-0.3 * li)
                        P.dma("sp", "ld_lam", lamt[:], D["lamv"][li].partition_broadcast(128), w=["lamt"])
                        P.dve(TT(lamt[:, 0:2, :], lamt[:, 0:2, :], lamt[:, 2:4, :], ALU.mult), r=["lamt"], w=["lamt2"])
                        P.dve(lambda en: en.tensor_reduce(out=sm[:, 20:22], in_=lamt[:, 0:2, :], axis=AX.X, op=ALU.add),
                              r=["lamt2"], w=["sm_l0"])
                        P.act(ACT(sm[:, 22:24], sm[:, 20:22], AF.Exp), r=["sm_l0"], w=["sm_l1"])
                        P.dve(STT(sm[:, 24:25], sm[:, 23:24], -lam_init, sm[:, 22:23], ALU.add, ALU.subtract),
                              r=["sm_l1"], w=["sm_nl"])
                        P.dma("sp", "ld_sg", sgv[:], D["subln_g"][li].partition_broadcast(128), w=["sgv0"])
                        P.dve(TS(sgv[:], sgv[:], 1.0 - lam_init, ALU.mult), r=["sgv0"], w=["sgv"])

                    if isA and kind == 0 and li == 0:
                        for l_ in range(1, n_layers):
                            compute_ada(l_)
                    P.tag = "%d.%d.%s.attn" % (kind, li, mx)

                    def o_post(ov, obk, t, h, ysl, gsl, gkey, ykey, sb=32):
                        K = lambda n: "%s_%d" % (n, sb)
                        P.dve(lambda en, ov=ov: en.reciprocal(out=sm[:, sb:sb + nm], in_=ov[:, :, e]),
                              r=[obk], w=[K("sm_rl")])
                        oi = nxt("otmp", 4)
                        yield
                        if isA:
                            P.dve(TT(sm[:, sb + 2:sb + 3], sm[:, sb + 1:sb + 2], sm[:, 24:25], ALU.mult),
                                  r=[K("sm_rl"), "sm_nl"], w=[K("sm_c1")])
                            P.act(ACT(otmp[oi][:], ov[:, 0, 0:e], AF.Copy, scale=sm[:, sb:sb + 1]),
                                  r=[obk, K("sm_rl")], w=["otmp%d" % oi])
                            yield
                            P.dve(STT(otmp[oi][:], ov[:, 1, 0:e], sm[:, sb + 2:sb + 3], otmp[oi][:], ALU.mult, ALU.add),
                                  r=[obk, K("sm_c1"), "otmp%d" % oi], w=["otmp%d" % oi])
                            oj = nxt("otmp", 4)
                            yield
                            P.act(ACT(otmp[oj][:], otmp[oi][:], AF.Square, accum_out=sm[:, sb + 4:sb + 5]),
                                  r=["otmp%d" % oi], w=["otmp%d" % oj, K("sm_os")])
                            yield
                            P.act(ACT(sm[:, sb + 5:sb + 6], sm[:, sb + 4:sb + 5], AF.Ln, scale=1.0 / 128, bias=EPS),
                                  r=[K("sm_os")], w=[K("sm_or") + "_l"])
                            yield
                            P.act(ACT(sm[:, sb + 5:sb + 6], sm[:, sb + 5:sb + 6], AF.Exp, scale=-0.5),
                                  r=[K("sm_or") + "_l"], w=[K("sm_or")])
                            yield
                            P.dve(STT(otmp[oi][:], otmp[oi][:], sm[:, sb + 5:sb + 6], sgv[:], ALU.mult, ALU.mult),
                                  r=["otmp%d" % oi, K("sm_or"), "sgv"], w=["otmp%d" % oi])
                            yield
                            P.dve(TT(ysl, otmp[oi][:], gsl, ALU.mult), r=["otmp%d" % oi, gkey], w=[ykey])
                        else:
                            P.act(ACT(otmp[oi][:, 0:e], ov[:, 0, 0:e], AF.Copy, scale=sm[:, sb:sb + 1]),
                                  r=[obk, K("sm_rl")], w=["otmp%d" % oi])
                            yield
                            P.dve(TT(ysl, otmp[oi][:, 0:e], gsl, ALU.mult), r=["otmp%d" % oi, gkey], w=[ykey])

                    def zipg(gens):
                        gens = list(gens)
                        while gens:
                            for g_ in list(gens):
                                try:
                                    next(g_)
                                except StopIteration:
                                    gens.remove(g_)

                    if samp:
                        NAT = {0: [0, 1, 2, 3], 1: [0, 1, 2, 3, 4, 5], 2: [2, 3, 4, 5, 6, 7], 3: [4, 5, 6, 7]}
                        if not isA and not own:
                            P.dve(lambda en: en.memset(natab_full[:, :, :], NEG), w=["stg0", "stg1"])
                            P.dve(lambda en: en.memset(natab_band[:, :, :], NEG), w=["sgm0", "sgm1"])
                        for h in range(nh):
                            if not isA and own:
                                P.dma("pool", "ld_rpb", natab_full[:, :, :], D["rpbO"][li, h], w=["stg0", "stg1"])
                                P.dve(STT(natab_full[:, :, :], natab_full[:, :, :], 8.0, maskO_bf, ALU.mult, ALU.add),
                                      r=["stg0", "stg1", ("acc", 5, 0), ("acc", 5, 1)], w=["stg0", "stg1"])
                            elif not isA:
                                rp = D["rpbT"][li, h]
                                for tab, tkeys, lo0, dd0, n in ((natab_full, ["stg0", "stg1"], 8, 0, 15),
                                                                (natab_band, ["sgm0", "sgm1"], 12, 4, 8)):
                                    for half in range(2):
                                        sl0 = lo0 + half
                                        P.dma("pool", "ld_rpb", tab[half * 64:(half + 1) * 64, sl0:sl0 + n, :],
                                              rp[dd0:dd0 + n].rearrange("d k q -> k d q"), w=tkeys)
                                        tsl = tab[half * 64:(half + 1) * 64, sl0:sl0 + n, :]
                                        P.dve(STT(tsl, tsl, 8.0, bc(namask[half * 64:(half + 1) * 64, :], 1, [64, n, 64]),
                                                  ALU.mult, ALU.add), r=tkeys + ["namask"], w=tkeys)
                            for qb in ([0] if own else range(4)):
                                if isA or own:
                                    tiles = [("l", kt) for kt in range(8)] + [("c", kt) for kt in range(4)]
                                else:
                                    tiles = [("l", a) for a in NAT[qb]] + [("c", kt) for kt in range(4)]
                                nt_ = len(tiles)
                                for m in range(nm):
                                    hm = nm * h + m
                                    pb = (hm % 2) * 64
                                    qv = qT[pb:pb + 64, (hm // 2), qb * 256:(qb + 1) * 256]
                                    pvq = []
                                    for i, (kd, kt) in enumerate(tiles):
                                        b = nxt("st", 2)
                                        bias = (not isA) and kd == "l"
                                        if kd == "l":
                                            ksl = kT[pb:pb + 64, (hm // 2), kt * 128:(kt + 1) * 128]
                                            kr = [("qkT", 1, (hm // 2), kt // 4, (hm % 2))]
                                        else:
                                            ksl = kTc[pb:pb + 64, (hm // 2), kt * 128:(kt + 1) * 128]
                                            kr = [("kTc", kt, (hm % 2))]
                                        P.pe(MM(st[b][:, 0:256], ksl, qv, True, not bias),
                                             r=kr + [("qkT", 0, (hm // 2), qb // 2, (hm % 2))], w=["ps_st%d" % b])
                                        if bias:
                                            tab, tkeys = (natab_full, ["stg0", "stg1"]) if (qb in (0, 3) or own) else (natab_band, ["sgm0", "sgm1"])
                                            s0 = 4 * kt if own else 15 - 2 * kt + 4 * qb
                                            P.pe(MM(st[b][:, 0:256], identb[:],
                                                    tab[:, s0:s0 + 4, :].rearrange("p s q -> p (s q)"), False, True),
                                                 r=tkeys + ["identb"], w=["ps_st%d" % b])
                                        P.act(ACT(PTb[:, i, :], st[b][:, 0:256], AF.Exp, scale=0.125, bias=negm[:, hm, 0:1]),
                                              r=["ps_st%d" % b, "negm"], w=[("PT", i)])

                                        def pv(i=i, kd=kd, kt=kt, m=m):
                                            for qt in range(2):
                                                ov = ob[qt][:, 0:nm * (e + 1)].rearrange("p (m e) -> p m e", m=nm)
                                                if kd == "l":
                                                    vsl, vr = va[:, kt, h, 0:e + 1], [("vaug", kt)]
                                                else:
                                                    vsl, vr = vca[:, kt, h, 0:e + 1], ["vaugc", "vaugc1"]
                                                P.pe(MM(ov[:, m, :], PTb[:, i, qt * 128:(qt + 1) * 128], vsl, i == 0, i == nt_ - 1),
                                                     r=[("PT", i)] + vr, w=["ps_ob%d" % qt])
                                        pvq.append(pv)
                                        while len(pvq) > 2:
                                            pvq.pop(0)()
                                    while pvq:
                                        pvq.pop(0)()
                                posts = []
                                for qt in range(2):
                                    t = 2 * qb + qt
                                    ov = ob[qt][:, 0:nm * (e + 1)].rearrange("p (m e) -> p m e", m=nm)
                                    gsl = gbuf[:, t, h * e:(h + 1) * e]
                                    posts.append(o_post(ov, "ps_ob%d" % qt, t, h, gsl, gsl, ("gbuf", t), ("gbuf", t), sb=32 if qt == 0 else 96))
                                zipg(posts)
                        for t in range(NTq):
                            for c in range(4):
                                P.pe(TR(tpb[:, c, :], gbuf[:, t, c * 128:(c + 1) * 128], identb[:]),
                                     r=[("gbuf", t), "identb"], w=["ps_tpb"])
                            P.act(ACT(YT[:, :, t * 128:(t + 1) * 128], tpb[:, 0:4, :], AF.Copy), r=["ps_tpb"], w=[("YT", t)])

                    def p_stage1(s, h, sb):
                        for m in range(nm):
                            hm = nm * h + m
                            pb = (hm % 2) * 64
                            qv = qT[pb:pb + 64, (hm // 2), s * 256:(s + 1) * 256]
                            for kt in range(2):
                                b = nxt("st", 2)
                                P.pe(MM(st[b][:, 0:256], kT[pb:pb + 64, (hm // 2), s * 256 + kt * 128: s * 256 + (kt + 1) * 128], qv),
                                     r=[("qkT", 0, (hm // 2), s // 2, (hm % 2)), ("qkT", 1, (hm // 2), s // 2, (hm % 2))], w=["ps_st%d" % b])
                                P.act(ACT(PTb[:, sb + m * 2 + kt, :], st[b][:, 0:256], AF.Exp, scale=0.125,
                                          bias=negm[:, hm, s:s + 1]),
                                      r=["ps_st%d" % b, "negm"], w=[("PT", sb + m * 2 + kt)])

                    def p_stage2(s, h, sb):
                        posts = []
                        for qt in range(2):
                            t = 2 * s + qt
                            b = nxt("ob", 2)
                            ov = ob[b][:, 0:nm * (e + 1)].rearrange("p (m e) -> p m e", m=nm)
                            for m in range(nm):
                                for kt in range(2):
                                    P.pe(MM(ov[:, m, :], PTb[:, sb + m * 2 + kt, qt * 128:(qt + 1) * 128],
                                            va[:, 2 * s + kt, h, 0:e + 1], kt == 0, kt == 1),
                                         r=[("PT", sb + m * 2 + kt), ("vaug", 2 * s + kt)], w=["ps_ob%d" % b])
                            posts.append(o_post(ov, "ps_ob%d" % b, t, h, Ytok[:, qt, h * e:(h + 1) * e], gbuf[:, t, h * e:(h + 1) * e],
                                                ("gbuf", t), ("Ytok", qt), sb=32 if qt == 0 else 96))
                        zipg(posts)

                    hcnt = 0
                    for s in range(4 if not samp else 0):
                        pending = None
                        for h in range(nh):
                            sb = (hcnt % 2) * 4
                            hcnt += 1
                            p_stage1(s, h, sb)
                            if pending is not None:
                                p_stage2(*pending)
                            pending = (s, h, sb)
                        p_stage2(*pending)
                        for qt in range(2):
                            ytok_to_YT(2 * s + qt, qt)

                    branch_merge(D["w_br_a"][li] if isA else D["w_br_c"][li], MRG + (0 if isA else 2048), isA)

                def ssd_mixer():
                    xbcT = kT
                    P.tag = "%d.%d.B.proj" % (kind, li)
                    SEGK2 = [("seg", 0), ("seg", 1)]
                    xtok = vaug[:, :, 0:512]
                    P.dma("sp", "ld_cw", convw[:], D["convwT"][li], w=["convw"])
                    P.dma("sp", "ld_cw", convb[:], D["convbT"][li], w=["convb"])
                    P.dma("sp", "ld_cw", dtbias[:], D["dt_bias"][li].partition_broadcast(128), w=["dtbias"])
                    P.dma("sp", "ld_cw", nega[:], D["a_log"][li].partition_broadcast(128), w=["nega0"])
                    P.dma("sp", "ld_cw", dsum[:], D["d_skip"][li].partition_broadcast(128), w=["dsum0"])
                    P.dma("sp", "ld_cw", ssdg[:], D["ssd_norm_g"][li].partition_broadcast(128), w=["ssdg"])
                    P.act(ACT(nega[:], nega[:], AF.Exp), r=["nega0"], w=["nega1"])
                    P.dve(TS(nega[:], nega[:], -1.0, ALU.mult), r=["nega1"], w=["nega"])
                    P.dve(TT(dsum[:, 0:8], dsum[:, 0:8], dsum[:, 8:16], ALU.add), r=["dsum0"], w=["dsum"])
                    P.dve(lambda en: en.memset(xpre[:], 0.0), w=["Rb", ("Rbh", 0), ("Rbh", 1)])
                    for blk, (c0, ncol) in enumerate(((XB, 512), (XB + 512, 256))):
                        wt, wk = wload(Win[:, c0:c0 + ncol], 8, ncol)
                        for cc in range(ncol // 128):
                            c = blk * 4 + cc

                            def evx_s(ps, pk, tb, c=c):
                                xps = Rbraw[:, 0:1028]
                                caf = seg[:, :, :].rearrange("p a b -> p (a b)")
                                P.act(ACT(xps[:, 2 + tb * 512:2 + (tb + 1) * 512], ps[:, :], AF.Copy), r=[pk], w=["Rb"])
                                if tb == 0:
                                    return
                                P.dve(TS(caf, xps[:, 0:1024], convw[:, c, 0:1], ALU.mult), r=["Rb", "convw"], w=SEGK2)
                                for j in range(1, 5):
                                    P.dve(STT(caf, xps[:, j:j + 1024], convw[:, c, j:j + 1], caf, ALU.mult, ALU.add),
                                          r=["Rb", "convw"] + SEGK2, w=SEGK2)
                                P.act(ACT(xbcT[:, c, :], caf, AF.Silu, bias=convb[:, c:c + 1]), r=SEGK2 + ["convb"],
                                      w=[("qkT", 1, c, tb_, hf) for tb_ in range(2) for hf in range(2)])

                            def evx(ps, pk, tb, c=c):
                                P.act(ACT(xpre[:, 2 * tb:2 * tb + 2, 2:258], ps[:, :].rearrange("p (s q) -> p s q", s=2),
                                          AF.Copy), r=[pk, "Rb"], w=[("Rbh", tb)])
                                sl = slice(2 * tb, 2 * tb + 2)
                                P.dve(TS(cacc[:, sl, :], xpre[:, sl, 0:256], convw[:, c, 0:1], ALU.mult),
                                      r=["Rb", ("Rbh", tb), "convw"], w=[("seg", tb)])
                                for j in range(1, 5):
                                    P.dve(STT(cacc[:, sl, :], xpre[:, sl, j:j + 256], convw[:, c, j:j + 1], cacc[:, sl, :],
                                              ALU.mult, ALU.add), r=["Rb", ("Rbh", tb), "convw", ("seg", tb)], w=[("seg", tb)])
                                P.act(ACT(xbcT[:, c, tb * 512:(tb + 1) * 512].rearrange("p (s q) -> p s q", s=2),
                                          cacc[:, sl, :], AF.Silu, bias=convb[:, c:c + 1]),
                                      r=[("seg", tb), "convb"], w=[("qkT", 1, c, tb, 0), ("qkT", 1, c, tb, 1)])
                            proj_fm(wt, wk, cc * 128, 128, evx_s if samp else evx, allb=True)
                    wt_dt, wk_dt = wload(Win[:, DTB:DTB + 16], 8, 16)
                    bdt = nxt("pj", 3)
                    for t in range(NT):
                        for k in range(8):
                            P.pe(MM(pj[bdt][:, t * 16:(t + 1) * 16], hT[:, k, t * 128:(t + 1) * 128], wt_dt[:, k, 0:16], k == 0, k == 7),
                                 r=[("hT", t), wk_dt], w=["ps_pj%d" % bdt])
                    DT0 = [("dt0", t) for t in range(NT)]
                    DT1 = [("dt1", t) for t in range(NT)]
                    DTK = [("dt", t) for t in range(NT)]
                    P.dve(TT(dtb[:, :, :], pj[bdt][:, 0:128].rearrange("p (t c) -> p t c", t=NT), bc(dtbias[:], 1, [128, NT, 16]), ALU.add),
                          r=["ps_pj%d" % bdt, "dtbias"], w=DT0)
                    P.act(ACT(dtb[:, :, :], dtb[:, :, :], AF.Exp), r=DT0, w=DT1)
                    P.act(ACT(dtb[:, :, :], dtb[:, :, :], AF.Ln, bias=1.0), r=DT1, w=DTK)
                    P.dve(TT(lab[:, :, :], dtb[:, :, :], bc(nega[:], 1, [128, NT, 16]), ALU.mult), r=DTK + ["nega"],
                          w=[("la", t) for t in range(NT)])
                    def evz(ps, pk, t):
                        P.act(ACT(gbuf[:, t, :], ps[:, :], AF.Silu), r=[pk], w=[("gbuf", t)])
                    proj_tm(ZB, 512, evz, q=True)
                    for t in range(NT):
                        for c in range(5):
                            P.pe(TR(tpb[:, c, :], xbcT[:, c, t * 128:(t + 1) * 128], identb[:]),
                                 r=[("qkT", 1, c, t // 4, 0), ("qkT", 1, c, t // 4, 1), "identb"], w=["ps_tpb"])
                        P.act(ACT(xtok[:, t, :], tpb[:, 0:4, :], AF.Copy), r=["ps_tpb"], w=[("vaug", t)])
                        P.act(ACT(btok[:, t, :], tpb[:, 4, :], AF.Copy), r=["ps_tpb"], w=[("btok", t)])

                    P.tag = "%d.%d.B.loop" % (kind, li)
                    nseq = 1 if samp else 4
                    nys = 8 if samp else 2
                    nch = NT // nseq
                    D1MAP = {"Rb": "xt0", ("seg", 0): "xt1", ("seg", 1): "xt1", "dec": "stg0", ("MT", 0): "stg1", ("MT", 1): "stg1",
                             "xdt": "hbf0", "wxdt": "hbf0", "ysb": "sgm0", ("ytmp", 0): "sgm1", ("ytmp", 1): "sgm1"}
                    D1OWN = {"acum", "eacum", "wexp0", "wexp", "cdec", ("cbm", 0), ("cbm", 1), ("hst", 0), ("hst", 1), "hstb"}

                    def km1(k):
                        if k in D1MAP:
                            return D1MAP[k]
                        if k in D1OWN:
                            return (k, "d1")
                        return k
                    BUF = {0: (acum, Rb, seg, dec, cbm, MTb, xdt, wxdt, hst, hstb, ysb, ytmp),
                           1: (acum2, xt[0][:, :].rearrange("p (h i) -> p h i", h=8), xt[1][:, :].rearrange("p (h i) -> p h i", h=8),
                               stgT[:, 0, :].bitcast(BF16).rearrange("p (h i) -> p h i", h=8),
                               cbm2, stgT[:, 1, :].bitcast(BF16).rearrange("p (h i) -> p h i", h=8),
                               hbf[0][:, 0:512], hbf[0][:, 512:1024], hst2, hstb2, sgm[0], sgm[1])}

                    pcnt = {("s", 0): 0, ("s", 1): 0, ("o", 0): 0, ("o", 1): 0}

                    def chunk_step(s, d, c, hstate, ysw):
                        acum, Rb, seg, dec, cbm, MTb, xdt, wxdt, hst, hstb, ysb, ytmp = BUF[d]
                        cd0 = 40 + 16 * d
                        STL = [(st[0], "ps_st0"), (st[1], "ps_st1")] if d == 0 else [(pj[0], "ps_pj0"), (pj[1], "ps_pj1")]
                        OBL = [(ob[0], "ps_ob0"), (ob[1], "ps_ob1")] if d == 0 else [(pj[2], "ps_pj2")]

                        def nxs():
                            pcnt[("s", d)] += 1
                            return STL[pcnt[("s", d)] % len(STL)]

                        def nxo():
                            pcnt[("o", d)] += 1
                            return OBL[pcnt[("o", d)] % len(OBL)]
                        t = s * nch + c
                        la_t = lab[:, t, d * 8:(d + 1) * 8]
                        ob_b, ko_b = nxo()
                        P.pe(MM(ob_b[:, 0:8], tri[d][:], la_t), r=[("la", t), "triU", "triL"], w=[ko_b])
                        P.pe(MM(ob_b[:, 8:16], onesf[:], la_t), r=[("la", t), "onesf"], w=[ko_b])
                        P.act(ACT(acum[:, 0:16], ob_b[:, 0:16], AF.Copy), r=[ko_b], w=["acum"])
                        yield
                        P.act(ACT(acum[:, 16:24], acum[:, 0:8], AF.Exp), r=["acum"], w=["eacum"])
                        P.dve(TT(acum[:, 24:32], acum[:, 8:16], acum[:, 0:8], ALU.subtract), r=["acum"], w=["wexp0"])
                        P.act(ACT(acum[:, 24:32], acum[:, 24:32], AF.Exp), r=["wexp0"], w=["wexp"])
                        P.act(ACT(sm[:, cd0:cd0 + 8], acum[:, 8:16], AF.Exp), r=["acum"], w=["cdec"])
                        yield
                        P.dve(TT(Rb[:], bc(tri[d][:], 1, [128, 8, 128]), bc(la_t, 2, [128, 8, 128]), ALU.mult),
                              r=[("la", t), "triU", "triL"], w=["Rb", ("Rbh", 0), ("Rbh", 1)])
                        sb_ = []
                        for hh in range(2):
                            yield
                            st_b2, k_b2 = nxs()
                            P.pe(MM(st_b2[:, :], onesf[:], Rb[:, hh * 4:(hh + 1) * 4, :]), r=["Rb", "onesf"],
                                 w=[k_b2], ni=2)
                            for h4 in range(4):
                                hd = hh * 4 + h4
                                P.dve(TS(seg[:, hd, :], st_b2[:, h4 * 128:(h4 + 1) * 128], acum[:, hd:hd + 1],
                                         ALU.subtract, 0.0, ALU.min),
                                      r=[k_b2, "acum"], w=[("seg", hh)])
                        yield
                        P.act(ACT(dec[:], seg[:], AF.Exp), r=[("seg", 0), ("seg", 1)], w=["dec"])
                        yield
                        for g in range(2):
                            st_b3, k_b3 = nxs()
                            P.pe(MM(st_b3[:, 0:128], xbcT[g * 64:(g + 1) * 64, 4, t * 128:(t + 1) * 128],
                                    xbcT[g * 64:(g + 1) * 64, 5, t * 128:(t + 1) * 128]),
                                 r=[("qkT", 1, 4, t // 4, g), ("qkT", 1, 5, t // 4, g)], w=[k_b3])
                            P.dve(TT(cbm[:, g, :], st_b3[:, 0:128], tri[d][:], ALU.mult),
                                  r=[k_b3, "triU", "triL"], w=[("cbm", g)])
                        for g in range(2):
                            P.dve(TT(MTb[:, g * 4:(g + 1) * 4, :], dec[:, g * 4:(g + 1) * 4, :],
                                     bc(cbm[:, g, :], 1, [128, 4, 128]), ALU.mult), r=["dec", ("cbm", g)], w=[("MT", g)])
                        yield
                        P.dve(TT(xdt[:].rearrange("p (h q) -> p h q", h=8), xtok[:, t, :].rearrange("p (h q) -> p h q", h=8),
                                 bc(dtb[:, t, d * 8:(d + 1) * 8], 2, [128, 8, 64]), ALU.mult),
                              r=[("vaug", t), ("dt", t)], w=["xdt"])
                        P.dve(TT(wxdt[:].rearrange("p (h q) -> p h q", h=8), xdt[:].rearrange("p (h q) -> p h q", h=8),
                                 bc(acum[:, 24:32], 2, [128, 8, 64]), ALU.mult), r=["xdt", "wexp"], w=["wxdt"])
                        yield
                        ob_by, ko_by = nxo()
                        for hd in range(8):
                            P.pe(MM(ob_by[:, hd * 64:(hd + 1) * 64], MTb[:, hd, :], xdt[:, hd * 64:(hd + 1) * 64]),
                                 r=[("MT", hd // 4), "xdt"], w=[ko_by])
                        first_dir_chunk = c not in ysw
                        ysw.add(c)
                        if hstate[d]:
                            P.act(ACT(ysb, ob_by[:, :], AF.Copy), r=[ko_by], w=["ysb"])
                            for g in range(2):
                                st_bo, k_bo = nxs()
                                for r4 in range(4):
                                    P.pe(MM(st_bo[:, r4 * 64:(r4 + 1) * 64],
                                            xbcT[g * 64:(g + 1) * 64, 5, t * 128:(t + 1) * 128],
                                            hstb[g * 64:(g + 1) * 64, r4 * 64:(r4 + 1) * 64]),
                                         r=[("qkT", 1, 5, t // 4, g), "hstb"], w=[k_bo])
                                P.dve(TT(ytmp[:, g * 256:(g + 1) * 256].rearrange("p (h q) -> p h q", h=4),
                                         st_bo[:, 0:256].rearrange("p (h q) -> p h q", h=4),
                                         bc(acum[:, 16 + 4 * g:20 + 4 * g], 2, [128, 4, 64]), ALU.mult),
                                      r=[k_bo, "eacum"], w=[("ytmp", g)])
                            P.dve(TT(ysb, ysb, ytmp, ALU.add), r=["ysb", ("ytmp", 0), ("ytmp", 1)], w=["ysb"])
                            ysrc, ykey = ysb, "ysb"
                        else:
                            ysrc, ykey = ob_by[:, :], ko_by
                        if first_dir_chunk:
                            P.act(ACT(ysum[:, c % nys, :], ysrc, AF.Copy), r=[ykey], w=[("ysum", c % nys)])
                        else:
                            P.dve(TT(ysum[:, c % nys, :], ysum[:, c % nys, :], ysrc, ALU.add),
                                  r=[ykey, ("ysum", c % nys)], w=[("ysum", c % nys)])
                        yield
                        st_bs, k_bs = nxs()
                        for g in range(2):
                            P.pe(MM(st_bs[0:64, g * 256:(g + 1) * 256], btok[:, t, g * 64:(g + 1) * 64],
                                    wxdt[:, g * 256:(g + 1) * 256]), r=[("btok", t), "wxdt"], w=[k_bs])
                        for g in range(2):
                            hs = hst[g * 64:(g + 1) * 64, :]
                            if hstate[d]:
                                P.dve(TT(hs.rearrange("p (r q) -> p r q", r=4), hs.rearrange("p (r q) -> p r q", r=4),
                                         bc(sm[g * 64:(g + 1) * 64, cd0 + 4 * g:cd0 + 4 + 4 * g], 2, [64, 4, 64]), ALU.mult),
                                      r=["cdec", ("hst", g)], w=[("hst", g)])
                                P.dve(TT(hs, hs, st_bs[0:64, g * 256:(g + 1) * 256], ALU.add),
                                      r=[k_bs, ("hst", g)], w=[("hst", g)])
                            else:
                                P.act(ACT(hs, st_bs[0:64, g * 256:(g + 1) * 256], AF.Copy), r=[k_bs],
                                      w=[("hst", g)])
                        P.act(ACT(hstb[:], hst[:], AF.Copy), r=[("hst", 0), ("hst", 1)], w=["hstb"])
                        hstate[d] = True

                        yield

                    for s in range(nseq):
                        hs = {0: False, 1: False}
                        ysw = set()
                        for d in range(2):
                            acum_, Rb_, seg_, dec_, cbm_, MTb_, xdt_, wxdt_, hst_d, hstb_d, ysb_, ytmp_ = BUF[d]
                            P.keymap = km1 if d == 1 else None
                            if samp:
                                P.dma("sp", "ld_hin", hin[:], D["sst"][li, d].rearrange("(b p) n -> p b n", p=128), w=["hin"])
                                for blk in range(4):
                                    bt = nxt("st", 2)
                                    P.pe(TR(st[bt][0:64, 0:128], hin[:, blk, :], identf[:]), r=["hin", "identf"],
                                         w=["ps_st%d" % bt])
                                    g = blk // 2
                                    P.act(ACT(hst_d[g * 64:(g + 1) * 64, (blk % 2) * 128:(blk % 2 + 1) * 128], st[bt][0:64, 0:128],
                                              AF.Copy), r=["ps_st%d" % bt], w=[("hst", g)])
                                P.act(ACT(hstb_d[:], hst_d[:], AF.Copy), r=[("hst", 0), ("hst", 1)], w=["hstb"])
                                hs[d] = True
                            P.keymap = None
                        for ci in range(nch):
                            gens = [(d, chunk_step(s, d, ci if d == 0 else nch - 1 - ci, hs, ysw)) for d in range(2)]
                            while gens:
                                for dg in list(gens):
                                    P.keymap = km1 if dg[0] == 1 else None
                                    try:
                                        next(dg[1])
                                    except StopIteration:
                                        gens.remove(dg)
                                    P.keymap = None
                        for d in range(2):
                            hst_o = BUF[d][8]
                            P.keymap = km1 if d == 1 else None
                            if not samp:
                                for blk in range(2):
                                    bt = nxt("st", 2)
                                    P.pe(TR(st[bt][:, 0:128], hst_o[:, blk * 128:(blk + 1) * 128], identf[:]),
                                         r=[("hst", 0), ("hst", 1), "identf"], w=["ps_st%d" % bt])
                                    hi = nxt("hout", 2)
                                    P.act(ACT(houts[hi][:], st[bt][:, 0:128], AF.Copy), r=["ps_st%d" % bt], w=["hout%d" % hi])
                                    for g in range(2):
                                        r0 = (4 * g + 2 * blk) * 64
                                        P.dma("sp", "o_ssd%d" % hi, O["nssd"][s, li, d, r0:r0 + 128, :],
                                              houts[hi][:, g * 64:(g + 1) * 64], r=["hout%d" % hi])
                            P.keymap = None
                        if own:
                            for c in range(nch):
                                P.dve(TT(ysb.rearrange("p (h q) -> p h q", h=8), xtok[:, c, :].rearrange("p (h q) -> p h q", h=8),
                                         bc(dsum[:, 0:8], 2, [128, 8, 64]), ALU.mult), r=[("vaug", c), "dsum", "ysb"], w=["ysb"])
                                P.dve(TT(ysum[:, c, :], ysum[:, c, :], ysb, ALU.add), r=["ysb", ("ysum", c)], w=[("ysum", c)])
                            for j in range(2):
                                P.dve(TS(ysb, ysum[:, 0, :], selO[:, j * 8:j * 8 + 1], ALU.mult), r=[("ysum", 0), "selO", "ysb"], w=["ysb"])
                                for c in range(1, nch):
                                    P.dve(STT(ysb, ysum[:, c, :], selO[:, j * 8 + c:j * 8 + c + 1], ysb, ALU.mult, ALU.add),
                                          r=[("ysum", c), "selO", "ysb"], w=["ysb"])
                                P.dve(TT(ysb, ysb, gbuf[:, j, :], ALU.mult), r=["ysb", ("gbuf", j)], w=["ysb"])
                                P.act(ACT(ytmp, ysb, AF.Square, accum_out=sm[:, 50:51]), r=["ysb"],
                                      w=["ytmp", ("ytmp", 0), ("ytmp", 1), "sm_ys"])
                                rstd_from_ss(sm[:, 50:51], sm[:, 51:52], 512, "sm_ys", "sm_yr")
                                P.dve(STT(Ytok[:, j, :], ysb, sm[:, 51:52], ssdg[:], ALU.mult, ALU.mult),
                                      r=["ysb", "sm_yr", "ssdg"], w=[("Ytok", j)])
                                ytok_to_YT(j, j)
                        for c in range(nch if not own else 0):
                            t = s * nch + c
                            oi = 0
                            P.dve(TT(ysb.rearrange("p (h q) -> p h q", h=8), xtok[:, t, :].rearrange("p (h q) -> p h q", h=8),
                                     bc(dsum[:, 0:8], 2, [128, 8, 64]), ALU.mult), r=[("vaug", t), "dsum", "ysb"], w=["ysb"])
                            P.dve(TT(ysb, ysb, ysum[:, c % nys, :], ALU.add), r=["ysb", ("ysum", c % nys)], w=["ysb"])
                            P.dve(TT(ysb, ysb, gbuf[:, t, :], ALU.mult), r=["ysb", ("gbuf", t)], w=["ysb"])
                            P.act(ACT(ytmp, ysb, AF.Square, accum_out=sm[:, 50:51]), r=["ysb"], w=["ytmp", ("ytmp", 0), ("ytmp", 1), "sm_ys"])
                            rstd_from_ss(sm[:, 50:51], sm[:, 51:52], 512, "sm_ys", "sm_yr")
                            P.dve(STT(Ytok[:, c % 2, :], ysb, sm[:, 51:52], ssdg[:], ALU.mult, ALU.mult),
                                  r=["ysb", "sm_yr", "ssdg"], w=[("Ytok", c % 2)])
                            ytok_to_YT(t, c % 2)
                    branch_merge(D["w_br_b"][li], MRG + 1024, False)

                if "A" in stages:
                    attn_mixer("A")
                if "B" in stages:
                    ssd_mixer()
                if "C" in stages:
                    attn_mixer("C")
                if "P" not in stages:
                    continue

                P.tag = "%d.%d.post" % (kind, li)
                for t in range(NTq):
                    xi = nxt("xt", 2)
                    P.act(ACT(hbf[xi][:], acc[:, t, :], AF.Copy), r=[("acc", t, 0), ("acc", t, 1)], w=["hbf0"])
                    for k in range(8):
                        P.pe(TR(tpb[:, k, :], hbf[xi][:, k * 128:(k + 1) * 128], identb[:]),
                             r=["hbf0", "identb"], w=["ps_tpb"])
                    P.act(ACT(hTq[:, :, t * 128:(t + 1) * 128], tpb[:, :, :], AF.Copy), r=["ps_tpb"], w=[(hqk, t)])
                xo = big[:, :].rearrange("p (j f) -> p j f", j=4)
                for t in range(NTq):
                    P.dma("sp" if t % 2 == 0 else "act", "ld_xa%d" % t, acc[:, t, :], xsrc_q[t * 128:(t + 1) * 128, :],
                          r=[(xkq, t)], w=[("acc", t, 0), ("acc", t, 1)])
                for t in range(NTq):
                    xi = t % 2
                    xb_ = acc[:, t, :]
                    XK = [("acc", t, 0), ("acc", t, 1)]
                    if t == 0:
                        wts = [wload(D["w_out"][li][:, cb * 512:(cb + 1) * 512], 8, 512) for cb in range(2)]
                    for cb in range(2):
                        wt, wk = wts[cb]
                        pB, kB_ = ALLB[nxt("allb", 7)]
                        for k in range(8):
                            P.pe(MM(pB[:, :], hTq[:, k, t * 128:(t + 1) * 128], wt[:, k, :], k == 0, k == 7),
                                 r=[(hqk, t), wk], w=[kB_])
                        si = nxt("sgm", 2)
                        P.dve(TT(sgm[si], pB[:, :], gate[:, cb * 512:(cb + 1) * 512], ALU.mult),
                              r=[kB_, "ada"], w=["sgm%d" % si])
                        P.dve(TT(xb_[:, cb * 512:(cb + 1) * 512], xb_[:, cb * 512:(cb + 1) * 512], sgm[si], ALU.add),
                              r=["sgm%d" % si, ("acc", t, cb)], w=[("acc", t, cb)])
                    if li < n_layers - 1 and gath:
                        P.dma("sp", "st_x%d" % xi, xscr_own[t * 128:(t + 1) * 128, :], xb_, r=XK,
                              w=[("xscr_own", t)])
                        if t == NTq - 1:
                            P.op("pool", lambda e: e.collective_compute(
                                "AllGather", ALU.bypass, replica_groups=[[0, 1, 2, 3], [4, 5, 6, 7]],
                                ins=[xscr_own], outs=[xgath]),
                                reads=[("xscr_own", 0), ("xscr_own", 1)], writes=[("xgath", t_) for t_ in range(NT)],
                                dma="cc_x", inc=1)
                    elif li < n_layers - 1:
                        P.dma("sp", "st_x%d" % xi, xscr[t * 128:(t + 1) * 128, :], xb_, r=XK,
                              w=[("xscr", t)])
                        if next_own:
                            for j in range(2):
                                sc = selO[:, j * 8 + t:j * 8 + t + 1]
                                if t == 0:
                                    P.dve(TS(xo[:, j, :], xb_, sc, ALU.mult), r=XK + ["selO"], w=[("xo", j)])
                                else:
                                    P.dve(STT(xo[:, j, :], xb_, sc, xo[:, j, :], ALU.mult, ALU.add),
                                          r=XK + ["selO", ("xo", j)], w=[("xo", j)])
                            if t == NT - 1:
                                for j in range(2):
                                    P.dma("sp", "st_xo", xscr_own[j * 128:(j + 1) * 128, :], xo[:, j, :], r=[("xo", j)],
                                          w=[("xscr_own", j)])
                    else:
                        if t == 0:
                            P.dma("sp", "ld_gs", gs, D["final_g"].partition_broadcast(128), r=["ada"], w=["gs"])
                        c0_ = 16 + 2 * xi
                        P.act(ACT(xt[xi][:], xb_, AF.Square, accum_out=sm[:, c0_:c0_ + 1]),
                              r=XK, w=["xt%d" % xi, "sm_fs%d" % xi])
                        rstd_from_ss(sm[:, c0_:c0_ + 1], sm[:, c0_ + 1:c0_ + 2], 1024, "sm_fs%d" % xi, "sm_fr%d" % xi)
                        P.dve(STT(xt[xi][:], xb_, sm[:, c0_ + 1:c0_ + 2], gs, ALU.mult, ALU.mult),
                              r=XK + ["sm_fr%d" % xi, "gs", "ada"], w=["xt%d" % xi])
                        P.dma("sp" if xi == 0 else "act", "o_y%d" % xi, yout[t * 128:(t + 1) * 128, :], xt[xi][:], r=["xt%d" % xi])

        compute_ada(0)
        if not do_prompt:
            for l_ in range(1, n_layers):
                compute_ada(l_)
        if do_prompt and do_sample and own_all and n_layers == 2:
            run_pass(1, [0])
            run_pass(0)
            run_pass(1, [1])
        else:
            if do_prompt:
                run_pass(0)
            if do_sample:
                run_pass(1)
        fk = [k for k in P.dma_count if k.startswith("o_") or k.startswith("st_x")]
        print("n_ops", len(P.ops), "sbuf_free", nc.sbuf_bytes_remaining, flush=True)
        import os, json
        if os.environ.get("KTAGS"):
            json.dump({e: [o["tag"] for o in P.ops if o["eng"] == e for _ in range(o.get("ni", 1))] for e in ENGS}, open(os.environ["KTAGS"], "w"))
        P.emit(final_keys=fk)
    return nc


_NC = {}


def _consts():
    ident = np.eye(128, dtype=np.float32)
    tt = np.arange(128)
    triU = (tt[:, None] <= tt[None, :]).astype(np.float32)
    triL = (tt[:, None] >= tt[None, :]).astype(np.float32)
    pos = np.arange(1024)
    inv = (10000.0 ** (-np.arange(16, dtype=np.float32) / 16)).astype(np.float32)
    C = np.zeros((64, 1024), np.float32)
    S = np.zeros((64, 1024), np.float32)
    RT = np.zeros((64, 64), np.float32)
    for d in range(64):
        half = d // 32
        qd = (d % 32) % 16
        p = (pos // 64) if half == 0 else (pos % 64)
        ang = p.astype(np.float32) * inv[qd]
        C[d] = np.cos(ang)
        S[d] = np.sin(ang)
        if (d % 32) < 16:
            RT[d + 16, d] = -1.0
        else:
            RT[d - 16, d] = 1.0
    col = np.arange(64)
    cs = np.clip(col - 8, 0, 48)
    inwin = (col[None, :] >= cs[:, None]) & (col[None, :] < cs[:, None] + 16)
    m = np.where(inwin.T, 0.0, NEG).astype(np.float32)
    namask = np.concatenate([m, m], axis=0)
    RT2 = np.zeros((128, 128), np.float32)
    RT2[:64, :64] = RT
    RT2[64:, 64:] = RT
    return dict(ident=ident, triU=triU, triL=triL, ropeC=np.concatenate([C, C], 0), ropeS=np.concatenate([S, S], 0),
                ropeRT=RT2, namask=namask)


def kernel(**inp):
    f = lambda a: np.ascontiguousarray(np.asarray(a, dtype=np.float32))
    n_cores = 8
    cst = _consts()
    col = np.arange(64)
    dc = np.clip(col[:, None] - col[None, :], -15, 15) + 15
    rpb = f(inp["na_rpb"])
    rpbT = np.ascontiguousarray(rpb[:, :, ::-1, :][:, :, :, dc])
    shared = dict(
        norm_g=f(inp["norm_g"]), w_ada=f(inp["w_ada"]), b_ada=f(inp["b_ada"]), w_in=f(inp["w_in"]),
        lamv=f(np.stack([inp["lam_q1"], inp["lam_q2"], inp["lam_k1"], inp["lam_k2"]], axis=1)),
        subln_g=f(inp["diff_subln_g"]),
        convwT=f(np.asarray(inp["conv_w"]).reshape(2, 5, 6, 128).transpose(0, 3, 2, 1)),
        convbT=f(np.asarray(inp["conv_b"]).reshape(2, 6, 128).transpose(0, 2, 1)),
        dt_bias=f(np.asarray(inp["dt_bias"]).reshape(2, 16)), a_log=f(np.asarray(inp["a_log"]).reshape(2, 16)),
        d_skip=f(np.asarray(inp["d_skip"]).reshape(2, 16)), ssd_norm_g=f(inp["ssd_norm_g"]), rpbT=rpbT,
        w_br_a=f(inp["w_br_a"]), w_br_b=f(inp["w_br_b"]), w_br_c=f(inp["w_br_c"]), w_out=f(inp["w_out"]),
        final_g=f(inp["final_g"]), **cst)
    xp = f(inp["x_prompt"])
    xs = f(inp["x_sample"])
    own_tabs = []
    kc_i = np.arange(64)
    dcc = np.clip(kc_i[:, None] - kc_i[None, :], -15, 15) + 15
    cs = np.clip(kc_i - 8, 0, 48)
    inwin_T = ((kc_i[:, None] >= cs[None, :]) & (kc_i[:, None] < cs[None, :] + 16))
    for qb in range(4):
        selO = np.zeros((128, 16), np.float32)
        for j in range(2):
            selO[:, j * 8 + 2 * qb + j] = 1.0
        rpbO = np.zeros((2, 8, 128, 32, 64), np.float32)
        maskO = np.full((128, 32, 64), NEG, np.float32)
        for a in range(8):
            for j in range(4):
                r_ = 4 * qb + j
                rs_ = min(max(r_ - 4, 0), 8)
                for half in range(2):
                    kr = 2 * a + half
                    if rs_ <= kr <= rs_ + 7:
                        dd = kr - r_ + 7
                        rpbO[:, :, half * 64:(half + 1) * 64, a * 4 + j, :] = rpb[:, :, dd, :][:, :, dcc]
                        maskO[half * 64:(half + 1) * 64, a * 4 + j, :] = np.where(inwin_T, 0.0, NEG)
        own_tabs.append(dict(selO=selO, rpbO=rpbO, maskO=maskO,
                             ropeCo=np.ascontiguousarray(cst["ropeC"][:, qb * 256:(qb + 1) * 256]),
                             ropeSo=np.ascontiguousarray(cst["ropeS"][:, qb * 256:(qb + 1) * 256])))
    in_maps = []
    for c in range(n_cores):
        b = c // 4
        cv = np.stack([f(inp["c_ctx"]), f(inp["c"])[b]], axis=0)
        d = dict(shared)
        d.update(
            xp=np.ascontiguousarray(xp[4 * c:4 * c + 4].reshape(T, 1024)),
            xs=np.ascontiguousarray(xs[b]),
            xso=np.ascontiguousarray(xs[b, (c % 4) * 256:(c % 4 + 1) * 256]),
            cdk=f(inp["cache_diff_k"])[b].reshape(2, 512, 512), cdv=f(inp["cache_diff_v"])[b].reshape(2, 512, 512),
            cnk=f(inp["cache_na_k"])[b].reshape(2, 512, 512), cnv=f(inp["cache_na_v"])[b].reshape(2, 512, 512),
            sst=f(inp["state_ssd"])[b].reshape(2, 2, 512, 64),
            cvecT=np.ascontiguousarray(cv.reshape(2, 8, 128).transpose(0, 2, 1)),
            **own_tabs[c % 4],
        )
        in_maps.append({k: np.ascontiguousarray(v) for k, v in d.items()})
    if "nc" not in _NC:
        _NC["nc"] = build()
    import os
    ncr = int(os.environ.get("KCORES", "8"))
    res = run_bass_kernel_spmd(_NC["nc"], in_maps[:ncr], core_ids=list(range(ncr)))
    R = list(res.results) + [res.results[0]] * (n_cores - ncr)
    y_prompt = np.concatenate([R[c]["yp"].reshape(4, 256, 1024) for c in range(n_cores)], axis=0)
    if R[0]["ys"].shape[0] == T:
        y_sample = np.stack([R[0]["ys"], R[4]["ys"]], axis=0)
    else:
        y_sample = np.stack([np.concatenate([R[4 * b + q]["ys"] for q in range(4)], axis=0) for b in range(2)], axis=0)
    cat = lambda k, shp: np.concatenate([R[c][k].reshape((4,) + shp) for c in range(n_cores)], axis=0)
    return (y_prompt, y_sample,
            cat("ndk", (2, 256, 4, 128)), cat("ndv", (2, 256, 4, 128)),
            cat("nnk", (2, 256, 8, 64)), cat("nnv", (2, 256, 8, 64)),
            cat("nssd", (2, 2, 8, 64, 64)))
```

```python
import contextlib
import math
import numpy as np
import concourse.bass as bass
import concourse.mybir as mybir
from concourse.bass_utils import run_bass_kernel_spmd

F32 = mybir.dt.float32
BF16 = mybir.dt.bfloat16
AF = mybir.ActivationFunctionType
ALU = mybir.AluOpType
AX = mybir.AxisListType

ENGS = ("pe", "act", "dve", "pool", "sp")
EPS = 1e-6
NEG = -30000.0


class Prog:
    def __init__(self, nc):
        self.nc = nc
        self.ops = []
        self.last_w = {}
        self.readers = {}
        self.dma_last = {}
        self.dma_count = {}
        self.stack = contextlib.ExitStack()

    def sb(self, name, shape, dt):
        return self.stack.enter_context(self.nc.sbuf_tensor("sb_" + name, list(shape), dt))

    def ps(self, name, shape, dt=F32):
        return self.stack.enter_context(self.nc.psum_tensor("ps_" + name, list(shape), dt))

    limit = None
    tag = ""

    keymap = None

    def op(self, eng, fn, reads=(), writes=(), dma=None, inc=16):
        oid = len(self.ops)
        if self.limit is not None and oid >= self.limit:
            return None
        if self.keymap is not None:
            reads = [self.keymap(k) for k in reads]
            writes = [self.keymap(k) for k in writes]
        deps = set()
        for k in reads:
            if k in self.last_w:
                deps.add(self.last_w[k])
            if isinstance(k, str) and k.startswith("ps_"):
                for r in self.readers.get(k, ()):
                    if self.ops[r]["eng"] != eng:
                        deps.add(r)
        for k in writes:
            if k in self.last_w:
                deps.add(self.last_w[k])
            last = {}
            for r in self.readers.get(k, ()):
                ro = self.ops[r]
                if ro["dma"] is not None:
                    deps.add(r)
                else:
                    last[ro["eng"]] = r
            deps.update(last.values())
        if dma is not None and dma in self.dma_last:
            deps.add(self.dma_last[dma])
        deps.discard(oid)
        if eng == "pe":
            deps = {d for d in deps if self.ops[d]["eng"] != "pe"}
        o = dict(id=oid, eng=eng, fn=fn, deps=deps, dma=dma, marked=False, mark=None, tag=self.tag)
        if dma is not None:
            self.dma_count[dma] = self.dma_count.get(dma, 0) + inc
            o["dma_val"] = self.dma_count[dma]
            o["inc"] = inc
            self.dma_last[dma] = oid
        self.ops.append(o)
        for k in reads:
            self.readers.setdefault(k, []).append(oid)
        for k in writes:
            self.last_w[k] = oid
            self.readers[k] = []
        return oid

    def pe(self, fn, r=(), w=(), ni=1):
        oid = self.op("pe", fn, r, w)
        if oid is not None:
            self.ops[oid]["ni"] = ni
        return oid

    def act(self, fn, r=(), w=()):
        return self.op("act", fn, r, w)

    def dve(self, fn, r=(), w=()):
        return self.op("dve", fn, r, w)

    def pool(self, fn, r=(), w=()):
        return self.op("pool", fn, r, w)

    def dma(self, q, key, out, in_, r=(), w=()):
        return self.op(q, lambda e: e.dma_start(out=out, in_=in_), r, w, dma=key)

    def emit(self, final_keys=()):
        nc = self.nc
        ops = self.ops
        for o in ops:
            for d in o["deps"]:
                p = ops[d]
                if p["dma"] is None:
                    p["marked"] = True
        cnt = {e: 0 for e in ENGS}
        for o in ops:
            if o["dma"] is None and o["marked"]:
                cnt[o["eng"]] += 1
                o["mark"] = cnt[o["eng"]]
        esem = {e: self.stack.enter_context(nc.semaphore("s_" + e)) for e in ENGS if e != "sp"}
        dsem = {k: self.stack.enter_context(nc.semaphore("d_%d" % i))
                for i, k in enumerate(self.dma_count)}
        per_eng = {e: [o for o in ops if o["eng"] == e] for e in ENGS}
        engobj = {"pe": "tensor", "act": "scalar", "dve": "vector", "pool": "gpsimd", "sp": "sync"}

        def run(e, eng):
            waited = {}
            for o in per_eng[e]:
                need = {}
                for d in o["deps"]:
                    p = ops[d]
                    if p["dma"] is not None:
                        sk, v = ("d", p["dma"]), p["dma_val"]
                    else:
                        sk, v = ("e", p["eng"]), p["mark"]
                    if need.get(sk, 0) < v:
                        need[sk] = v
                for sk, v in need.items():
                    if waited.get(sk, 0) >= v:
                        continue
                    sem = dsem[sk[1]] if sk[0] == "d" else esem[sk[1]]
                    eng.wait_ge(sem, v)
                    waited[sk] = v
                ins = o["fn"](eng)
                if o["dma"] is not None:
                    ins.then_inc(dsem[o["dma"]], o["inc"])
                elif o["marked"]:
                    ins.then_inc(esem[e], 1)
            if e == "sp":
                for k in final_keys:
                    eng.wait_ge(dsem[k], self.dma_count[k])

        with nc.Block() as block:
            for e in ENGS:
                getattr(block, engobj[e])(lambda eng, e=e: run(e, eng))


def MM(out, lhsT, rhs, start=True, stop=True):
    return lambda e: e.matmul(out, lhsT=lhsT, rhs=rhs, start=start, stop=stop)


def TR(out, in_, ident):
    return lambda e: e.transpose(out, in_, ident)


def ACT(out, in_, func, **kw):
    return lambda e: e.activation(out=out, in_=in_, func=func, **kw)


def TT(out, in0, in1, op):
    return lambda e: e.tensor_tensor(out=out, in0=in0, in1=in1, op=op)


def TS(out, in0, s1, op0, s2=None, op1=None):
    if op1 is None:
        return lambda e: e.tensor_scalar(out=out, in0=in0, scalar1=s1, scalar2=None, op0=op0)
    return lambda e: e.tensor_scalar(out=out, in0=in0, scalar1=s1, scalar2=s2, op0=op0, op1=op1)


def STT(out, in0, scalar, in1, op0, op1):
    return lambda e: e.scalar_tensor_tensor(out=out, in0=in0, scalar=scalar, in1=in1, op0=op0, op1=op1)


def CP(out, in_):
    return lambda e: e.tensor_copy(out=out, in_=in_)


def bc(ap, axis, shape):
    return ap.unsqueeze(axis).to_broadcast(list(shape))


QA, KA, VA, GA = 0, 512, 1024, 1536
ZB, XB, DTB = 2048, 2560, 3328
QC, KC, VC, GC = 3344, 3856, 4368, 4880
MRG = 5392
IN_COLS = 8464
T = 1024
NT = 8


def build(n_layers=2, do_prompt=True, do_sample=True, stages="ABCP", limit=None, own_last=True, own_all=True):
    nc = bass.Bass("TRN2", target_bir_lowering=False)

    def din(name, shape):
        return nc.dram_tensor(name, list(shape), F32, kind="ExternalInput").ap()

    def dout(name, shape):
        return nc.dram_tensor(name, list(shape), F32, kind="ExternalOutput").ap()

    D = {}
    for name, shape in [
        ("xp", (T, 1024)), ("xs", (T, 1024)), ("xso", (256, 1024)),
        ("cdk", (2, 512, 512)), ("cdv", (2, 512, 512)), ("cnk", (2, 512, 512)), ("cnv", (2, 512, 512)),
        ("sst", (2, 2, 512, 64)),
        ("cvecT", (2, 128, 8)), ("norm_g", (2, 1024)), ("w_ada", (2, 1024, 3072)), ("b_ada", (2, 3072)),
        ("w_in", (2, 1024, IN_COLS)), ("lamv", (2, 4, 64)), ("subln_g", (2, 128)),
        ("convwT", (2, 128, 6, 5)), ("convbT", (2, 128, 6)), ("dt_bias", (2, 16)), ("a_log", (2, 16)),
        ("d_skip", (2, 16)), ("ssd_norm_g", (2, 512)), ("rpbT", (2, 8, 15, 64, 64)),
        ("w_br_a", (2, 512, 1024)), ("w_br_b", (2, 512, 1024)), ("w_br_c", (2, 512, 1024)),
        ("w_out", (2, 1024, 1024)), ("final_g", (1024,)),
        ("ident", (128, 128)), ("triU", (128, 128)), ("triL", (128, 128)),
        ("ropeC", (128, 1024)), ("ropeS", (128, 1024)), ("ropeRT", (128, 128)), ("namask", (128, 64)),
        ("selO", (128, 16)), ("ropeCo", (128, 256)), ("ropeSo", (128, 256)),
        ("rpbO", (2, 8, 128, 32, 64)), ("maskO", (128, 32, 64)),
    ]:
        D[name] = din(name, shape)
    O = {}
    for name, shape in [
        ("yp", (T, 1024)), ("ys", (256 if (own_all or (own_last and n_layers > 1)) else T, 1024)),
        ("ndk", (4, 2, 256, 512)), ("ndv", (4, 2, 256, 512)), ("nnk", (4, 2, 256, 512)), ("nnv", (4, 2, 256, 512)),
        ("nssd", (4, 2, 2, 512, 64)),
    ]:
        O[name] = dout(name, shape)
    xscr = nc.dram_tensor("xscr", [T, 1024], F32, kind="Internal").ap()
    ada_scr = nc.dram_tensor("ada_scr", [2, 2, 128, 3072], F32, kind="Internal").ap()
    xscr_own = nc.dram_tensor("xscr_own", [256, 1024], F32, kind="Internal").ap()
    xgath = nc.dram_tensor("xgath", [T, 1024], F32, kind="Internal").ap()

    P = Prog(nc)
    P.limit = limit
    with P.stack:
        hT = P.sb("hT", [128, 8, T], BF16)
        acc = P.sb("acc", [128, NT, 1024], F32)
        YT = P.sb("YT", [128, 4, T], BF16)
        big = P.sb("big", [128, 4096], F32)
        qT = big[:, 0:2048].bitcast(BF16).rearrange("p (j t) -> p j t", j=4)
        PTb = big[:, 2048:3584].bitcast(BF16).rearrange("p (k q) -> p k q", k=12)
        ysum = big[:, :].rearrange("p (c f) -> p c f", c=8)
        kT = P.sb("kT", [128, 8, T], BF16)
        vaug = P.sb("vaug", [128, NT, 528], BF16)
        gbuf = P.sb("gbuf", [128, NT, 512], BF16)
        wb = [P.sb("wb%d" % i, [128, 8, 512], BF16) for i in range(2)]
        ada = P.sb("ada", [128, 3072], F32)
        xt = [P.sb("xt%d" % i, [128, 1024], F32) for i in range(2)]
        htmp = P.sb("htmp", [128, 1024], F32)
        hbf = [P.sb("hbf%d" % i, [128, 1024], BF16) for i in range(1)] * 2
        stgT = P.sb("stg", [128, 2, 512], F32)
        stg = [stgT[:, i, :] for i in range(2)]
        natab_full = stgT[:, :, :].rearrange("p a b -> p (a b)").bitcast(BF16).rearrange("p (s q) -> p s q", s=32)
        sgmT = P.sb("sgm", [128, 2, 512], F32)
        sgm = [sgmT[:, i, :] for i in range(2)]
        natab_band = sgmT[:, :, :].rearrange("p a b -> p (a b)").bitcast(BF16).rearrange("p (s q) -> p s q", s=32)
        identb = P.sb("identb", [128, 128], BF16)
        identf = P.sb("identf", [128, 128], F32)
        tri = [P.sb("triU", [128, 128], F32), P.sb("triL", [128, 128], F32)]
        onesb = P.sb("onesb", [128, 128], BF16)
        onesf = P.sb("onesf", [128, 128], F32)
        sm = P.sb("sm", [128, 104], F32)
        cbcs = [P.sb("cbc%d" % i, [128, 8, 128], BF16) for i in range(2)]
        sqb = [P.sb("sqb%d" % i, [128, 512], BF16) for i in range(2)]
        Ytok = P.sb("Ytok", [128, 2, 512], BF16)
        otmp = [P.sb("otmp%d" % i, [128, 128], F32) for i in range(4)]
        nrm2 = P.sb("nrm2", [128, 2, 8, 4], F32)
        negm = P.sb("negm", [128, 8, 4], F32)
        sgv = P.sb("sgv", [128, 128], F32)
        lamt = P.sb("lamt", [128, 4, 64], F32)
        kTc = P.sb("kTc", [128, 4, 512], BF16)
        vaugc = P.sb("vaugc", [128, 4, 528], BF16)
        nrm2c = P.sb("nrm2c", [128, 8], F32)
        nrm2s = P.sb("nrm2s", [128, 2, 8], F32)
        ropeRT = P.sb("ropeRT", [128, 128], F32)
        namask = P.sb("namask", [128, 64], F32)
        hin = P.sb("hin", [128, 4, 64], F32)
        hTo = P.sb("hTo", [128, 8, 256], BF16)
        selO = P.sb("selO", [128, 16], F32)
        convw = P.sb("convw", [128, 6, 5], F32)
        convb = P.sb("convb", [128, 6], F32)
        dtb = P.sb("dtb", [128, NT, 16], F32)
        lab = P.sb("lab", [128, NT, 16], F32)
        dtbias = P.sb("dtbias", [128, 16], F32)
        nega = P.sb("nega", [128, 16], F32)
        dsum = P.sb("dsum", [128, 16], F32)
        ssdg = P.sb("ssdg", [128, 512], F32)
        btok = P.sb("btok", [128, NT, 128], BF16)
        Rbraw = P.sb("Rb", [128, 1040], F32)
        Rb = Rbraw[:, 0:1024].rearrange("p (h i) -> p h i", h=8)
        xpre = Rbraw[:, :].rearrange("p (s q) -> p s q", s=4)
        seg = P.sb("seg", [128, 8, 128], F32)
        cacc = seg[:, :, :].rearrange("p a b -> p (a b)").rearrange("p (s q) -> p s q", s=4)
        dec = P.sb("dec", [128, 8, 128], BF16)
        MTb = P.sb("MTb", [128, 8, 128], BF16)
        cbm = P.sb("cbm", [128, 2, 128], BF16)
        xdt = P.sb("xdt", [128, 512], BF16)
        wxdt = P.sb("wxdt", [128, 512], BF16)
        acum = P.sb("acum", [128, 32], F32)
        hst = P.sb("hst", [128, 256], F32)
        hst2 = P.sb("hst2", [128, 256], F32)
        hstb2 = P.sb("hstb2", [128, 256], BF16)
        acum2 = P.sb("acum2", [128, 32], F32)
        cbm2 = P.sb("cbm2", [128, 2, 128], BF16)
        hstb = P.sb("hstb", [128, 256], BF16)
        ysb = htmp[:, 0:512]
        ytmp = htmp[:, 512:1024]
        houts = [P.sb("hout%d" % i, [128, 128], F32) for i in range(2)]
        pj = [P.ps("ps_pj%d" % i, [128, 512]) for i in range(3)]
        st = [P.ps("ps_st%d" % i, [128, 512]) for i in range(2)]
        ob = [P.ps("ps_ob%d" % i, [128, 512]) for i in range(2)]
        tpb = P.ps("ps_tpb", [128, 8, 128], BF16)
        ALLB = [(pj[i], "ps_pj%d" % i) for i in range(3)] + [(st[i], "ps_st%d" % i) for i in range(2)] + \
               [(ob[i], "ps_ob%d" % i) for i in range(2)]
        cnt = {"allb": 0, "pj": 0, "st": 0, "ob": 0, "w": 0, "stg": 0, "sgm": 0, "xt": 0, "sqb": 0, "otmp": 0, "hout": 0}

        def nxt(name, n):
            i = cnt[name] % n
            cnt[name] += 1
            return i

        P.dma("pool", "c_id", identb[:], D["ident"], w=["identb"])
        P.dma("sp", "c_misc", identf[:], D["ident"], w=["identf"])
        P.dma("sp", "c_misc", tri[0][:], D["triU"], w=["triU"])
        P.dma("sp", "c_misc", tri[1][:], D["triL"], w=["triL"])
        P.dve(lambda e: e.memset(onesb[:], 1.0), w=["onesb"])
        P.dve(lambda e: e.memset(onesf[:], 1.0), w=["onesf"])
        P.dma("sp", "c_misc", ropeRT[:], D["ropeRT"], w=["ropeRT"])
        P.dma("sp", "c_misc", namask[:], D["namask"], w=["namask"])
        P.dma("sp", "c_misc", selO[:], D["selO"], w=["selO"])

        def wload(view, kc, ncols):
            s = nxt("w", 2)
            P.dma("pool", "wq%d" % s, wb[s][:, 0:kc, 0:ncols], view.rearrange("(k p) c -> p k c", p=128),
                  w=["wb%d" % s])
            return wb[s], "wb%d" % s

        def rstd_from_ss(ss_ap, out_ap, n, rk, wk):
            P.act(ACT(out_ap, ss_ap, AF.Ln, scale=1.0 / n, bias=EPS), r=[rk], w=[wk + "_l"])
            P.act(ACT(out_ap, out_ap, AF.Exp, scale=-0.5), r=[wk + "_l"], w=[wk])

        def compute_ada(li):
            P.tag = "ada%d" % li
            for kd in range(2):
                c0 = 64 + kd * 16
                P.dma("sp", "ld_sm", sm[:, c0:c0 + 8], D["cvecT"][kd], w=[("sm_c", kd)])
                P.act(ACT(sm[:, c0 + 8:c0 + 16], sm[:, c0:c0 + 8], AF.Silu), r=[("sm_c", kd)], w=[("sm_sc", kd)])
                P.dve(CP(cbcs[kd][:], bc(sm[:, c0 + 8:c0 + 16], 2, [128, 8, 128])), r=[("sm_sc", kd)], w=["cbc%d" % kd])
            for cb in range(6):
                wt, wk = wload(D["w_ada"][li][:, cb * 512:(cb + 1) * 512], 8, 512)
                for kd in range(2):
                    P.dma("sp", "ld_ba%d" % kd, sgm[kd], D["b_ada"][li][cb * 512:(cb + 1) * 512].partition_broadcast(128),
                          w=["sgm%d" % kd])
                    b = nxt("pj", 3)
                    for k in range(8):
                        P.pe(MM(pj[b][:, :], cbcs[kd][:, k, :], wt[:, k, :], k == 0, k == 7),
                             r=["cbc%d" % kd, wk], w=["ps_pj%d" % b])
                    P.dve(TT(sgm[kd], sgm[kd], pj[b][:, :], ALU.add), r=["ps_pj%d" % b, "sgm%d" % kd], w=["sgm%d" % kd])
                    P.dma("sp", "st_ada%d" % kd, ada_scr[kd, li][:, cb * 512:(cb + 1) * 512], sgm[kd],
                          r=["sgm%d" % kd], w=[("ada_scr", kd, li)])

        def run_pass(kind, layers=None):
            samp = kind == 1
            xin = D["xs"] if samp else D["xp"]
            yout = O["ys"] if samp else O["yp"]

            for li in (range(n_layers) if layers is None else layers):
                xsrc = xin if li == 0 else xscr
                Win = D["w_in"][li]
                gath = samp and own_all
                if gath and li > 0:
                    xsrc = xgath
                own = gath or (samp and own_last and li == n_layers - 1 and n_layers > 1)
                next_own = samp and own_last and li == n_layers - 2 and not gath
                hTq, NTq, hqk = (hTo, 2, "hTo") if own else (hT, NT, "hT")
                xsrc_q = (D["xso"] if (gath and li == 0) else xscr_own) if own else xsrc
                xkq = "xscr_own" if own else ("xgath" if (gath and li > 0) else "xscr")
                xka = "xgath" if (gath and li > 0) else "xscr"
                P.tag = "%d.%d.pre" % (kind, li)
                P.dma("sp", "ld_ada", ada[:], ada_scr[kind, li], r=[("ada_scr", kind, li)], w=["ada"])
                P.dma("sp", "ld_gs", htmp[:], D["norm_g"][li].partition_broadcast(128),
                      w=["htmp", "ysb", "ytmp", ("ytmp", 0), ("ytmp", 1)])
                shift = ada[:, 0:1024]
                scale = ada[:, 1024:2048]
                gate = ada[:, 2048:3072]
                gs = scale
                P.dve(STT(gs, scale, 1.0, htmp[:], ALU.add, ALU.mult), r=["ada", "htmp"], w=["ada", "gs"])
                pre_tiles = [(xsrc, t, (xka, t), hT, "hT") for t in range(NT)]
                if own:
                    pre_tiles += [(xsrc_q, t, ("xscr_own", t), hTo, "hTo") for t in range(2)]
                HTK = ["htmp", "ysb", "ytmp", ("ytmp", 0), ("ytmp", 1)]

                def pre_tile(xs_, t, xkey, hdst, hk, par):
                    xtb, xk = xt[par], "xt%d" % par
                    if par == 0:
                        ht_, htk, hb_, hbk, c0, tp_, tpk = htmp[:], HTK, hbf[0][:], ["hbf0"], 16, tpb, "ps_tpb"
                    else:
                        ht_ = sgmT[:, :, :].rearrange("p a b -> p (a b)")
                        htk = ["sgm0", "sgm1"]
                        hb_, hbk, c0 = stgT[:, 0, :].bitcast(BF16), ["stg0"], 18
                        tp_, tpk = ob[1][:, :].bitcast(BF16).rearrange("p (k q) -> p k q", k=8), "ps_ob1"
                    ssk, rsk = "sm_ss%d" % par, "sm_rs%d" % par
                    P.dma("sp", "ld_x%d" % par, xtb[:], xs_[t * 128:(t + 1) * 128, :], r=[xkey], w=[xk])
                    yield
                    P.act(ACT(ht_, xtb[:], AF.Square, accum_out=sm[:, c0:c0 + 1]), r=[xk], w=htk + [ssk])
                    yield
                    P.act(ACT(sm[:, c0 + 1:c0 + 2], sm[:, c0:c0 + 1], AF.Ln, scale=1.0 / 1024, bias=EPS), r=[ssk], w=[rsk + "_l"])
                    yield
                    P.act(ACT(sm[:, c0 + 1:c0 + 2], sm[:, c0 + 1:c0 + 2], AF.Exp, scale=-0.5), r=[rsk + "_l"], w=[rsk])
                    yield
                    P.dve(STT(ht_, xtb[:], sm[:, c0 + 1:c0 + 2], gs, ALU.mult, ALU.mult), r=[xk, rsk, "gs", "ada"], w=htk)
                    yield
                    P.dve(TT(hb_, ht_, shift, ALU.add), r=htk + ["ada"], w=hbk)
                    yield
                    for k in range(8):
                        P.pe(TR(tp_[:, k, :], hb_[:, k * 128:(k + 1) * 128], identb[:]), r=hbk + ["identb"], w=[tpk])
                    yield
                    P.act(ACT(hdst[:, :, t * 128:(t + 1) * 128], tp_[:, :, :], AF.Copy), r=[tpk], w=[(hk, t)])

                def zip2(gens):
                    gens = list(gens)
                    while gens:
                        for g_ in list(gens):
                            try:
                                next(g_)
                            except StopIteration:
                                gens.remove(g_)
                for i_ in range(0, len(pre_tiles), 2):
                    zip2([pre_tile(*pre_tiles[i_ + j_], j_) for j_ in range(2) if i_ + j_ < len(pre_tiles)])
                hT_all = [("hT", t) for t in range(NT)]

                def proj_tm(col0, ncols, evac, wt=None, wk=None, wcol=0, q=False):
                    if wt is None:
                        wt, wk = wload(Win[:, col0:col0 + ncols], 8, ncols)
                        wcol = 0
                    hsrc, nt_, hk = (hTq, NTq, hqk) if q else (hT, NT, "hT")
                    for t in range(nt_):
                        b = nxt("pj", 3)
                        for k in range(8):
                            P.pe(MM(pj[b][:, 0:ncols], hsrc[:, k, t * 128:(t + 1) * 128],
                                    wt[:, k, wcol:wcol + ncols], k == 0, k == 7),
                                 r=[(hk, t), wk], w=["ps_pj%d" % b])
                        evac(pj[b], "ps_pj%d" % b, t)
                    return wt, wk

                def proj_fm(wt, wk, wcol, m, evac, q=False):
                    if q and own:
                        b = nxt("pj", 3)
                        for k in range(8):
                            P.pe(MM(pj[b][0:m, 0:256], wt[:, k, wcol:wcol + m], hTo[:, k, :], k == 0, k == 7),
                                 r=[("hTo", 0), ("hTo", 1), wk], w=["ps_pj%d" % b])
                        evac(pj[b], "ps_pj%d" % b, 0)
                        return
                    for tb in range(2):
                        b = nxt("pj", 3)
                        for k in range(8):
                            P.pe(MM(pj[b][0:m, :], wt[:, k, wcol:wcol + m], hT[:, k, tb * 512:(tb + 1) * 512],
                                    k == 0, k == 7),
                                 r=hT_all[tb * 4:(tb + 1) * 4] + [wk], w=["ps_pj%d" % b])
                        evac(pj[b], "ps_pj%d" % b, tb)

                def branch_merge(w_br, mcol, first):
                    P.tag = "%d.%d.merge" % (kind, li)
                    for cb in range(2):
                        wA, wAk = wload(w_br[:, cb * 512:(cb + 1) * 512], 4, 512)
                        wM, wMk = wload(Win[:, mcol + cb * 512: mcol + (cb + 1) * 512], 8, 512)
                        for t in range(NTq):
                            pA, kA_ = ALLB[nxt("allb", 7)]
                            for k in range(4):
                                P.pe(MM(pA[:, :], YT[:, k, t * 128:(t + 1) * 128], wA[:, k, :], k == 0, k == 3),
                                     r=[("YT", t), wAk], w=[kA_])
                            pL, kL_ = ALLB[nxt("allb", 7)]
                            for k in range(8):
                                P.pe(MM(pL[:, :], hTq[:, k, t * 128:(t + 1) * 128], wM[:, k, :], k == 0, k == 7),
                                     r=[(hqk, t), wMk], w=[kL_])
                            si = nxt("sgm", 2)
                            P.act(ACT(sgm[si], pL[:, :], AF.Sigmoid), r=[kL_], w=["sgm%d" % si])
                            asl = acc[:, t, cb * 512:(cb + 1) * 512]
                            if first:
                                P.dve(TT(asl, sgm[si], pA[:, :], ALU.mult),
                                      r=["sgm%d" % si, kA_], w=[("acc", t, cb)])
                            else:
                                P.dve(TT(sgm[si], sgm[si], pA[:, :], ALU.mult),
                                      r=["sgm%d" % si, kA_], w=["sgm%d" % si])
                                P.pool(TT(asl, asl, sgm[si], ALU.add),
                                       r=["sgm%d" % si, ("acc", t, cb)], w=[("acc", t, cb)])

                def ytok_to_YT(t, qt):
                    for c in range(4):
                        P.pe(TR(tpb[:, c, :], Ytok[:, qt, c * 128:(c + 1) * 128], identb[:]),
                             r=[("Ytok", qt), "identb"], w=["ps_tpb"])
                    P.act(ACT(YT[:, :, t * 128:(t + 1) * 128], tpb[:, 0:4, :], AF.Copy), r=["ps_tpb"], w=[("YT", t)])

                def attn_mixer(mx):
                    isA = mx == "A"
                    P.tag = "%d.%d.%s.proj" % (kind, li, mx)
                    qc, kc, vc, gc = (QA, KA, VA, GA) if isA else (QC, KC, VC, GC)
                    nh = 4 if isA else 8
                    e = 128 if isA else 64
                    nm = 2 if isA else 1
                    okey, vkey = ("ndk", "ndv") if isA else ("nnk", "nnv")
                    va = vaug[:, :, 0:nh * (e + 2)].rearrange("p t (h e) -> p t h e", h=nh)
                    P.dve(lambda en: en.memset(va[:, :, :, e:e + 1], 1.0), w=[("vaug", t) for t in range(NT)])

                    ropeCo, ropeSo = acc[:, 4, 0:256], acc[:, 4, 256:512]
                    maskO_bf = acc[:, 5, :].bitcast(BF16).rearrange("p (s q) -> p s q", s=32)
                    ropeC = Rbraw[:, 0:1024]
                    ropeS = seg[:, :, :].rearrange("p a b -> p (a b)")
                    SEGK = [("seg", 0), ("seg", 1)]
                    if samp:
                        vca = vaugc[:, :, 0:nh * (e + 2)].rearrange("p t (h e) -> p t h e", h=nh)
                        ck = D["cdk" if isA else "cnk"][li]
                        cv = D["cdv" if isA else "cnv"][li]
                        P.dma("pool", "ld_ck", gbuf[:, 0:4, :], ck.rearrange("(t p) c -> p t c", p=128),
                              w=[("gbuf", t) for t in range(4)])
                        for tl in range(4):
                            P.dma("pool", "ld_cv", vca[:, tl, :, 0:e],
                                  cv[tl * 128:(tl + 1) * 128, :].rearrange("p (h e) -> p h e", h=nh), w=["vaugc"])
                        P.dve(lambda en: en.memset(vca[:, :, :, e:e + 1], 1.0), w=["vaugc1"])
                        for tl in range(4):
                            for pr in range(4):
                                P.pe(TR(tpb[:, pr, :], gbuf[:, tl, pr * 128:(pr + 1) * 128], identb[:]),
                                     r=[("gbuf", tl), "identb"], w=["ps_tpb"])
                            P.act(ACT(kTc[:, :, tl * 128:(tl + 1) * 128], tpb[:, 0:4, :], AF.Copy), r=["ps_tpb"],
                                  w=[("kTc", tl, 0), ("kTc", tl, 1)])
                        for hm in range(8):
                            pb = (hm % 2) * 64
                            si = nxt("sqb", 2)
                            P.act(ACT(sqb[si][0:64, :], kTc[pb:pb + 64, (hm // 2), :], AF.Square),
                                  r=[("kTc", tl, (hm % 2)) for tl in range(4)], w=["sqb%d" % si])
                            b = nxt("st", 2)
                            P.pe(MM(st[b][:, :], onesb[0:64, :], sqb[si][0:64, :]), r=["sqb%d" % si, "onesb"], w=["ps_st%d" % b])
                            P.dve(lambda en, b=b, hm=hm: en.tensor_reduce(out=nrm2c[:, hm:hm + 1], in_=st[b][:, :], axis=AX.X,
                                                                          op=ALU.max), r=["ps_st%d" % b], w=[("nrm2c", hm)])
                        if isA:
                            P.dma("sp", "ld_rope", ropeC, D["ropeC"], w=["Rb"])
                            P.dma("sp", "ld_rope", ropeS, D["ropeS"], w=SEGK)
                            if own:
                                P.dma("sp", "ld_rope", acc[:, 4, 0:256], D["ropeCo"], w=[("acc", 4, 0)])
                                P.dma("sp", "ld_rope", acc[:, 4, 256:512], D["ropeSo"], w=[("acc", 4, 0)])
                        elif own:
                            P.dma("pool", "ld_mo", maskO_bf, D["maskO"], w=[("acc", 5, 0), ("acc", 5, 1)])
                    pend = []
                    for which, c0, dst in ((0, qc, qT), (1, kc, kT)):
                        wt, wk = wload(Win[:, c0:c0 + 512], 8, 512)
                        for pr in range(4):
                            def ev(ps, pk, tb, pr=pr, which=which, dst=dst):
                                qo = own and which == 0
                                W = 256 if qo else 512
                                dsl = dst[:, pr, tb * 512:tb * 512 + W]
                                wkeys = [("qkT", which, pr, tb, 0), ("qkT", which, pr, tb, 1)]
                                if samp and isA:
                                    qi = nxt("sgm", 2)
                                    qf, qk_ = sgm[qi][:, 0:W], "sgm%d" % qi
                                    rc = ropeCo if qo else ropeC[:, tb * 512:(tb + 1) * 512]
                                    rs = ropeSo if qo else ropeS[:, tb * 512:(tb + 1) * 512]
                                    rk = [("acc", 4, 0)] if qo else ["Rb"] + SEGK
                                    P.act(ACT(qf, ps[:, 0:W], AF.Copy), r=[pk], w=[qk_])

                                    def rope_part(qf=qf, qk_=qk_, rc=rc, rs=rs, rk=rk, W=W, dsl=dsl, wkeys=wkeys):
                                        br = nxt("st", 2)
                                        P.pe(MM(st[br][:, 0:W], ropeRT[:], qf), r=[qk_, "ropeRT"], w=["ps_st%d" % br],
                                             ni=2 if W == 512 else 1)
                                        P.dve(TT(qf, qf, rc, ALU.mult), r=[qk_] + rk, w=[qk_])
                                        P.dve(TT(ysb[:, 0:W], st[br][:, 0:W], rs, ALU.mult), r=["ps_st%d" % br] + rk, w=["ysb"])
                                        P.dve(TT(dsl, qf, ysb[:, 0:W], ALU.add), r=[qk_, "ysb"], w=wkeys)
                                    pend.append(rope_part)
                                else:
                                    P.act(ACT(dsl, ps[:, 0:W], AF.Copy), r=[pk], w=wkeys)
                                si = nxt("sqb", 2)
                                P.act(ACT(sqb[si][:, 0:W], ps[:, 0:W], AF.Square), r=[pk], w=["sqb%d" % si])
                                for half in range(2):
                                    def norm_part(si=si, tb=tb, half=half, hm=2 * pr + half, W=W):
                                        b = nxt("st", 2)
                                        P.pe(MM(st[b][:, 0:W], onesb[half * 64:(half + 1) * 64, :], sqb[si][half * 64:(half + 1) * 64, 0:W]),
                                             r=["sqb%d" % si, "onesb"], w=["ps_st%d" % b])
                                        ns = W // 256
                                        P.dve(lambda en, b=b: en.tensor_reduce(
                                            out=nrm2[:, which, hm, tb * 2:tb * 2 + ns],
                                            in_=st[b][:, 0:W].rearrange("p (s q) -> p s q", s=ns), axis=AX.X, op=ALU.max),
                                            r=["ps_st%d" % b], w=[("nrm2", which, hm, tb)])
                                    pend.append(norm_part)
                                while len(pend) > (3 if (samp and isA) else 2):
                                    pend.pop(0)()
                            proj_fm(wt, wk, pr * 128, 128, ev, q=(which == 0))
                        while pend:
                            pend.pop(0)()
                        if which == 1 and not samp:
                            def evk(ps, pk, t):
                                si = nxt("stg", 2)
                                P.act(ACT(stg[si], ps[:, :], AF.Copy), r=[pk], w=["stg%d" % si])
                                P.dma("sp", "o_%s%d" % (okey, si), O[okey][t // 2, li, (t % 2) * 128:(t % 2 + 1) * 128, :],
                                      stg[si], r=["stg%d" % si])
                            proj_tm(c0, 512, evk, wt, wk, 0)
                    def evv(ps, pk, t):
                        if not samp:
                            si = nxt("stg", 2)
                            P.act(ACT(stg[si], ps[:, :], AF.Copy), r=[pk], w=["stg%d" % si])
                            P.dma("sp", "o_%s%d" % (vkey, si), O[vkey][t // 2, li, (t % 2) * 128:(t % 2 + 1) * 128, :],
                                  stg[si], r=["stg%d" % si])
                        P.dve(CP(va[:, t, :, 0:e], ps[:, :].rearrange("p (h e) -> p h e", h=nh)),
                              r=[pk], w=[("vaug", t)])
                    proj_tm(vc, 512, evv)
                    def evg(ps, pk, t):
                        P.act(ACT(gbuf[:, t, :], ps[:, :], AF.Silu), r=[pk], w=[("gbuf", t)])
                    proj_tm(gc, 512, evg, q=True)
                    nr = [("nrm2", w_, hm, tb) for w_ in range(2) for hm in range(8) for tb in range(2)]
                    if own:
                        nr = [("nrm2", 0, hm, 0) for hm in range(8)] + [("nrm2", 1, hm, tb) for hm in range(8) for tb in range(2)]
                        P.dve(CP(nrm2s[:, 0, :], nrm2[:, 0, :, 0]), r=nr, w=["nrm2s"])
                        P.dve(lambda en: en.tensor_reduce(out=nrm2s[:, 1, :], in_=nrm2[:, 1, :, :], axis=AX.X, op=ALU.max),
                              r=nr + ["nrm2s"], w=["nrm2s"])
                    elif samp:
                        P.dve(lambda en: en.tensor_reduce(out=nrm2s[:], in_=nrm2[:, :, :, :], axis=AX.X, op=ALU.max),
                              r=nr, w=["nrm2s"])
                    if samp:
                        P.dve(TT(nrm2s[:, 1, :], nrm2s[:, 1, :], nrm2c[:], ALU.max),
                              r=["nrm2s"] + [("nrm2c", hm) for hm in range(8)], w=["nrm2s"])
                        nmv = negm[:, :, 0]
                        P.dve(TT(nmv, nrm2s[:, 0, :], nrm2s[:, 1, :], ALU.mult), r=["nrm2s"], w=["negm0"])
                    else:
                        nmv = negm[:]
                        P.dve(TT(nmv, nrm2[:, 0, :, :], nrm2[:, 1, :, :], ALU.mult), r=nr, w=["negm0"])
                    P.act(ACT(nmv, nmv, AF.Sqrt), r=["negm0"], w=["negm1"])
                    P.dve(TS(nmv, nmv, -0.125, ALU.mult), r=["negm1"], w=["negm"])
                    if isA:
                        lam_init = 0.8 - 0.6 * math.exp(-0.3 * li)
                        P.dma("sp", "ld_lam", lamt[:], D["lamv"][li].partition_broadcast(128), w=["lamt"])
                        P.dve(TT(lamt[:, 0:2, :], lamt[:, 0:2, :], lamt[:, 2:4, :], ALU.mult), r=["lamt"], w=["lamt2"])
                        P.dve(lambda en: en.tensor_reduce(out=sm[:, 20:22], in_=lamt[:, 0:2, :], axis=AX.X, op=ALU.add),
                              r=["lamt2"], w=["sm_l0"])
                        P.act(ACT(sm[:, 22:24], sm[:, 20:22], AF.Exp), r=["sm_l0"], w=["sm_l1"])
                        P.dve(STT(sm[:, 24:25], sm[:, 23:24], -lam_init, sm[:, 22:23], ALU.add, ALU.subtract),
                              r=["sm_l1"], w=["sm_nl"])
                        P.dma("sp", "ld_sg", sgv[:], D["subln_g"][li].partition_broadcast(128), w=["sgv0"])
                        P.dve(TS(sgv[:], sgv[:], 1.0 - lam_init, ALU.mult), r=["sgv0"], w=["sgv"])

                    if isA and kind == 0 and li == 0:
                        for l_ in range(1, n_layers):
                            compute_ada(l_)
                    P.tag = "%d.%d.%s.attn" % (kind, li, mx)

                    def o_post(ov, obk, t, h, ysl, gsl, gkey, ykey, sb=32):
                        K = lambda n: "%s_%d" % (n, sb)
                        P.dve(lambda en, ov=ov: en.reciprocal(out=sm[:, sb:sb + nm], in_=ov[:, :, e]),
                              r=[obk], w=[K("sm_rl")])
                        oi = nxt("otmp", 4)
                        yield
                        if isA:
                            P.dve(TT(sm[:, sb + 2:sb + 3], sm[:, sb + 1:sb + 2], sm[:, 24:25], ALU.mult),
                                  r=[K("sm_rl"), "sm_nl"], w=[K("sm_c1")])
                            P.act(ACT(otmp[oi][:], ov[:, 0, 0:e], AF.Copy, scale=sm[:, sb:sb + 1]),
                                  r=[obk, K("sm_rl")], w=["otmp%d" % oi])
                            yield
                            P.dve(STT(otmp[oi][:], ov[:, 1, 0:e], sm[:, sb + 2:sb + 3], otmp[oi][:], ALU.mult, ALU.add),
                                  r=[obk, K("sm_c1"), "otmp%d" % oi], w=["otmp%d" % oi])
                            oj = nxt("otmp", 4)
                            yield
                            P.act(ACT(otmp[oj][:], otmp[oi][:], AF.Square, accum_out=sm[:, sb + 4:sb + 5]),
                                  r=["otmp%d" % oi], w=["otmp%d" % oj, K("sm_os")])
                            yield
                            P.act(ACT(sm[:, sb + 5:sb + 6], sm[:, sb + 4:sb + 5], AF.Ln, scale=1.0 / 128, bias=EPS),
                                  r=[K("sm_os")], w=[K("sm_or") + "_l"])
                            yield
                            P.act(ACT(sm[:, sb + 5:sb + 6], sm[:, sb + 5:sb + 6], AF.Exp, scale=-0.5),
                                  r=[K("sm_or") + "_l"], w=[K("sm_or")])
                            yield
                            P.dve(STT(otmp[oi][:], otmp[oi][:], sm[:, sb + 5:sb + 6], sgv[:], ALU.mult, ALU.mult),
                                  r=["otmp%d" % oi, K("sm_or"), "sgv"], w=["otmp%d" % oi])
                            yield
                            P.dve(TT(ysl, otmp[oi][:], gsl, ALU.mult), r=["otmp%d" % oi, gkey], w=[ykey])
                        else:
                            P.act(ACT(otmp[oi][:, 0:e], ov[:, 0, 0:e], AF.Copy, scale=sm[:, sb:sb + 1]),
                                  r=[obk, K("sm_rl")], w=["otmp%d" % oi])
                            yield
                            P.dve(TT(ysl, otmp[oi][:, 0:e], gsl, ALU.mult), r=["otmp%d" % oi, gkey], w=[ykey])

                    def zipg(gens):
                        gens = list(gens)
                        while gens:
                            for g_ in list(gens):
                                try:
                                    next(g_)
                                except StopIteration:
                                    gens.remove(g_)

                    if samp:
                        NAT = {0: [0, 1, 2, 3], 1: [0, 1, 2, 3, 4, 5], 2: [2, 3, 4, 5, 6, 7], 3: [4, 5, 6, 7]}
                        if not isA and not own:
                            P.dve(lambda en: en.memset(natab_full[:, :, :], NEG), w=["stg0", "stg1"])
                            P.dve(lambda en: en.memset(natab_band[:, :, :], NEG), w=["sgm0", "sgm1"])
                        for h in range(nh):
                            if not isA and own:
                                P.dma("pool", "ld_rpb", natab_full[:, :, :], D["rpbO"][li, h], w=["stg0", "stg1"])
                                P.dve(STT(natab_full[:, :, :], natab_full[:, :, :], 8.0, maskO_bf, ALU.mult, ALU.add),
                                      r=["stg0", "stg1", ("acc", 5, 0), ("acc", 5, 1)], w=["stg0", "stg1"])
                            elif not isA:
                                rp = D["rpbT"][li, h]
                                for tab, tkeys, lo0, dd0, n in ((natab_full, ["stg0", "stg1"], 8, 0, 15),
                                                                (natab_band, ["sgm0", "sgm1"], 12, 4, 8)):
                                    for half in range(2):
                                        sl0 = lo0 + half
                                        P.dma("pool", "ld_rpb", tab[half * 64:(half + 1) * 64, sl0:sl0 + n, :],
                                              rp[dd0:dd0 + n].rearrange("d k q -> k d q"), w=tkeys)
                                        tsl = tab[half * 64:(half + 1) * 64, sl0:sl0 + n, :]
                                        P.dve(STT(tsl, tsl, 8.0, bc(namask[half * 64:(half + 1) * 64, :], 1, [64, n, 64]),
                                                  ALU.mult, ALU.add), r=tkeys + ["namask"], w=tkeys)
                            for qb in ([0] if own else range(4)):
                                if isA or own:
                                    tiles = [("l", kt) for kt in range(8)] + [("c", kt) for kt in range(4)]
                                else:
                                    tiles = [("l", a) for a in NAT[qb]] + [("c", kt) for kt in range(4)]
                                nt_ = len(tiles)
                                for m in range(nm):
                                    hm = nm * h + m
                                    pb = (hm % 2) * 64
                                    qv = qT[pb:pb + 64, (hm // 2), qb * 256:(qb + 1) * 256]
                                    pvq = []
                                    for i, (kd, kt) in enumerate(tiles):
                                        b = nxt("st", 2)
                                        bias = (not isA) and kd == "l"
                                        if kd == "l":
                                            ksl = kT[pb:pb + 64, (hm // 2), kt * 128:(kt + 1) * 128]
                                            kr = [("qkT", 1, (hm // 2), kt // 4, (hm % 2))]
                                        else:
                                            ksl = kTc[pb:pb + 64, (hm // 2), kt * 128:(kt + 1) * 128]
                                            kr = [("kTc", kt, (hm % 2))]
                                        P.pe(MM(st[b][:, 0:256], ksl, qv, True, not bias),
                                             r=kr + [("qkT", 0, (hm // 2), qb // 2, (hm % 2))], w=["ps_st%d" % b])
                                        if bias:
                                            tab, tkeys = (natab_full, ["stg0", "stg1"]) if (qb in (0, 3) or own) else (natab_band, ["sgm0", "sgm1"])
                                            s0 = 4 * kt if own else 15 - 2 * kt + 4 * qb
                                            P.pe(MM(st[b][:, 0:256], identb[:],
                                                    tab[:, s0:s0 + 4, :].rearrange("p s q -> p (s q)"), False, True),
                                                 r=tkeys + ["identb"], w=["ps_st%d" % b])
                                        P.act(ACT(PTb[:, i, :], st[b][:, 0:256], AF.Exp, scale=0.125, bias=negm[:, hm, 0:1]),
                                              r=["ps_st%d" % b, "negm"], w=[("PT", i)])

                                        def pv(i=i, kd=kd, kt=kt, m=m):
                                            for qt in range(2):
                                                ov = ob[qt][:, 0:nm * (e + 1)].rearrange("p (m e) -> p m e", m=nm)
                                                if kd == "l":
                                                    vsl, vr = va[:, kt, h, 0:e + 1], [("vaug", kt)]
                                                else:
                                                    vsl, vr = vca[:, kt, h, 0:e + 1], ["vaugc", "vaugc1"]
                                                P.pe(MM(ov[:, m, :], PTb[:, i, qt * 128:(qt + 1) * 128], vsl, i == 0, i == nt_ - 1),
                                                     r=[("PT", i)] + vr, w=["ps_ob%d" % qt])
                                        pvq.append(pv)
                                        while len(pvq) > 2:
                                            pvq.pop(0)()
                                    while pvq:
                                        pvq.pop(0)()
                                posts = []
                                for qt in range(2):
                                    t = 2 * qb + qt
                                    ov = ob[qt][:, 0:nm * (e + 1)].rearrange("p (m e) -> p m e", m=nm)
                                    gsl = gbuf[:, t, h * e:(h + 1) * e]
                                    posts.append(o_post(ov, "ps_ob%d" % qt, t, h, gsl, gsl, ("gbuf", t), ("gbuf", t), sb=32 if qt == 0 else 96))
                                zipg(posts)
                        for t in range(NTq):
                            for c in range(4):
                                P.pe(TR(tpb[:, c, :], gbuf[:, t, c * 128:(c + 1) * 128], identb[:]),
                                     r=[("gbuf", t), "identb"], w=["ps_tpb"])
                            P.act(ACT(YT[:, :, t * 128:(t + 1) * 128], tpb[:, 0:4, :], AF.Copy), r=["ps_tpb"], w=[("YT", t)])

                    def p_stage1(s, h, sb):
                        for m in range(nm):
                            hm = nm * h + m
                            pb = (hm % 2) * 64
                            qv = qT[pb:pb + 64, (hm // 2), s * 256:(s + 1) * 256]
                            for kt in range(2):
                                b = nxt("st", 2)
                                P.pe(MM(st[b][:, 0:256], kT[pb:pb + 64, (hm // 2), s * 256 + kt * 128: s * 256 + (kt + 1) * 128], qv),
                                     r=[("qkT", 0, (hm // 2), s // 2, (hm % 2)), ("qkT", 1, (hm // 2), s // 2, (hm % 2))], w=["ps_st%d" % b])
                                P.act(ACT(PTb[:, sb + m * 2 + kt, :], st[b][:, 0:256], AF.Exp, scale=0.125,
                                          bias=negm[:, hm, s:s + 1]),
                                      r=["ps_st%d" % b, "negm"], w=[("PT", sb + m * 2 + kt)])

                    def p_stage2(s, h, sb):
                        posts = []
                        for qt in range(2):
                            t = 2 * s + qt
                            b = nxt("ob", 2)
                            ov = ob[b][:, 0:nm * (e + 1)].rearrange("p (m e) -> p m e", m=nm)
                            for m in range(nm):
                                for kt in range(2):
                                    P.pe(MM(ov[:, m, :], PTb[:, sb + m * 2 + kt, qt * 128:(qt + 1) * 128],
                                            va[:, 2 * s + kt, h, 0:e + 1], kt == 0, kt == 1),
                                         r=[("PT", sb + m * 2 + kt), ("vaug", 2 * s + kt)], w=["ps_ob%d" % b])
                            posts.append(o_post(ov, "ps_ob%d" % b, t, h, Ytok[:, qt, h * e:(h + 1) * e], gbuf[:, t, h * e:(h + 1) * e],
                                                ("gbuf", t), ("Ytok", qt), sb=32 if qt == 0 else 96))
                        zipg(posts)

                    hcnt = 0
                    for s in range(4 if not samp else 0):
                        pending = None
                        for h in range(nh):
                            sb = (hcnt % 2) * 4
                            hcnt += 1
                            p_stage1(s, h, sb)
                            if pending is not None:
                                p_stage2(*pending)
                            pending = (s, h, sb)
                        p_stage2(*pending)
                        for qt in range(2):
                            ytok_to_YT(2 * s + qt, qt)

                    branch_merge(D["w_br_a"][li] if isA else D["w_br_c"][li], MRG + (0 if isA else 2048), isA)

                def ssd_mixer():
                    xbcT = kT
                    P.tag = "%d.%d.B.proj" % (kind, li)
                    SEGK2 = [("seg", 0), ("seg", 1)]
                    xtok = vaug[:, :, 0:512]
                    def evz(ps, pk, t):
                        P.act(ACT(gbuf[:, t, :], ps[:, :], AF.Silu), r=[pk], w=[("gbuf", t)])
                    proj_tm(ZB, 512, evz, q=True)
                    P.dma("sp", "ld_cw", convw[:], D["convwT"][li], w=["convw"])
                    P.dma("sp", "ld_cw", convb[:], D["convbT"][li], w=["convb"])
                    P.dma("sp", "ld_cw", dtbias[:], D["dt_bias"][li].partition_broadcast(128), w=["dtbias"])
                    P.dma("sp", "ld_cw", nega[:], D["a_log"][li].partition_broadcast(128), w=["nega0"])
                    P.dma("sp", "ld_cw", dsum[:], D["d_skip"][li].partition_broadcast(128), w=["dsum0"])
                    P.dma("sp", "ld_cw", ssdg[:], D["ssd_norm_g"][li].partition_broadcast(128), w=["ssdg"])
                    P.act(ACT(nega[:], nega[:], AF.Exp), r=["nega0"], w=["nega1"])
                    P.dve(TS(nega[:], nega[:], -1.0, ALU.mult), r=["nega1"], w=["nega"])
                    P.dve(TT(dsum[:, 0:8], dsum[:, 0:8], dsum[:, 8:16], ALU.add), r=["dsum0"], w=["dsum"])
                    P.dve(lambda en: en.memset(xpre[:], 0.0), w=["Rb", ("Rbh", 0), ("Rbh", 1)])
                    for blk, (c0, ncol) in enumerate(((XB, 512), (XB + 512, 256))):
                        wt, wk = wload(Win[:, c0:c0 + ncol], 8, ncol)
                        for cc in range(ncol // 128):
                            c = blk * 4 + cc

                            def evx_s(ps, pk, tb, c=c):
                                xps = Rbraw[:, 0:1028]
                                caf = seg[:, :, :].rearrange("p a b -> p (a b)")
                                P.act(ACT(xps[:, 2 + tb * 512:2 + (tb + 1) * 512], ps[:, :], AF.Copy), r=[pk], w=["Rb"])
                                if tb == 0:
                                    return
                                P.dve(TS(caf, xps[:, 0:1024], convw[:, c, 0:1], ALU.mult), r=["Rb", "convw"], w=SEGK2)
                                for j in range(1, 5):
                                    P.dve(STT(caf, xps[:, j:j + 1024], convw[:, c, j:j + 1], caf, ALU.mult, ALU.add),
                                          r=["Rb", "convw"] + SEGK2, w=SEGK2)
                                P.act(ACT(xbcT[:, c, :], caf, AF.Silu, bias=convb[:, c:c + 1]), r=SEGK2 + ["convb"],
                                      w=[("qkT", 1, c, tb_, hf) for tb_ in range(2) for hf in range(2)])

                            def evx(ps, pk, tb, c=c):
                                P.act(ACT(xpre[:, 2 * tb:2 * tb + 2, 2:258], ps[:, :].rearrange("p (s q) -> p s q", s=2),
                                          AF.Copy), r=[pk, "Rb"], w=[("Rbh", tb)])
                                sl = slice(2 * tb, 2 * tb + 2)
                                P.dve(TS(cacc[:, sl, :], xpre[:, sl, 0:256], convw[:, c, 0:1], ALU.mult),
                                      r=["Rb", ("Rbh", tb), "convw"], w=[("seg", tb)])
                                for j in range(1, 5):
                                    P.dve(STT(cacc[:, sl, :], xpre[:, sl, j:j + 256], convw[:, c, j:j + 1], cacc[:, sl, :],
                                              ALU.mult, ALU.add), r=["Rb", ("Rbh", tb), "convw", ("seg", tb)], w=[("seg", tb)])
                                P.act(ACT(xbcT[:, c, tb * 512:(tb + 1) * 512].rearrange("p (s q) -> p s q", s=2),
                                          cacc[:, sl, :], AF.Silu, bias=convb[:, c:c + 1]),
                                      r=[("seg", tb), "convb"], w=[("qkT", 1, c, tb, 0), ("qkT", 1, c, tb, 1)])
                            proj_fm(wt, wk, cc * 128, 128, evx_s if samp else evx)
                    wt_dt, wk_dt = wload(Win[:, DTB:DTB + 16], 8, 16)
                    bdt = nxt("pj", 3)
                    for t in range(NT):
                        for k in range(8):
                            P.pe(MM(pj[bdt][:, t * 16:(t + 1) * 16], hT[:, k, t * 128:(t + 1) * 128], wt_dt[:, k, 0:16], k == 0, k == 7),
                                 r=[("hT", t), wk_dt], w=["ps_pj%d" % bdt])
                    DT0 = [("dt0", t) for t in range(NT)]
                    DT1 = [("dt1", t) for t in range(NT)]
                    DTK = [("dt", t) for t in range(NT)]
                    P.dve(TT(dtb[:, :, :], pj[bdt][:, 0:128].rearrange("p (t c) -> p t c", t=NT), bc(dtbias[:], 1, [128, NT, 16]), ALU.add),
                          r=["ps_pj%d" % bdt, "dtbias"], w=DT0)
                    P.act(ACT(dtb[:, :, :], dtb[:, :, :], AF.Exp), r=DT0, w=DT1)
                    P.act(ACT(dtb[:, :, :], dtb[:, :, :], AF.Ln, bias=1.0), r=DT1, w=DTK)
                    P.dve(TT(lab[:, :, :], dtb[:, :, :], bc(nega[:], 1, [128, NT, 16]), ALU.mult), r=DTK + ["nega"],
                          w=[("la", t) for t in range(NT)])
                    for t in range(NT):
                        for c in range(5):
                            P.pe(TR(tpb[:, c, :], xbcT[:, c, t * 128:(t + 1) * 128], identb[:]),
                                 r=[("qkT", 1, c, t // 4, 0), ("qkT", 1, c, t // 4, 1), "identb"], w=["ps_tpb"])
                        P.act(ACT(xtok[:, t, :], tpb[:, 0:4, :], AF.Copy), r=["ps_tpb"], w=[("vaug", t)])
                        P.act(ACT(btok[:, t, :], tpb[:, 4, :], AF.Copy), r=["ps_tpb"], w=[("btok", t)])

                    P.tag = "%d.%d.B.loop" % (kind, li)
                    nseq = 1 if samp else 4
                    nys = 8 if samp else 2
                    nch = NT // nseq
                    D1MAP = {"Rb": "xt0", ("seg", 0): "xt1", ("seg", 1): "xt1", "dec": "stg0", ("MT", 0): "stg1", ("MT", 1): "stg1",
                             "xdt": "hbf0", "wxdt": "hbf0", "ysb": "sgm0", ("ytmp", 0): "sgm1", ("ytmp", 1): "sgm1"}
                    D1OWN = {"acum", "eacum", "wexp0", "wexp", "cdec", ("cbm", 0), ("cbm", 1), ("hst", 0), ("hst", 1), "hstb"}

                    def km1(k):
                        if k in D1MAP:
                            return D1MAP[k]
                        if k in D1OWN:
                            return (k, "d1")
                        return k
                    BUF = {0: (acum, Rb, seg, dec, cbm, MTb, xdt, wxdt, hst, hstb, ysb, ytmp),
                           1: (acum2, xt[0][:, :].rearrange("p (h i) -> p h i", h=8), xt[1][:, :].rearrange("p (h i) -> p h i", h=8),
                               stgT[:, 0, :].bitcast(BF16).rearrange("p (h i) -> p h i", h=8),
                               cbm2, stgT[:, 1, :].bitcast(BF16).rearrange("p (h i) -> p h i", h=8),
                               hbf[0][:, 0:512], hbf[0][:, 512:1024], hst2, hstb2, sgm[0], sgm[1])}

                    pcnt = {("s", 0): 0, ("s", 1): 0, ("o", 0): 0, ("o", 1): 0}

                    def chunk_step(s, d, c, hstate, ysw):
                        acum, Rb, seg, dec, cbm, MTb, xdt, wxdt, hst, hstb, ysb, ytmp = BUF[d]
                        cd0 = 40 + 16 * d
                        STL = [(st[0], "ps_st0"), (st[1], "ps_st1")] if d == 0 else [(pj[0], "ps_pj0"), (pj[1], "ps_pj1")]
                        OBL = [(ob[0], "ps_ob0"), (ob[1], "ps_ob1")] if d == 0 else [(pj[2], "ps_pj2")]

                        def nxs():
                            pcnt[("s", d)] += 1
                            return STL[pcnt[("s", d)] % len(STL)]

                        def nxo():
                            pcnt[("o", d)] += 1
                            return OBL[pcnt[("o", d)] % len(OBL)]
                        t = s * nch + c
                        la_t = lab[:, t, d * 8:(d + 1) * 8]
                        ob_b, ko_b = nxo()
                        P.pe(MM(ob_b[:, 0:8], tri[d][:], la_t), r=[("la", t), "triU", "triL"], w=[ko_b])
                        P.pe(MM(ob_b[:, 8:16], onesf[:], la_t), r=[("la", t), "onesf"], w=[ko_b])
                        P.act(ACT(acum[:, 0:16], ob_b[:, 0:16], AF.Copy), r=[ko_b], w=["acum"])
                        yield
                        P.act(ACT(acum[:, 16:24], acum[:, 0:8], AF.Exp), r=["acum"], w=["eacum"])
                        P.dve(TT(acum[:, 24:32], acum[:, 8:16], acum[:, 0:8], ALU.subtract), r=["acum"], w=["wexp0"])
                        P.act(ACT(acum[:, 24:32], acum[:, 24:32], AF.Exp), r=["wexp0"], w=["wexp"])
                        P.act(ACT(sm[:, cd0:cd0 + 8], acum[:, 8:16], AF.Exp), r=["acum"], w=["cdec"])
                        yield
                        P.dve(TT(Rb[:], bc(tri[d][:], 1, [128, 8, 128]), bc(la_t, 2, [128, 8, 128]), ALU.mult),
                              r=[("la", t), "triU", "triL"], w=["Rb", ("Rbh", 0), ("Rbh", 1)])
                        sb_ = []
                        for hh in range(2):
                            yield
                            st_b2, k_b2 = nxs()
                            P.pe(MM(st_b2[:, :], onesf[:], Rb[:, hh * 4:(hh + 1) * 4, :]), r=["Rb", "onesf"],
                                 w=[k_b2], ni=2)
                            for h4 in range(4):
                                hd = hh * 4 + h4
                                P.dve(TS(seg[:, hd, :], st_b2[:, h4 * 128:(h4 + 1) * 128], acum[:, hd:hd + 1],
                                         ALU.subtract, 0.0, ALU.min),
                                      r=[k_b2, "acum"], w=[("seg", hh)])
                        yield
                        P.act(ACT(dec[:], seg[:], AF.Exp), r=[("seg", 0), ("seg", 1)], w=["dec"])
                        yield
                        for g in range(2):
                            st_b3, k_b3 = nxs()
                            P.pe(MM(st_b3[:, 0:128], xbcT[g * 64:(g + 1) * 64, 4, t * 128:(t + 1) * 128],
                                    xbcT[g * 64:(g + 1) * 64, 5, t * 128:(t + 1) * 128]),
                                 r=[("qkT", 1, 4, t // 4, g), ("qkT", 1, 5, t // 4, g)], w=[k_b3])
                            P.dve(TT(cbm[:, g, :], st_b3[:, 0:128], tri[d][:], ALU.mult),
                                  r=[k_b3, "triU", "triL"], w=[("cbm", g)])
                        for g in range(2):
                            P.dve(TT(MTb[:, g * 4:(g + 1) * 4, :], dec[:, g * 4:(g + 1) * 4, :],
                                     bc(cbm[:, g, :], 1, [128, 4, 128]), ALU.mult), r=["dec", ("cbm", g)], w=[("MT", g)])
                        yield
                        P.dve(TT(xdt[:].rearrange("p (h q) -> p h q", h=8), xtok[:, t, :].rearrange("p (h q) -> p h q", h=8),
                                 bc(dtb[:, t, d * 8:(d + 1) * 8], 2, [128, 8, 64]), ALU.mult),
                              r=[("vaug", t), ("dt", t)], w=["xdt"])
                        P.dve(TT(wxdt[:].rearrange("p (h q) -> p h q", h=8), xdt[:].rearrange("p (h q) -> p h q", h=8),
                                 bc(acum[:, 24:32], 2, [128, 8, 64]), ALU.mult), r=["xdt", "wexp"], w=["wxdt"])
                        yield
                        ob_by, ko_by = nxo()
                        for hd in range(8):
                            P.pe(MM(ob_by[:, hd * 64:(hd + 1) * 64], MTb[:, hd, :], xdt[:, hd * 64:(hd + 1) * 64]),
                                 r=[("MT", hd // 4), "xdt"], w=[ko_by])
                        first_dir_chunk = c not in ysw
                        ysw.add(c)
                        if hstate[d]:
                            P.act(ACT(ysb, ob_by[:, :], AF.Copy), r=[ko_by], w=["ysb"])
                            for g in range(2):
                                st_bo, k_bo = nxs()
                                for r4 in range(4):
                                    P.pe(MM(st_bo[:, r4 * 64:(r4 + 1) * 64],
                                            xbcT[g * 64:(g + 1) * 64, 5, t * 128:(t + 1) * 128],
                                            hstb[g * 64:(g + 1) * 64, r4 * 64:(r4 + 1) * 64]),
                                         r=[("qkT", 1, 5, t // 4, g), "hstb"], w=[k_bo])
                                P.dve(TT(ytmp[:, g * 256:(g + 1) * 256].rearrange("p (h q) -> p h q", h=4),
                                         st_bo[:, 0:256].rearrange("p (h q) -> p h q", h=4),
                                         bc(acum[:, 16 + 4 * g:20 + 4 * g], 2, [128, 4, 64]), ALU.mult),
                                      r=[k_bo, "eacum"], w=[("ytmp", g)])
                            P.dve(TT(ysb, ysb, ytmp, ALU.add), r=["ysb", ("ytmp", 0), ("ytmp", 1)], w=["ysb"])
                            ysrc, ykey = ysb, "ysb"
                        else:
                            ysrc, ykey = ob_by[:, :], ko_by
                        if first_dir_chunk:
                            P.act(ACT(ysum[:, c % nys, :], ysrc, AF.Copy), r=[ykey], w=[("ysum", c % nys)])
                        else:
                            P.dve(TT(ysum[:, c % nys, :], ysum[:, c % nys, :], ysrc, ALU.add),
                                  r=[ykey, ("ysum", c % nys)], w=[("ysum", c % nys)])
                        yield
                        st_bs, k_bs = nxs()
                        for g in range(2):
                            P.pe(MM(st_bs[0:64, g * 256:(g + 1) * 256], btok[:, t, g * 64:(g + 1) * 64],
                                    wxdt[:, g * 256:(g + 1) * 256]), r=[("btok", t), "wxdt"], w=[k_bs])
                        for g in range(2):
                            hs = hst[g * 64:(g + 1) * 64, :]
                            if hstate[d]:
                                P.dve(TT(hs.rearrange("p (r q) -> p r q", r=4), hs.rearrange("p (r q) -> p r q", r=4),
                                         bc(sm[g * 64:(g + 1) * 64, cd0 + 4 * g:cd0 + 4 + 4 * g], 2, [64, 4, 64]), ALU.mult),
                                      r=["cdec", ("hst", g)], w=[("hst", g)])
                                P.dve(TT(hs, hs, st_bs[0:64, g * 256:(g + 1) * 256], ALU.add),
                                      r=[k_bs, ("hst", g)], w=[("hst", g)])
                            else:
                                P.act(ACT(hs, st_bs[0:64, g * 256:(g + 1) * 256], AF.Copy), r=[k_bs],
                                      w=[("hst", g)])
                        P.act(ACT(hstb[:], hst[:], AF.Copy), r=[("hst", 0), ("hst", 1)], w=["hstb"])
                        hstate[d] = True

                        yield

                    for s in range(nseq):
                        hs = {0: False, 1: False}
                        ysw = set()
                        for d in range(2):
                            acum_, Rb_, seg_, dec_, cbm_, MTb_, xdt_, wxdt_, hst_d, hstb_d, ysb_, ytmp_ = BUF[d]
                            P.keymap = km1 if d == 1 else None
                            if samp:
                                P.dma("sp", "ld_hin", hin[:], D["sst"][li, d].rearrange("(b p) n -> p b n", p=128), w=["hin"])
                                for blk in range(4):
                                    bt = nxt("st", 2)
                                    P.pe(TR(st[bt][0:64, 0:128], hin[:, blk, :], identf[:]), r=["hin", "identf"],
                                         w=["ps_st%d" % bt])
                                    g = blk // 2
                                    P.act(ACT(hst_d[g * 64:(g + 1) * 64, (blk % 2) * 128:(blk % 2 + 1) * 128], st[bt][0:64, 0:128],
                                              AF.Copy), r=["ps_st%d" % bt], w=[("hst", g)])
                                P.act(ACT(hstb_d[:], hst_d[:], AF.Copy), r=[("hst", 0), ("hst", 1)], w=["hstb"])
                                hs[d] = True
                            P.keymap = None
                        for ci in range(nch):
                            gens = [(d, chunk_step(s, d, ci if d == 0 else nch - 1 - ci, hs, ysw)) for d in range(2)]
                            while gens:
                                for dg in list(gens):
                                    P.keymap = km1 if dg[0] == 1 else None
                                    try:
                                        next(dg[1])
                                    except StopIteration:
                                        gens.remove(dg)
                                    P.keymap = None
                        for d in range(2):
                            hst_o = BUF[d][8]
                            P.keymap = km1 if d == 1 else None
                            if not samp:
                                for blk in range(2):
                                    bt = nxt("st", 2)
                                    P.pe(TR(st[bt][:, 0:128], hst_o[:, blk * 128:(blk + 1) * 128], identf[:]),
                                         r=[("hst", 0), ("hst", 1), "identf"], w=["ps_st%d" % bt])
                                    hi = nxt("hout", 2)
                                    P.act(ACT(houts[hi][:], st[bt][:, 0:128], AF.Copy), r=["ps_st%d" % bt], w=["hout%d" % hi])
                                    for g in range(2):
                                        r0 = (4 * g + 2 * blk) * 64
                                        P.dma("sp", "o_ssd%d" % hi, O["nssd"][s, li, d, r0:r0 + 128, :],
                                              houts[hi][:, g * 64:(g + 1) * 64], r=["hout%d" % hi])
                            P.keymap = None
                        if own:
                            for c in range(nch):
                                P.dve(TT(ysb.rearrange("p (h q) -> p h q", h=8), xtok[:, c, :].rearrange("p (h q) -> p h q", h=8),
                                         bc(dsum[:, 0:8], 2, [128, 8, 64]), ALU.mult), r=[("vaug", c), "dsum", "ysb"], w=["ysb"])
                                P.dve(TT(ysum[:, c, :], ysum[:, c, :], ysb, ALU.add), r=["ysb", ("ysum", c)], w=[("ysum", c)])
                            for j in range(2):
                                P.dve(TS(ysb, ysum[:, 0, :], selO[:, j * 8:j * 8 + 1], ALU.mult), r=[("ysum", 0), "selO", "ysb"], w=["ysb"])
                                for c in range(1, nch):
                                    P.dve(STT(ysb, ysum[:, c, :], selO[:, j * 8 + c:j * 8 + c + 1], ysb, ALU.mult, ALU.add),
                                          r=[("ysum", c), "selO", "ysb"], w=["ysb"])
                                P.dve(TT(ysb, ysb, gbuf[:, j, :], ALU.mult), r=["ysb", ("gbuf", j)], w=["ysb"])
                                P.act(ACT(ytmp, ysb, AF.Square, accum_out=sm[:, 50:51]), r=["ysb"],
                                      w=["ytmp", ("ytmp", 0), ("ytmp", 1), "sm_ys"])
                                rstd_from_ss(sm[:, 50:51], sm[:, 51:52], 512, "sm_ys", "sm_yr")
                                P.dve(STT(Ytok[:, j, :], ysb, sm[:, 51:52], ssdg[:], ALU.mult, ALU.mult),
                                      r=["ysb", "sm_yr", "ssdg"], w=[("Ytok", j)])
                                ytok_to_YT(j, j)
                        def comb(c, par):
                            t = s * nch + c
                            if par == 0:
                                yb, yt_, yk, ytk, c0 = ysb, ytmp, ["ysb"], ["ytmp", ("ytmp", 0), ("ytmp", 1)], 50
                            else:
                                yb, yt_, yk, ytk, c0 = sgm[0], sgm[1], ["sgm0"], ["sgm1"], 52
                            P.dve(TT(yb.rearrange("p (h q) -> p h q", h=8), xtok[:, t, :].rearrange("p (h q) -> p h q", h=8),
                                     bc(dsum[:, 0:8], 2, [128, 8, 64]), ALU.mult), r=[("vaug", t), "dsum"] + yk, w=yk)
                            yield
                            P.dve(TT(yb, yb, ysum[:, c % nys, :], ALU.add), r=yk + [("ysum", c % nys)], w=yk)
                            yield
                            P.dve(TT(yb, yb, gbuf[:, t, :], ALU.mult), r=yk + [("gbuf", t)], w=yk)
                            yield
                            P.act(ACT(yt_, yb, AF.Square, accum_out=sm[:, c0:c0 + 1]), r=yk, w=ytk + ["sm_ys%d" % par])
                            yield
                            P.act(ACT(sm[:, c0 + 1:c0 + 2], sm[:, c0:c0 + 1], AF.Ln, scale=1.0 / 512, bias=EPS),
                                  r=["sm_ys%d" % par], w=["sm_yr%d_l" % par])
                            yield
                            P.act(ACT(sm[:, c0 + 1:c0 + 2], sm[:, c0 + 1:c0 + 2], AF.Exp, scale=-0.5), r=["sm_yr%d_l" % par],
                                  w=["sm_yr%d" % par])
                            yield
                            P.dve(STT(Ytok[:, c % 2, :], yb, sm[:, c0 + 1:c0 + 2], ssdg[:], ALU.mult, ALU.mult),
                                  r=yk + ["sm_yr%d" % par, "ssdg"], w=[("Ytok", c % 2)])
                            yield
                            ytok_to_YT(t, c % 2)
                        if not own:
                            for c2 in range(0, nch, 2):
                                gens = [comb(c2 + j_, j_) for j_ in range(2) if c2 + j_ < nch]
                                while gens:
                                    for g_ in list(gens):
                                        try:
                                            next(g_)
                                        except StopIteration:
                                            gens.remove(g_)
                    branch_merge(D["w_br_b"][li], MRG + 1024, False)

                if "A" in stages:
                    attn_mixer("A")
                if "B" in stages:
                    ssd_mixer()
                if "C" in stages:
                    attn_mixer("C")
                if "P" not in stages:
                    continue

                P.tag = "%d.%d.post" % (kind, li)
                for t in range(NTq):
                    xi = nxt("xt", 2)
                    P.act(ACT(hbf[xi][:], acc[:, t, :], AF.Copy), r=[("acc", t, 0), ("acc", t, 1)], w=["hbf0"])
                    for k in range(8):
                        P.pe(TR(tpb[:, k, :], hbf[xi][:, k * 128:(k + 1) * 128], identb[:]),
                             r=["hbf0", "identb"], w=["ps_tpb"])
                    P.act(ACT(hTq[:, :, t * 128:(t + 1) * 128], tpb[:, :, :], AF.Copy), r=["ps_tpb"], w=[(hqk, t)])
                xo = big[:, :].rearrange("p (j f) -> p j f", j=4)
                for t in range(NTq):
                    P.dma("sp" if t % 2 == 0 else "act", "ld_xa%d" % t, acc[:, t, :], xsrc_q[t * 128:(t + 1) * 128, :],
                          r=[(xkq, t)], w=[("acc", t, 0), ("acc", t, 1)])
                for t in range(NTq):
                    xi = t % 2
                    xb_ = acc[:, t, :]
                    XK = [("acc", t, 0), ("acc", t, 1)]
                    if t == 0:
                        wts = [wload(D["w_out"][li][:, cb * 512:(cb + 1) * 512], 8, 512) for cb in range(2)]
                    for cb in range(2):
                        wt, wk = wts[cb]
                        pB, kB_ = ALLB[nxt("allb", 7)]
                        for k in range(8):
                            P.pe(MM(pB[:, :], hTq[:, k, t * 128:(t + 1) * 128], wt[:, k, :], k == 0, k == 7),
                                 r=[(hqk, t), wk], w=[kB_])
                        si = nxt("sgm", 2)
                        P.dve(TT(sgm[si], pB[:, :], gate[:, cb * 512:(cb + 1) * 512], ALU.mult),
                              r=[kB_, "ada"], w=["sgm%d" % si])
                        P.dve(TT(xb_[:, cb * 512:(cb + 1) * 512], xb_[:, cb * 512:(cb + 1) * 512], sgm[si], ALU.add),
                              r=["sgm%d" % si, ("acc", t, cb)], w=[("acc", t, cb)])
                    if li < n_layers - 1 and gath:
                        P.dma("sp", "st_x%d" % xi, xscr_own[t * 128:(t + 1) * 128, :], xb_, r=XK,
                              w=[("xscr_own", t)])
                        if t == NTq - 1:
                            P.op("pool", lambda e: e.collective_compute(
                                "AllGather", ALU.bypass, replica_groups=[[0, 1, 2, 3], [4, 5, 6, 7]],
                                ins=[xscr_own], outs=[xgath]),
                                reads=[("xscr_own", 0), ("xscr_own", 1)], writes=[("xgath", t_) for t_ in range(NT)],
                                dma="cc_x", inc=1)
                    elif li < n_layers - 1:
                        P.dma("sp", "st_x%d" % xi, xscr[t * 128:(t + 1) * 128, :], xb_, r=XK,
                              w=[("xscr", t)])
                        if next_own:
                            for j in range(2):
                                sc = selO[:, j * 8 + t:j * 8 + t + 1]
                                if t == 0:
                                    P.dve(TS(xo[:, j, :], xb_, sc, ALU.mult), r=XK + ["selO"], w=[("xo", j)])
                                else:
                                    P.dve(STT(xo[:, j, :], xb_, sc, xo[:, j, :], ALU.mult, ALU.add),
                                          r=XK + ["selO", ("xo", j)], w=[("xo", j)])
                            if t == NT - 1:
                                for j in range(2):
                                    P.dma("sp", "st_xo", xscr_own[j * 128:(j + 1) * 128, :], xo[:, j, :], r=[("xo", j)],
                                          w=[("xscr_own", j)])
                    else:
                        if t == 0:
                            P.dma("sp", "ld_gs", gs, D["final_g"].partition_broadcast(128), r=["ada"], w=["gs"])
                        c0_ = 16 + 2 * xi
                        P.act(ACT(xt[xi][:], xb_, AF.Square, accum_out=sm[:, c0_:c0_ + 1]),
                              r=XK, w=["xt%d" % xi, "sm_fs%d" % xi])
                        rstd_from_ss(sm[:, c0_:c0_ + 1], sm[:, c0_ + 1:c0_ + 2], 1024, "sm_fs%d" % xi, "sm_fr%d" % xi)
                        P.dve(STT(xt[xi][:], xb_, sm[:, c0_ + 1:c0_ + 2], gs, ALU.mult, ALU.mult),
                              r=XK + ["sm_fr%d" % xi, "gs", "ada"], w=["xt%d" % xi])
                        P.dma("sp" if xi == 0 else "act", "o_y%d" % xi, yout[t * 128:(t + 1) * 128, :], xt[xi][:], r=["xt%d" % xi])

        compute_ada(0)
        if not do_prompt:
            for l_ in range(1, n_layers):
                compute_ada(l_)
        if do_prompt and do_sample and own_all and n_layers == 2:
            run_pass(1, [0])
            run_pass(0)
            run_pass(1, [1])
        else:
            if do_prompt:
                run_pass(0)
            if do_sample:
                run_pass(1)
        fk = [k for k in P.dma_count if k.startswith("o_") or k.startswith("st_x")]
        print("n_ops", len(P.ops), "sbuf_free", nc.sbuf_bytes_remaining, flush=True)
        import os, json
        if os.environ.get("KTAGS"):
            json.dump({e: [o["tag"] for o in P.ops if o["eng"] == e for _ in range(o.get("ni", 1))] for e in ENGS}, open(os.environ["KTAGS"], "w"))
        P.emit(final_keys=fk)
    return nc


_NC = {}


def _consts():
    ident = np.eye(128, dtype=np.float32)
    tt = np.arange(128)
    triU = (tt[:, None] <= tt[None, :]).astype(np.float32)
    triL = (tt[:, None] >= tt[None, :]).astype(np.float32)
    pos = np.arange(1024)
    inv = (10000.0 ** (-np.arange(16, dtype=np.float32) / 16)).astype(np.float32)
    C = np.zeros((64, 1024), np.float32)
    S = np.zeros((64, 1024), np.float32)
    RT = np.zeros((64, 64), np.float32)
    for d in range(64):
        half = d // 32
        qd = (d % 32) % 16
        p = (pos // 64) if half == 0 else (pos % 64)
        ang = p.astype(np.float32) * inv[qd]
        C[d] = np.cos(ang)
        S[d] = np.sin(ang)
        if (d % 32) < 16:
            RT[d + 16, d] = -1.0
        else:
            RT[d - 16, d] = 1.0
    col = np.arange(64)
    cs = np.clip(col - 8, 0, 48)
    inwin = (col[None, :] >= cs[:, None]) & (col[None, :] < cs[:, None] + 16)
    m = np.where(inwin.T, 0.0, NEG).astype(np.float32)
    namask = np.concatenate([m, m], axis=0)
    RT2 = np.zeros((128, 128), np.float32)
    RT2[:64, :64] = RT
    RT2[64:, 64:] = RT
    return dict(ident=ident, triU=triU, triL=triL, ropeC=np.concatenate([C, C], 0), ropeS=np.concatenate([S, S], 0),
                ropeRT=RT2, namask=namask)


def kernel(**inp):
    f = lambda a: np.ascontiguousarray(np.asarray(a, dtype=np.float32))
    n_cores = 8
    cst = _consts()
    col = np.arange(64)
    dc = np.clip(col[:, None] - col[None, :], -15, 15) + 15
    rpb = f(inp["na_rpb"])
    rpbT = np.ascontiguousarray(rpb[:, :, ::-1, :][:, :, :, dc])
    shared = dict(
        norm_g=f(inp["norm_g"]), w_ada=f(inp["w_ada"]), b_ada=f(inp["b_ada"]), w_in=f(inp["w_in"]),
        lamv=f(np.stack([inp["lam_q1"], inp["lam_q2"], inp["lam_k1"], inp["lam_k2"]], axis=1)),
        subln_g=f(inp["diff_subln_g"]),
        convwT=f(np.asarray(inp["conv_w"]).reshape(2, 5, 6, 128).transpose(0, 3, 2, 1)),
        convbT=f(np.asarray(inp["conv_b"]).reshape(2, 6, 128).transpose(0, 2, 1)),
        dt_bias=f(np.asarray(inp["dt_bias"]).reshape(2, 16)), a_log=f(np.asarray(inp["a_log"]).reshape(2, 16)),
        d_skip=f(np.asarray(inp["d_skip"]).reshape(2, 16)), ssd_norm_g=f(inp["ssd_norm_g"]), rpbT=rpbT,
        w_br_a=f(inp["w_br_a"]), w_br_b=f(inp["w_br_b"]), w_br_c=f(inp["w_br_c"]), w_out=f(inp["w_out"]),
        final_g=f(inp["final_g"]), **cst)
    xp = f(inp["x_prompt"])
    xs = f(inp["x_sample"])
    own_tabs = []
    kc_i = np.arange(64)
    dcc = np.clip(kc_i[:, None] - kc_i[None, :], -15, 15) + 15
    cs = np.clip(kc_i - 8, 0, 48)
    inwin_T = ((kc_i[:, None] >= cs[None, :]) & (kc_i[:, None] < cs[None, :] + 16))
    for qb in range(4):
        selO = np.zeros((128, 16), np.float32)
        for j in range(2):
            selO[:, j * 8 + 2 * qb + j] = 1.0
        rpbO = np.zeros((2, 8, 128, 32, 64), np.float32)
        maskO = np.full((128, 32, 64), NEG, np.float32)
        for a in range(8):
            for j in range(4):
                r_ = 4 * qb + j
                rs_ = min(max(r_ - 4, 0), 8)
                for half in range(2):
                    kr = 2 * a + half
                    if rs_ <= kr <= rs_ + 7:
                        dd = kr - r_ + 7
                        rpbO[:, :, half * 64:(half + 1) * 64, a * 4 + j, :] = rpb[:, :, dd, :][:, :, dcc]
                        maskO[half * 64:(half + 1) * 64, a * 4 + j, :] = np.where(inwin_T, 0.0, NEG)
        own_tabs.append(dict(selO=selO, rpbO=rpbO, maskO=maskO,
                             ropeCo=np.ascontiguousarray(cst["ropeC"][:, qb * 256:(qb + 1) * 256]),
                             ropeSo=np.ascontiguousarray(cst["ropeS"][:, qb * 256:(qb + 1) * 256])))
    in_maps = []
    for c in range(n_cores):
        b = c // 4
        cv = np.stack([f(inp["c_ctx"]), f(inp["c"])[b]], axis=0)
        d = dict(shared)
        d.update(
            xp=np.ascontiguousarray(xp[4 * c:4 * c + 4].reshape(T, 1024)),
            xs=np.ascontiguousarray(xs[b]),
            xso=np.ascontiguousarray(xs[b, (c % 4) * 256:(c % 4 + 1) * 256]),
            cdk=f(inp["cache_diff_k"])[b].reshape(2, 512, 512), cdv=f(inp["cache_diff_v"])[b].reshape(2, 512, 512),
            cnk=f(inp["cache_na_k"])[b].reshape(2, 512, 512), cnv=f(inp["cache_na_v"])[b].reshape(2, 512, 512),
            sst=f(inp["state_ssd"])[b].reshape(2, 2, 512, 64),
            cvecT=np.ascontiguousarray(cv.reshape(2, 8, 128).transpose(0, 2, 1)),
            **own_tabs[c % 4],
        )
        in_maps.append({k: np.ascontiguousarray(v) for k, v in d.items()})
    if "nc" not in _NC:
        _NC["nc"] = build()
    import os
    ncr = int(os.environ.get("KCORES", "8"))
    res = run_bass_kernel_spmd(_NC["nc"], in_maps[:ncr], core_ids=list(range(ncr)))
    R = list(res.results) + [res.results[0]] * (n_cores - ncr)
    y_prompt = np.concatenate([R[c]["yp"].reshape(4, 256, 1024) for c in range(n_cores)], axis=0)
    if R[0]["ys"].shape[0] == T:
        y_sample = np.stack([R[0]["ys"], R[4]["ys"]], axis=0)
    else:
        y_sample = np.stack([np.concatenate([R[4 * b + q]["ys"] for q in range(4)], axis=0) for b in range(2)], axis=0)
    cat = lambda k, shp: np.concatenate([R[c][k].reshape((4,) + shp) for c in range(n_cores)], axis=0)
    return (y_prompt, y_sample,
            cat("ndk", (2, 256, 4, 128)), cat("ndv", (2, 256, 4, 128)),
            cat("nnk", (2, 256, 8, 64)), cat("nnv", (2, 256, 8, 64)),
            cat("nssd", (2, 2, 8, 64, 64)))
```
